# Optimizing a Trainium2 kernel written in Bass

```python
import math
import jax
import jax.numpy as jnp
from jax import lax
import numpy as np

D_MODEL = 2048
BATCH = 16
SEQ = 256
DEPTH = 2
DEC_BATCH = 4
DEC_SEQ = 4096
PAST_LEN = 256

EPS = 1e-6
GRID_W = 64
NA_HEAD_DIM = 64
NA_W = D_MODEL // 2
NA_HEADS = NA_W // NA_HEAD_DIM
NA_KH = 8
NA_KW = 16
ATTN_BLOCK = 128
SSM_HEADDIM = 64
SSM_W = D_MODEL // 2
SSM_HEADS = SSM_W // SSM_HEADDIM
SSM_GROUPS = 4
SSM_STATE = 128
SSM_CONV = 3
SSM_CHUNK = 128
CONV_CH = SSM_W + 2 * SSM_GROUPS * SSM_STATE
SGU_W = D_MODEL // 2
SGU_GROUPS = 8
SGU_CHUNK = 128
N_IN = 4 * NA_W + CONV_CH + SSM_W + 2 * SSM_HEADS + 3 * SGU_W + 3 * D_MODEL

kernel_name = 'hybrid_na_ssd_sgu_diffusion_step'


def _in_splits():
    sizes = (NA_W, NA_W, NA_W, NA_W, CONV_CH, SSM_W, 2 * SSM_HEADS, SGU_W, SGU_W, SGU_W)
    return tuple(np.cumsum(sizes).tolist())


def _rms(x, g):
    xf = x.astype(jnp.float32)
    xf = xf * lax.rsqrt(jnp.mean(jnp.square(xf), -1, keepdims=True) + EPS)
    return (xf * g.astype(jnp.float32)).astype(x.dtype)


def _attn_context(q, k, v):
    b_, L, H, d = q.shape
    nb = L // ATTN_BLOCK
    qb = jnp.moveaxis(q.reshape(b_, nb, ATTN_BLOCK, H, d), 1, 0)
    scale = d ** -0.5

    def blk(q_i):
        s = jnp.einsum('bqhd,bkhd->bhqk', q_i, k).astype(jnp.float32) * scale
        p = jax.nn.softmax(s, -1).astype(v.dtype)
        return jnp.einsum('bhqk,bkhd->bqhd', p, v)

    o = lax.map(blk, qb)
    return jnp.moveaxis(o, 0, 1).reshape(b_, L, H * d)


def _neighbourhood_attn(q, k, v, k_ctx, v_ctx, rpb):
    b_, L, H, d = q.shape
    rows = L // GRID_W
    kh = min(NA_KH, rows)
    kw = NA_KW
    n_loc = kh * kw
    scale = d ** -0.5
    kg = k.reshape(b_, rows, GRID_W, H, d)
    vg = v.reshape(b_, rows, GRID_W, H, d)
    q_rows = jnp.moveaxis(q.reshape(b_, rows, GRID_W, H, d), 1, 0)
    cols = jnp.arange(GRID_W)
    col_start = jnp.clip(cols - kw // 2, 0, GRID_W - kw)
    col_idx = col_start[:, None] + jnp.arange(kw)[None, :]
    col_bias_idx = col_idx - cols[:, None] + (NA_KW - 1)
    rpb_f = rpb.astype(jnp.float32)

    def row_block(args):
        r, q_r = args
        rs = jnp.clip(r - kh // 2, 0, rows - kh)
        k_rows = lax.dynamic_slice_in_dim(kg, rs, kh, axis=1)
        v_rows = lax.dynamic_slice_in_dim(vg, rs, kh, axis=1)
        k_win = k_rows[:, :, col_idx]
        v_win = v_rows[:, :, col_idx]
        s_loc = jnp.einsum('bjhd,bajkhd->bhjak', q_r, k_win).astype(jnp.float32) * scale
        row_bias_idx = rs + jnp.arange(kh) - r + (NA_KH - 1)
        bias = rpb_f[:, row_bias_idx[:, None, None], col_bias_idx[None]]
        s_loc = s_loc + jnp.transpose(bias, (0, 2, 1, 3))[None]
        s_ctx = jnp.einsum('bjhd,bkhd->bhjk', q_r, k_ctx).astype(jnp.float32) * scale
        s = jnp.concatenate([s_loc.reshape(b_, H, GRID_W, n_loc), s_ctx], -1)
        p = jax.nn.softmax(s, -1).astype(v.dtype)
        p_loc = p[..., :n_loc].reshape(b_, H, GRID_W, kh, kw)
        o = jnp.einsum('bhjak,bajkhd->bjhd', p_loc, v_win)
        return o + jnp.einsum('bhjk,bkhd->bjhd', p[..., n_loc:], v_ctx)

    out = lax.map(row_block, (jnp.arange(rows), q_rows))
    return jnp.moveaxis(out, 0, 1).reshape(b_, L, H * d)


def _dwconv(x, w, b):
    kk = w.shape[-1]
    y = lax.conv_general_dilated(
        x, jnp.transpose(w)[:, None, :].astype(x.dtype), window_strides=(1,),
        padding=[(kk // 2, kk // 2)], dimension_numbers=('NWC', 'WIO', 'NWC'),
        feature_group_count=x.shape[-1])
    return y + b.astype(x.dtype)


def _ssd(x, dt, a, bm, cm, h0):
    b_, L, H, P = x.shape
    G, N = bm.shape[-2:]
    hg = H // G
    nc = L // SSM_CHUNK
    Q = SSM_CHUNK
    x = x.reshape(b_, nc, Q, G, hg, P)
    dt = dt.reshape(b_, nc, Q, G, hg)
    bm = bm.reshape(b_, nc, Q, G, N)
    cm = cm.reshape(b_, nc, Q, G, N)
    acum = jnp.cumsum(dt * a.reshape(G, hg), axis=2)
    seg = acum[:, :, :, None] - acum[:, :, None]
    lower = jnp.tril(jnp.ones((Q, Q), dtype=bool))[:, :, None, None]
    lmat = jnp.exp(jnp.where(lower, seg, -jnp.inf))
    xdt = x * dt[..., None]
    cb = jnp.einsum('bcign,bcjgn->bcijg', cm, bm)
    y_diag = jnp.einsum('bcijg,bcijgh,bcjghp->bcighp', cb, lmat, xdt)
    decay_end = jnp.exp(acum[:, :, -1:] - acum)
    states = jnp.einsum('bcjgn,bcjgh,bcjghp->bcghpn', bm, decay_end, xdt)
    chunk_decay = jnp.exp(acum[:, :, -1])

    def step(h, inp):
        s_c, d_c = inp
        return h * d_c[..., None, None] + s_c, h

    h_init = h0.astype(jnp.float32).reshape(b_, G, hg, P, N)
    h_last, h_prev = lax.scan(step, h_init, (jnp.moveaxis(states, 1, 0), jnp.moveaxis(chunk_decay, 1, 0)))
    h_prev = jnp.moveaxis(h_prev, 0, 1)
    y_off = jnp.einsum('bcign,bcghpn,bcigh->bcighp', cm, h_prev, jnp.exp(acum))
    y = (y_diag + y_off).reshape(b_, L, H, P)
    return y, h_last.reshape(b_, H, P, N)


def _ssm_branch(xbc, z, dt_raw, p, h0_f, h0_b):
    b_, L, _ = xbc.shape
    f32 = jnp.float32
    xbc = jax.nn.silu(_dwconv(xbc, p['conv_w'], p['conv_b']).astype(f32))
    xs, bm, cm = jnp.split(xbc, [SSM_W, SSM_W + SSM_GROUPS * SSM_STATE], -1)
    xs = xs.reshape(b_, L, SSM_HEADS, SSM_HEADDIM)
    bm = bm.reshape(b_, L, SSM_GROUPS, SSM_STATE)
    cm = cm.reshape(b_, L, SSM_GROUPS, SSM_STATE)
    dt = jax.nn.softplus(dt_raw.astype(f32).reshape(b_, L, 2, SSM_HEADS) + p['dt_bias'].astype(f32))
    a = -jnp.exp(p['a_log'].astype(f32))
    y_f, h_f = _ssd(xs, dt[:, :, 0], a[0], bm, cm, h0_f)
    y_b, h_b = _ssd(jnp.flip(xs, 1), jnp.flip(dt[:, :, 1], 1), a[1], jnp.flip(bm, 1), jnp.flip(cm, 1), h0_b)
    y = y_f + jnp.flip(y_b, 1) + xs * p['d_skip'].astype(f32)[:, None]
    y = y.reshape(b_, L, SSM_W) * jax.nn.silu(z.astype(f32))
    return _rms(y, p['g_ssm']).astype(z.dtype), h_f.astype(h0_f.dtype), h_b.astype(h0_b.dtype)


def _sgu_branch(u, v, gate, p):
    b_, L, _ = v.shape
    nc = L // SGU_CHUNK
    vf = v.astype(jnp.float32)
    mu = jnp.mean(vf, -1, keepdims=True)
    var = jnp.mean(jnp.square(vf - mu), -1, keepdims=True)
    vn = (vf - mu) * lax.rsqrt(var + EPS) * p['g_sgu'].astype(jnp.float32)
    vn = vn.reshape(b_, nc, SGU_CHUNK, SGU_GROUPS, SGU_W // SGU_GROUPS)
    vs = jnp.einsum('gpq,bcqge->bcpge', p['w_s'].astype(jnp.float32), vn)
    vs = vs + jnp.transpose(p['b_s'].astype(jnp.float32))[:, :, None]
    y = u.astype(jnp.float32) * vs.reshape(b_, L, SGU_W) * jax.nn.silu(gate.astype(jnp.float32))
    return y.astype(u.dtype)


def _layer(x, cvec, p, ctx):
    b_, L, _ = x.shape
    m = jnp.matmul(jax.nn.silu(cvec), p['w_mod']) + p['b_mod']
    shift, scale, gate = jnp.split(m[:, None, :], 3, -1)
    h = (_rms(x, p['g_pre']) * (1 + scale) + shift).astype(x.dtype)
    proj = jnp.matmul(h, p['w_in'])
    q, k, v, g_a, xbc, z, dt_raw, u, v_c, g_c, g_m = jnp.split(proj, _in_splits(), -1)
    q = q.reshape(b_, L, NA_HEADS, NA_HEAD_DIM)
    k = k.reshape(b_, L, NA_HEADS, NA_HEAD_DIM)
    v = v.reshape(b_, L, NA_HEADS, NA_HEAD_DIM)
    if ctx is None:
        o_a = _attn_context(q, k, v)
        h0_f = jnp.zeros((b_, SSM_HEADS, SSM_HEADDIM, SSM_STATE), x.dtype)
        h0_b = h0_f
    else:
        k_ctx, v_ctx, h0_f, h0_b = ctx
        o_a = _neighbourhood_attn(q, k, v, k_ctx, v_ctx, p['rpb'])
    o_a = o_a * jax.nn.silu(g_a)
    o_b, h_f, h_b = _ssm_branch(xbc, z, dt_raw, p, h0_f, h0_b)
    o_c = _sgu_branch(u, v_c, g_c, p)
    gate_a, gate_b, gate_c = jnp.split(jax.nn.sigmoid(g_m), 3, -1)
    merged = (gate_a * jnp.matmul(o_a, p['w_br_a']) + gate_b * jnp.matmul(o_b, p['w_br_b'])
              + gate_c * jnp.matmul(o_c, p['w_br_c']))
    y = _rms(jnp.matmul(merged, p['w_out']), p['g_post'])
    return (x + gate * y).astype(x.dtype), (k, v, h_f, h_b)


def setup_inputs(seed: int = 0) -> dict:
    key = jax.random.key(seed)
    ks = jax.random.split(key, 32)
    f32 = jnp.float32
    D = D_MODEL

    def nrm(k, shape, scale=1.0):
        return jax.random.normal(k, shape, f32) * scale

    dt0 = jnp.exp(jax.random.uniform(ks[20], (DEPTH, 2, SSM_HEADS), f32, math.log(1e-3), math.log(1e-1)))
    dt_bias = dt0 + jnp.log(-jnp.expm1(-dt0))
    a_log = jnp.log(jax.random.uniform(ks[21], (DEPTH, 2, SSM_HEADS), f32, 1.0, 16.0))
    return {
        'x_prompt': nrm(ks[0], (BATCH, SEQ, D)),
        'x_sample': nrm(ks[1], (DEC_BATCH, DEC_SEQ, D)),
        'c': nrm(ks[2], (DEC_BATCH, D)),
        'cache_k': nrm(ks[3], (DEC_BATCH, DEPTH, PAST_LEN, NA_HEADS, NA_HEAD_DIM)),
        'cache_v': nrm(ks[4], (DEC_BATCH, DEPTH, PAST_LEN, NA_HEADS, NA_HEAD_DIM)),
        'state_ssm_fwd': nrm(ks[5], (DEC_BATCH, DEPTH, SSM_HEADS, SSM_HEADDIM, SSM_STATE), 0.5),
        'state_ssm_bwd': nrm(ks[6], (DEC_BATCH, DEPTH, SSM_HEADS, SSM_HEADDIM, SSM_STATE), 0.5),
        'c_ctx': nrm(ks[7], (D,)),
        'w_mod': nrm(ks[8], (DEPTH, D, 3 * D), 0.2 * D ** -0.5),
        'b_mod': nrm(ks[9], (DEPTH, 3 * D), 0.01),
        'g_pre': 1.0 + nrm(ks[10], (DEPTH, D), 0.01),
        'g_post': 1.0 + nrm(ks[11], (DEPTH, D), 0.01),
        'w_in': nrm(ks[12], (DEPTH, D, N_IN), D ** -0.5),
        'rpb': nrm(ks[13], (DEPTH, NA_HEADS, 2 * NA_KH - 1, 2 * NA_KW - 1), 0.1),
        'conv_w': nrm(ks[14], (DEPTH, CONV_CH, SSM_CONV), SSM_CONV ** -0.5),
        'conv_b': nrm(ks[15], (DEPTH, CONV_CH), 0.01),
        'dt_bias': dt_bias,
        'a_log': a_log,
        'd_skip': 1.0 + nrm(ks[16], (DEPTH, SSM_HEADS), 0.01),
        'g_ssm': 1.0 + nrm(ks[17], (DEPTH, SSM_W), 0.01),
        'w_s': nrm(ks[18], (DEPTH, SGU_GROUPS, SGU_CHUNK, SGU_CHUNK), SGU_CHUNK ** -0.5),
        'b_s': nrm(ks[19], (DEPTH, SGU_GROUPS, SGU_CHUNK), 0.01),
        'g_sgu': 1.0 + nrm(ks[22], (DEPTH, SGU_W), 0.01),
        'w_br_a': nrm(ks[23], (DEPTH, NA_W, D), NA_W ** -0.5),
        'w_br_b': nrm(ks[24], (DEPTH, SSM_W, D), SSM_W ** -0.5),
        'w_br_c': nrm(ks[25], (DEPTH, SGU_W, D), SGU_W ** -0.5),
        'w_out': nrm(ks[26], (DEPTH, D, D), D ** -0.5),
    }


def reference(x_prompt, x_sample, c, cache_k, cache_v, state_ssm_fwd, state_ssm_bwd, c_ctx,
              w_mod, b_mod, g_pre, g_post, w_in, rpb, conv_w, conv_b, dt_bias, a_log, d_skip,
              g_ssm, w_s, b_s, g_sgu, w_br_a, w_br_b, w_br_c, w_out):
    y_p = x_prompt
    y_s = x_sample
    ks_l, vs_l, hf_l, hb_l = [], [], [], []
    for l in range(DEPTH):
        p = {'w_mod': w_mod[l], 'b_mod': b_mod[l], 'g_pre': g_pre[l], 'g_post': g_post[l],
             'w_in': w_in[l], 'rpb': rpb[l], 'conv_w': conv_w[l], 'conv_b': conv_b[l],
             'dt_bias': dt_bias[l], 'a_log': a_log[l], 'd_skip': d_skip[l], 'g_ssm': g_ssm[l],
             'w_s': w_s[l], 'b_s': b_s[l], 'g_sgu': g_sgu[l], 'w_br_a': w_br_a[l],
             'w_br_b': w_br_b[l], 'w_br_c': w_br_c[l], 'w_out': w_out[l]}
        y_p, (k_l, v_l, h_f, h_b) = _layer(y_p, c_ctx[None, :], p, None)
        ks_l.append(k_l)
        vs_l.append(v_l)
        hf_l.append(h_f)
        hb_l.append(h_b)
        y_s, _ = _layer(y_s, c, p, (cache_k[:, l], cache_v[:, l], state_ssm_fwd[:, l], state_ssm_bwd[:, l]))
    new_cache_k = jnp.stack(ks_l, axis=1)
    new_cache_v = jnp.stack(vs_l, axis=1)
    new_state_fwd = jnp.stack(hf_l, axis=1)
    new_state_bwd = jnp.stack(hb_l, axis=1)
    return (y_p, y_s, new_cache_k, new_cache_v, new_state_fwd, new_state_bwd)
```

```python
import os
import types
import numpy as np
from contextlib import ExitStack
import concourse.bass as bass
import concourse.mybir as mybir
from concourse.bass_utils import run_bass_kernel_spmd

F32 = mybir.dt.float32
BF16 = mybir.dt.bfloat16
AF = mybir.ActivationFunctionType
ALU = mybir.AluOpType
AX = mybir.AxisListType

D = 2048
NTOK = 2560
NPT = 512
NST = 2048
NIN = 16416
DEPTH = 2
EPS = 1e-6
NEG = -30000.0
OFF = dict(q=0, k=1024, v=2048, ga=3072, xbc=4096, z=6144, dt=7168, u=7200, vc=8224, gc=9248, gm=10272)
N_CORES = 8


class Buf:
    __slots__ = ("w", "r", "name")

    def __init__(self, name=""):
        self.w = None
        self.r = {}
        self.name = name


class KB:
    def __init__(self, nc, es, nds=28):
        self.nc = nc
        self.E = {"pe": nc.tensor, "act": nc.scalar, "dve": nc.vector, "pool": nc.gpsimd, "sp": nc.sync}
        self.sem = {}
        self.cnt = {}
        for k in ("pe", "act", "dve", "pool"):
            self.sem[k] = es.enter_context(nc.semaphore("s_" + k))
            self.cnt[k] = 0
        for i in range(nds):
            self.sem[("d", i)] = es.enter_context(nc.semaphore("sd%d" % i))
            self.cnt[("d", i)] = 0
        self.nds = nds
        self.dnext = {"sp": 0, "pool": 0, "act": 0}
        self.seen = {e: {} for e in self.E}

    def wait(self, eng, toks):
        seen = self.seen[eng]
        for t in toks:
            k, v = t
            if v <= 0 or seen.get(k, 0) >= v:
                continue
            if k == "pe" and eng == "pe":
                continue
            self.E[eng].wait_ge(self.sem[k], v)
            seen[k] = v

    @staticmethod
    def deps(reads, writes):
        toks = []
        for b in reads:
            if b.w is not None:
                toks.append(b.w)
        for b in writes:
            if b.w is not None:
                toks.append(b.w)
            toks.extend(b.r.items())
        return toks

    @staticmethod
    def mark(tok, reads, writes):
        k, v = tok
        for b in reads:
            if b.r.get(k, 0) < v:
                b.r[k] = v
        for b in writes:
            b.w = tok
            b.r = {}

    def op(self, eng, fn, reads=(), writes=(), inc=True, disjoint=()):
        if eng == "pool" and not os.environ.get("KDBG_POOLC"):
            eng = "dve"
        toks = self.deps(reads, [b for b in writes if b not in disjoint])
        for b in disjoint:
            if b.w is not None and b.w[0] != eng:
                toks.append(b.w)
            toks.extend(b.r.items())
        self.wait(eng, toks)
        ins = fn(self.E[eng])
        if inc:
            self.cnt[eng] += 1
            ins.then_inc(self.sem[eng], 1)
            tok = (eng, self.cnt[eng])
        else:
            tok = (eng, self.cnt[eng] + 1)
        for b in disjoint:
            keep = dict(b.r)
            self.mark(tok, [], [b])
            b.r = keep
        self.mark(tok, reads, [b for b in writes if b not in disjoint])
        return tok

    def dma(self, q, out, in_, reads=(), writes=()):
        lo, n = {"sp": (0, int(os.environ.get("KDBG_NSP", "16"))), "pool": (16, 8), "act": (24, 4)}[q]
        j = self.dnext[q]
        self.dnext[q] = (j + 1) % n
        i = lo + j
        k = ("d", i)
        toks = self.deps(reads, writes)
        toks.append((k, self.cnt[k]))
        self.wait(q, toks)
        self.cnt[k] += 16
        self.E[q].dma_start(out=out, in_=in_).then_inc(self.sem[k], 16)
        tok = (k, self.cnt[k])
        self.mark(tok, reads, writes)
        return tok

    def barrier(self):
        toks = [(k, v) for k, v in self.cnt.items() if v > 0]
        for e in self.E:
            self.wait(e, toks)


def build(n_layers=DEPTH, stop=None, dbg=(), dbg_in=(), start=None, skip=()):
    nc = bass.Bass("TRN2", target_bir_lowering=False)

    def din(name, shape):
        return nc.dram_tensor(name, list(shape), F32, kind="ExternalInput").ap()

    def dout(name, shape):
        return nc.dram_tensor(name, list(shape), F32, kind="ExternalOutput").ap()

    def dscr(name, shape, dt):
        kind = "ExternalOutput" if name in dbg else ("ExternalInput" if name in dbg_in else "Internal")
        return nc.dram_tensor(name, list(shape), dt, kind=kind).ap()

    x_in_all = din("x_in", [2, NTOK, D])
    cvT = din("cvT", [128, 16, 2])
    ck = din("ck", [DEPTH, 256, 1024])
    cv = din("cv", [DEPTH, 256, 1024])
    st0 = din("st0", [DEPTH, 2, 1024, 128])
    w_mod = din("w_mod", [DEPTH, D, 3 * D])
    b_mod = din("b_mod", [DEPTH, 3 * D])
    g_pre = din("g_pre", [DEPTH, D])
    g_post = din("g_post", [DEPTH, D])
    w_in = din("w_in", [DEPTH, D, NIN])
    Fb = din("Fb", [DEPTH, 2, 128, 16 * 6 * 128])
    rm_all = din("rm", [2, 5, 128, 6 * 128])
    cwT = din("cwT", [DEPTH, 128, 16, 3])
    cbT = din("cbT", [DEPTH, 128, 16])
    dt_bias = din("dt_bias", [DEPTH, 32])
    a_log = din("a_log", [DEPTH, 32])
    d_skip = din("d_skip", [DEPTH, 16])
    g_ssm = din("g_ssm", [DEPTH, 1024])
    wsT = din("wsT", [DEPTH, 128, 8, 128])
    b_s = din("b_s", [DEPTH, 1024])
    g_sguT = din("g_sguT", [DEPTH, 128, 8])
    w_br = [din("w_br_a", [DEPTH, 1024, D]), din("w_br_b", [DEPTH, 1024, D]), din("w_br_c", [DEPTH, 1024, D])]
    w_out = din("w_out", [DEPTH, D, D])
    consts = din("consts", [128, 6, 128])
    y_out_all = dout("y_out", [2, NTOK, D])
    ko_all = dout("ko", [2, DEPTH, NPT, 1024])
    vo_all = dout("vo", [2, DEPTH, NPT, 1024])
    sfo_all = dout("sfo", [2, DEPTH, 2, 1024, 128])
    m_s = dscr("m_s", [2, 3 * D], F32)
    sbo_all = dout("sbo", [2, DEPTH, 2, 1024, 128])
    es = ExitStack()
    with es:
        kb = KB(nc, es)
        PS = [es.enter_context(nc.psum_tensor("ps%d" % i, [128, 512], F32)) for i in range(8)]
        PB = [Buf("ps%d" % i) for i in range(8)]

        def PSb(i):
            return PS[i][:].bitcast(BF16)

        cst = es.enter_context(nc.sbuf_tensor("cst", [128, 6, 128], F32))
        cstb = es.enter_context(nc.sbuf_tensor("cstb", [128, 6, 128], BF16))
        B_cst = Buf("cst")
        kb.dma("sp", cst[:], consts[:, :, :], writes=[B_cst])
        kb.op("dve", lambda e: e.tensor_copy(out=cstb[:], in_=cst[:]), reads=[B_cst], writes=[B_cst])
        identb = cstb[:, 0, :]
        identf = cst[:, 0, :]

        uid = [0]

        def sbuf_alloc(stack):
            def f(name, shape, dt):
                uid[0] += 1
                return stack.enter_context(nc.sbuf_tensor("%s_%d" % (name, uid[0]), list(shape), dt))
            return f

        def MM(out, lhsT, rhs, start, stop, R, W, inc):
            return kb.op("pe", lambda e: e.matmul(out, lhsT=lhsT, rhs=rhs, start=start, stop=stop), R, W, inc)

        def TR(out, in_, ident, R, W, inc=True):
            return kb.op("pe", lambda e: e.transpose(out, in_, ident), R, W, inc)

        def ACT(out, in_, func, R, W, bias=None, scale=None, accum_out=None, DJ=()):
            kw = {}
            if bias is not None:
                kw["bias"] = bias
            if scale is not None:
                kw["scale"] = scale
            if accum_out is not None:
                kw["accum_out"] = accum_out
            return kb.op("act", lambda e: e.activation(out=out, in_=in_, func=func, **kw), R, W, disjoint=DJ)

        def TT(eng, out, in0, in1, op, R, W):
            return kb.op(eng, lambda e: e.tensor_tensor(out=out, in0=in0, in1=in1, op=op), R, W)

        def TS(eng, out, in0, s1, s2, op0, op1, R, W):
            if op1 is None:
                return kb.op(eng, lambda e: e.tensor_scalar(out=out, in0=in0, scalar1=s1, scalar2=None, op0=op0), R, W)
            return kb.op(eng, lambda e: e.tensor_scalar(out=out, in0=in0, scalar1=s1, scalar2=s2, op0=op0, op1=op1), R, W)

        def STT(out, in0, scalar, in1, op0, op1, R, W, DJ=()):
            return kb.op("dve", lambda e: e.scalar_tensor_tensor(out=out, in0=in0, scalar=scalar, in1=in1, op0=op0, op1=op1), R, W, disjoint=DJ)

        def CP(eng, out, in_, R, W):
            if eng == "act" and os.environ.get("KDBG_NOACTCP"):
                eng = "dve"
            if eng == "act":
                return kb.op("act", lambda e: e.activation(out=out, in_=in_, func=AF.Identity), R, W)
            return kb.op(eng, lambda e: e.tensor_copy(out=out, in_=in_), R, W)

        SBm = Buf("m")
        REG = {}

        def make_slot(slot):
            x_in = x_in_all[slot]
            y_out = y_out_all[slot]
            ko, vo, sfo, sbo = ko_all[slot], vo_all[slot], sfo_all[slot], sbo_all[slot]
            rm = rm_all[slot]
            y1_s = dscr("s%d_" % slot + "y1_s", [NTOK, D], F32)
            qT_s = dscr("s%d_" % slot + "qT_s", [1024, NTOK], BF16)
            kT_s = dscr("s%d_" % slot + "kT_s", [1024, NTOK], BF16)
            v_s = dscr("s%d_" % slot + "v_s", [NTOK, 1024], BF16)
            ga_s = dscr("s%d_" % slot + "ga_s", [NTOK, 1024], BF16)
            xbcT_s = dscr("s%d_" % slot + "xbcT_s", [2048, NTOK], BF16)
            zs_s = dscr("s%d_" % slot + "zs_s", [NTOK, 1024], BF16)
            dt_s = dscr("s%d_" % slot + "dt_s", [128, 20 * 32], F32)
            uT_s = dscr("s%d_" % slot + "uT_s", [1024, NTOK], BF16)
            vc_s = dscr("s%d_" % slot + "vc_s", [NTOK, 1024], BF16)
            gcT_s = dscr("s%d_" % slot + "gcT_s", [1024, NTOK], BF16)
            gmT_s = dscr("s%d_" % slot + "gmT_s", [3 * D, NTOK], BF16)
            oT_s = [dscr("s%d_" % slot + "oaT_s", [1024, NTOK], BF16), dscr("s%d_" % slot + "obT_s", [1024, NTOK], BF16), dscr("s%d_" % slot + "ocT_s", [1024, NTOK], BF16)]
            mgT_s = dscr("s%d_" % slot + "mgT_s", [D, NTOK], BF16)
            ya_s = dscr("s%d_" % slot + "ya_s", [NTOK, 1024], F32)
            stc_s = dscr("s%d_" % slot + "stc_s", [20, 2, 128, 1024], F32)
            hpb_s = dscr("s%d_" % slot + "hpb_s", [20, 128, 1024], BF16)
            CT_s = dscr("s%d_" % slot + "CT_s", [128, 4 * NTOK], BF16)
            ea_s = dscr("s%d_" % slot + "ea_s", [128, 640], F32)
            cd_s = dscr("s%d_" % slot + "cd_s", [128, 640], F32)
            SB = {n: Buf(n) for n in ("m", "q", "k", "v", "ga", "xbc", "z", "dt", "u", "vc", "gc", "gm", "oa", "ob",
                                      "oc", "mg", "ya", "stc", "hpb", "y1", "ko", "vo", "sfo", "sbo", "yout", "ct", "ea", "cd")}
            SB["m"] = SBm

            def phase_mod(l):
                with ExitStack() as st:
                    sb = sbuf_alloc(st)
                    cvt = sb("cvt", [128, 16, 2], F32)
                    scT = sb("scT", [128, 16, 2], BF16)
                    bm = sb("bm", [2, 3 * D], F32)
                    mrow = sb("mrow", [2, 3 * D], F32)
                    wb = [sb("wm%d" % i, [128, 16, 512], BF16) for i in range(3)]
                    Bw = [Buf() for _ in range(3)]
                    Bc, Bs, Bb, Bm = Buf(), Buf(), Buf(), Buf()
                    kb.dma("sp", cvt[:], cvT[:, :, :], writes=[Bc])
                    ACT(scT[:], cvt[:], AF.Silu, [Bc], [Bs])
                    kb.dma("sp", bm[:], b_mod[l:l + 1, :].partition_broadcast(2), writes=[Bb])
                    wv = w_mod[l].rearrange("(kc p) f -> p kc f", p=128)
                    for cb in range(12):
                        w = wb[cb % 3]
                        kb.dma("pool", w[:], wv[:, :, cb * 512:(cb + 1) * 512], writes=[Bw[cb % 3]])
                        for kc in range(16):
                            MM(PS[0][0:2, :], scT[:, kc, :], w[:, kc, :], kc == 0, kc == 15, [Bs, Bw[cb % 3]], [PB[0]], kc == 15)
                        TT("dve", mrow[:, cb * 512:(cb + 1) * 512], PS[0][0:2, :], bm[:, cb * 512:(cb + 1) * 512], ALU.add,
                           [PB[0], Bb], [Bm])
                    kb.dma("sp", m_s[:, :], mrow[:], reads=[Bm], writes=[SB["m"]])

            def phase_norm(l, xsrc, Bx, hT, BhT):
                with ExitStack() as st:
                    sb = sbuf_alloc(st)
                    gp = sb("gp", [128, D], F32)
                    A = [sb("A%d" % g, [128, D], F32) for g in range(2)]
                    Bs_ = [sb("Bs%d" % g, [128, D], F32) for g in range(2)]
                    xt = [sb("xt%d" % i, [128, D], F32) for i in range(2)]
                    hx = [sb("hx%d" % i, [128, D], BF16) for i in range(2)]
                    junk = sb("junk", [128, D], BF16)
                    ss = [sb("ss%d" % i, [128, 4], F32) for i in range(2)]
                    Bgp, BA, BBs = Buf(), [Buf(), Buf()], [Buf(), Buf()]
                    Bxt, Bhx, Bj, Bss = [Buf(), Buf()], [Buf(), Buf()], Buf(), [Buf(), Buf()]
                    kb.dma("sp", gp[:], g_pre[l:l + 1, :].partition_broadcast(128), writes=[Bgp])
                    for g in range(2):
                        kb.dma("sp", A[g][:], m_s[g:g + 1, D:2 * D].partition_broadcast(128), reads=[SB["m"]], writes=[BA[g]])
                        kb.dma("sp", Bs_[g][:], m_s[g:g + 1, 0:D].partition_broadcast(128), reads=[SB["m"]], writes=[BBs[g]])
                        STT(A[g][:], A[g][:], 1.0, gp[:], ALU.add, ALU.mult, [Bgp], [BA[g]])
                    for ts in range(20):
                        g = 0 if ts < 4 else 1
                        i = ts % 2
                        kb.dma("sp", xt[i][:], xsrc[ts * 128:(ts + 1) * 128, :], reads=[Bx], writes=[Bxt[i]])
                        ACT(junk[:], xt[i][:], AF.Square, [Bxt[i]], [Bj, Bss[i]], accum_out=ss[i][:, 0:1])
                        TS("dve", ss[i][:, 1:2], ss[i][:, 0:1], 1.0 / D, EPS, ALU.mult, ALU.add, [], [Bss[i]])
                        ACT(ss[i][:, 2:3], ss[i][:, 1:2], AF.Sqrt, [], [Bss[i]])
                        kb.op("dve", lambda e: e.reciprocal(out=ss[i][:, 3:4], in_=ss[i][:, 2:3]), [], [Bss[i]])
                        STT(xt[i][:], xt[i][:], ss[i][:, 3:4], A[g][:], ALU.mult, ALU.mult, [Bss[i], BA[g]], [Bxt[i]])
                        TT("dve", hx[i][:], xt[i][:], Bs_[g][:], ALU.add, [Bxt[i], BBs[g]], [Bhx[i]])
                        for half in range(2):
                            bk = 2 * i + half
                            for j in range(8):
                                kc = half * 8 + j
                                TR(PSb(bk)[:, j * 128:(j + 1) * 128], hx[i][:, kc * 128:(kc + 1) * 128], identb,
                                   [Bhx[i], B_cst], [PB[bk]], j == 7)
                            CP("act" if half == 0 else "dve",
                               hT[:, half * 8:(half + 1) * 8, ts * 128:(ts + 1) * 128],
                               PSb(bk).rearrange("p (a b) -> p a b", a=8), [PB[bk]], [BhT[ts]])

            def phase_proj(l, hT, BhT):
                with ExitStack() as st:
                    sb = sbuf_alloc(st)
                    wb = [sb("wp%d" % i, [128, 16, 512], BF16) for i in range(3)]
                    Bw = [Buf() for _ in range(3)]
                    wdt = sb("wdt", [128, 16, 32], BF16)
                    Bwdt = Buf()
                    sfm = [sb("sfm%d" % i, [128, NTOK], BF16) for i in range(3)]
                    Bsfm = [Buf() for _ in range(3)]
                    stm = [sb("stm%d" % i, [128, 512], BF16) for i in range(4)]
                    Bstm = [Buf() for _ in range(4)]
                    s32 = [sb("s32%d" % i, [128, 512], F32) for i in range(2)]
                    Bs32 = [Buf() for _ in range(2)]
                    dts = sb("dts", [128, 20, 32], F32)
                    Bdts = Buf()
                    wv = w_in[l].rearrange("(kc p) f -> p kc f", p=128)
                    blocks = []

                    def addF(c0, n, scr, key, ev):
                        for b in range(n // 512):
                            blocks.append(("F", c0 + b * 512, scr, key, b * 512, ev))

                    def addT(c0, n, scr, key, ev):
                        for b in range(n // 512):
                            blocks.append(("T", c0 + b * 512, scr, key, b * 512, ev))
                    addF(OFF["q"], 1024, qT_s, "q", "qs")
                    addF(OFF["k"], 1024, kT_s, "k", "cp")
                    addT(OFF["v"], 1024, v_s, "v", "cp")
                    addT(OFF["ga"], 1024, ga_s, "ga", "silu")
                    addF(OFF["xbc"], 2048, xbcT_s, "xbc", "cp")
                    addT(OFF["z"], 1024, zs_s, "z", "silu")
                    addF(OFF["u"], 1024, uT_s, "u", "cp")
                    addT(OFF["vc"], 1024, vc_s, "vc", "cp")
                    addF(OFF["gc"], 1024, gcT_s, "gc", "silu")
                    addF(OFF["gm"], 6144, gmT_s, "gm", "sig")
                    nb = len(blocks)
                    if os.environ.get("KDBG_NB"):
                        blocks = blocks[:int(os.environ["KDBG_NB"])]
                        nb = len(blocks)
                    state = dict(bank=0, fm=0, tm=0, s32=0, ev=0)

                    def load(bi):
                        c0 = blocks[bi][1]
                        kb.dma("pool", wb[bi % 3][:], wv[:, :, c0:c0 + 512], writes=[Bw[bi % 3]])

                    def nbank():
                        b = state["bank"]
                        state["bank"] = (b + 1) % int(os.environ.get("KDBG_NBANK", "7"))
                        return b

                    def evac(ev, out, bank, W):
                        if ev == "silu":
                            ACT(out, PS[bank][:], AF.Silu, [PB[bank]], W)
                        elif ev == "sig":
                            ACT(out, PS[bank][:], AF.Sigmoid, [PB[bank]], W)
                        elif ev == "qs":
                            TS("dve", out, PS[bank][:], 0.125, None, ALU.mult, None, [PB[bank]], W)
                        else:
                            CP("dve", out, PS[bank][:], [PB[bank]], W)

                    if not os.environ.get("KDBG_NODT"):
                        kb.dma("pool", wdt[:], wv[:, :, OFF["dt"]:OFF["dt"] + 32], writes=[Bwdt])
                    load(0)
                    load(1)
                    for bi in range(nb):
                        lay, c0, scr, key, off, ev = blocks[bi]
                        if bi + 2 < nb:
                            load(bi + 2)
                        w = wb[bi % 3]
                        BW = Bw[bi % 3]
                        if lay == "F":
                            for fcl in range(4):
                                si = state["fm"]
                                state["fm"] = (si + 1) % 3
                                for tt in range(5):
                                    bk = nbank()
                                    for kc in range(16):
                                        MM(PS[bk][:], w[:, kc, fcl * 128:(fcl + 1) * 128], hT[:, kc, tt * 512:(tt + 1) * 512],
                                           kc == 0, kc == 15, [BW, BhT], [PB[bk]], kc == 15)
                                    evac(ev, sfm[si][:, tt * 512:(tt + 1) * 512], bk, [Bsfm[si]])
                                r0 = off + fcl * 128
                                kb.dma("sp", scr[r0:r0 + 128, :], sfm[si][:], reads=[Bsfm[si]], writes=[SB[key]])
                            if key == "k":
                                for ts in range(4):
                                    bk = nbank()
                                    for kc in range(16):
                                        MM(PS[bk][:], hT[:, kc, ts * 128:(ts + 1) * 128], w[:, kc, :], kc == 0, kc == 15,
                                           [BW, BhT], [PB[bk]], kc == 15)
                                    j = state["s32"]
                                    state["s32"] = j ^ 1
                                    CP("dve", s32[j][:], PS[bk][:], [PB[bk]], [Bs32[j]])
                                    kb.dma("sp", ko[l, ts * 128:(ts + 1) * 128, off:off + 512], s32[j][:], reads=[Bs32[j]],
                                           writes=[SB["ko"]])
                        else:
                            for ts in range(20):
                                bk = nbank()
                                for kc in range(16):
                                    MM(PS[bk][:], hT[:, kc, ts * 128:(ts + 1) * 128], w[:, kc, :], kc == 0, kc == 15,
                                       [BW, BhT], [PB[bk]], kc == 15)
                                si = state["tm"]
                                state["tm"] = (si + 1) % 4
                                if key == "v" and ts < 4:
                                    j = state["s32"]
                                    state["s32"] = j ^ 1
                                    CP("dve", s32[j][:], PS[bk][:], [PB[bk]], [Bs32[j]])
                                    kb.dma("sp", vo[l, ts * 128:(ts + 1) * 128, off:off + 512], s32[j][:], reads=[Bs32[j]],
                                           writes=[SB["vo"]])
                                evac(ev, stm[si][:], bk, [Bstm[si]])
                                kb.dma("sp", scr[ts * 128:(ts + 1) * 128, off:off + 512], stm[si][:], reads=[Bstm[si]],
                                       writes=[SB[key]])
                        if key == "z" and off == 512 and not os.environ.get("KDBG_NODT"):
                            for ts in range(20):
                                bk = nbank()
                                for kc in range(16):
                                    MM(PS[bk][:, 0:32], hT[:, kc, ts * 128:(ts + 1) * 128], wdt[:, kc, :], kc == 0, kc == 15,
                                       [Bwdt, BhT], [PB[bk]], kc == 15)
                                CP("dve", dts[:, ts, :], PS[bk][:, 0:32], [PB[bk]], [Bdts])
                            kb.dma("sp", dt_s[:, :], dts[:].rearrange("p t c -> p (t c)"), reads=[Bdts], writes=[SB["dt"]])

            def phase_attn(l):
                kTv = kT_s.rearrange("(fc p) t -> p fc t", p=128)
                qTv = qT_s.rearrange("(fc p) t -> p fc t", p=128)
                oaTv = oT_s[0].rearrange("(fc p) t -> p fc t", p=128)
                with ExitStack() as st:
                    sb = sbuf_alloc(st)
                    qt = [sb("qt%d" % i, [128, 8, 128], BF16) for i in range(2)]
                    sga = [sb("sga%d" % i, [128, 1024], BF16) for i in range(2)]
                    PT = [sb("PT%d" % i, [128, 1024], BF16) for i in range(3)]
                    oa = [sb("oa%d" % i, [128, 1024], BF16) for i in range(2)]
                    rden = [sb("rden%d" % i, [128, 4], F32) for i in range(2)]
                    oaT = sb("oaT", [128, 8, 512], BF16)
                    Bqt, Bsga = [Buf(), Buf()], [Buf(), Buf()]
                    BPT, Boa, Brd, BoaT = [Buf() for _ in range(3)], [Buf(), Buf()], [Buf(), Buf()], Buf()
                    cnt = dict(q=0, pt=0, s=0, o=0, oa=0)

                    def block(tok0, chunks_fn, nch, Rk, oslot, flush):
                        qi = cnt["q"] % 2
                        cnt["q"] += 1
                        kb.dma("sp", qt[qi][:], qTv[:, :, tok0:tok0 + 128], reads=[SB["q"]], writes=[Bqt[qi]])
                        kb.dma("sp", sga[qi][:], ga_s[tok0:tok0 + 128, :], reads=[SB["ga"]], writes=[Bsga[qi]])
                        oi = cnt["oa"] % 2
                        cnt["oa"] += 1
                        pend = {}

                        def qk(h):
                            hp, fc = (h % 2) * 64, h // 2
                            sr = cnt["s"] % 2
                            cnt["s"] += 1
                            pend[h] = sr
                            for ci in range(nch):
                                bank = 2 * sr + ci // 4
                                o_ = (ci % 4) * 128
                                kap, vap, bap = chunks_fn(h, ci)
                                lastb = (ci % 4 == 3) or (ci == nch - 1)
                                MM(PS[bank][:, o_:o_ + 128], kap, qt[qi][hp:hp + 64, fc, :], True, bap is None,
                                   Rk + [Bqt[qi]], [PB[bank]], lastb and bap is None)
                                if bap is not None:
                                    MM(PS[bank][:, o_:o_ + 128], identb, bap, False, True, Rk + [B_cst], [PB[bank]], lastb)

                        def pv(h):
                            sr = pend.pop(h)
                            pi = cnt["pt"] % 3
                            cnt["pt"] += 1
                            for b in range((nch + 3) // 4):
                                ncol = min(nch - 4 * b, 4) * 128
                                ACT(PT[pi][:, b * 512:b * 512 + ncol], PS[2 * sr + b][:, 0:ncol], AF.Exp, [PB[2 * sr + b]], [BPT[pi]],
                                    DJ=([BPT[pi]] if b > 0 else ()))
                            hh = h % 4
                            ob = 4 + ((cnt["o"] // 4) % 2)
                            cnt["o"] += 1
                            Ov = PS[ob][:, 0:260].rearrange("p (a b) -> p a b", a=4)
                            for ci in range(nch):
                                kap, vap, bap = chunks_fn(h, ci)
                                MM(Ov[:, hh, :], PT[pi][:, ci * 128:(ci + 1) * 128], vap, ci == 0, ci == nch - 1,
                                   Rk + [BPT[pi]], [PB[ob]], ci == nch - 1)
                            if hh == 3:
                                ri = (h // 4) % 2
                                kb.op("dve", lambda e: e.reciprocal(out=rden[ri][:], in_=Ov[:, :, 64]), [PB[ob]], [Brd[ri]])
                                for k4 in range(4):
                                    h2 = h - 3 + k4
                                    STT(oa[oi][:, h2 * 64:(h2 + 1) * 64], Ov[:, k4, 0:64], rden[ri][:, k4:k4 + 1],
                                        sga[qi][:, h2 * 64:(h2 + 1) * 64], ALU.mult, ALU.mult,
                                        [PB[ob], Brd[ri], Bsga[qi]], [Boa[oi]], DJ=([Boa[oi]] if h2 > 0 else ()))
                        qk(0)
                        for h in range(16):
                            if h + 1 < 16:
                                qk(h + 1)
                            pv(h)
                        for fc in range(8):
                            TR(PSb(6)[:, fc * 128:(fc + 1) * 128], oa[oi][:, fc * 128:(fc + 1) * 128], identb,
                               [Boa[oi], B_cst], [PB[6]], fc == 7)
                        CP("dve", oaT[:, :, oslot * 128:(oslot + 1) * 128], PSb(6).rearrange("p (a b) -> p a b", a=8),
                           [PB[6]], [BoaT])
                        if flush is not None:
                            kb.dma("sp", oaTv[:, :, flush:flush + 512], oaT[:], reads=[BoaT], writes=[SB["oa"]])

                    with ExitStack() as st2:
                        sb2 = sbuf_alloc(st2)
                        kTp = sb2("kTp", [128, 8, 512], BF16)
                        vp = sb2("vp", [128, 4, 16, 65], BF16)
                        Bkp, Bvp = Buf(), Buf()
                        kb.dma("sp", kTp[:], kTv[:, :, 0:512], reads=[SB["k"]], writes=[Bkp])
                        kb.op("pool", lambda e: e.memset(vp[:, :, :, 64:65], 1.0), [], [Bvp])
                        for c in range(4):
                            kb.dma("sp", vp[:, c, :, 0:64], v_s[c * 128:(c + 1) * 128, :].rearrange("p (h d) -> p h d", d=64),
                                   reads=[SB["v"]], writes=[Bvp])
                        for sq in range(2):
                            for t in range(2):
                                def cf(h, ci, sq=sq):
                                    hp, fc = (h % 2) * 64, h // 2
                                    return (kTp[hp:hp + 64, fc, sq * 256 + ci * 128: sq * 256 + (ci + 1) * 128],
                                            vp[:, sq * 2 + ci, h, :], None)
                                pslot = sq * 2 + t
                                block(sq * 256 + t * 128, cf, 2, [Bkp, Bvp], pslot, 0 if pslot == 3 else None)
                        kb.barrier()
                    with ExitStack() as st2:
                        sb2 = sbuf_alloc(st2)
                        kTs = sb2("kTs", [128, 8, 2560], BF16)
                        vs = sb2("vs", [128, 20, 16, 65], BF16)
                        kcT = sb2("kcT", [128, 8, 256], BF16)
                        vcx = sb2("vcx", [128, 2, 16, 65], BF16)
                        bint = sb2("bint", [128, 16, 768], BF16)
                        bsp = sb2("bsp", [128, 16, 768], BF16)
                        rmt = [sb2("rmt%d" % i, [128, 768], BF16) for i in range(2)]
                        Bks, Bvs, Bkc, Bvc, Bbi, Bbs, Brm = Buf(), Buf(), Buf(), Buf(), Buf(), Buf(), [Buf(), Buf()]
                        oth = REG[1 - slot]
                        okTv = oth.kT_s.rearrange("(fc p) t -> p fc t", p=128)
                        kb.op("pool", lambda e: e.memset(vs[:, 0:2, :, :], 0.0), [], [Bvs])
                        kb.op("pool", lambda e: e.memset(vs[:, 18:20, :, :], 0.0), [], [Bvs])
                        kb.op("pool", lambda e: e.memset(vs[:, :, :, 64:65], 1.0), [], [Bvs])
                        if slot == 1:
                            kb.dma("sp", kTs[:, :, 0:256], okTv[:, :, 2304:2560], reads=[oth.SB["k"]], writes=[Bks])
                            for c in range(2):
                                kb.dma("sp", vs[:, c, :, 0:64],
                                       oth.v_s[2304 + c * 128:2304 + (c + 1) * 128, :].rearrange("p (h d) -> p h d", d=64),
                                       reads=[oth.SB["v"]], writes=[Bvs])
                            kb.op("pool", lambda e: e.memset(kTs[:, :, 2304:2560], 0.0), [], [Bks])
                        else:
                            kb.op("pool", lambda e: e.memset(kTs[:, :, 0:256], 0.0), [], [Bks])
                            kb.dma("sp", kTs[:, :, 2304:2560], okTv[:, :, 512:768], reads=[oth.SB["k"]], writes=[Bks])
                            for c in range(2):
                                kb.dma("sp", vs[:, 18 + c, :, 0:64],
                                       oth.v_s[512 + c * 128:512 + (c + 1) * 128, :].rearrange("p (h d) -> p h d", d=64),
                                       reads=[oth.SB["v"]], writes=[Bvs])
                        kb.dma("sp", kTs[:, :, 256:2304], kTv[:, :, 512:2560], reads=[SB["k"]], writes=[Bks])
                        for c in range(16):
                            kb.dma("sp", vs[:, 2 + c, :, 0:64],
                                   v_s[512 + c * 128:512 + (c + 1) * 128, :].rearrange("p (h d) -> p h d", d=64),
                                   reads=[SB["v"]], writes=[Bvs])
                        kb.op("pool", lambda e: e.memset(vcx[:, :, :, 64:65], 1.0), [], [Bvc])
                        with ExitStack() as st3:
                            ckb = st3.enter_context(nc.sbuf_tensor("ckb%d_%d" % (l, slot), [128, 2, 1024], BF16))
                            Bck = Buf()
                            kb.dma("pool", ckb[:], ck[l].rearrange("(c p) f -> p c f", p=128), writes=[Bck])
                            for c in range(2):
                                kb.dma("pool", vcx[:, c, :, 0:64], cv[l, c * 128:(c + 1) * 128, :].rearrange("p (h d) -> p h d", d=64),
                                       writes=[Bvc])
                                for fc in range(8):
                                    TR(PSb(6)[:, fc * 128:(fc + 1) * 128], ckb[:, c, fc * 128:(fc + 1) * 128], identb,
                                       [Bck, B_cst], [PB[6]], fc == 7)
                                CP("dve", kcT[:, :, c * 128:(c + 1) * 128], PSb(6).rearrange("p (a b) -> p a b", a=8),
                                   [PB[6]], [Bkc])
                            kb.barrier()

                        def load_bias(dst, Bdst, kind, ridx, j):
                            kb.dma("pool", dst[:].rearrange("p a b -> p (a b)"), Fb[l, kind], writes=[Bdst])
                            kb.dma("pool", rmt[j][:], rm[ridx], writes=[Brm[j]])
                            TT("dve", dst[:], dst[:], rmt[j][:].unsqueeze(1).broadcast_to([128, 16, 768]), ALU.add,
                               [Brm[j]], [Bdst])
                        load_bias(bint, Bbi, 0, 0, 0)
                        rmj = 1
                        for ml in range(16):
                            special = ml in (0, 1, 14, 15)
                            if special:
                                load_bias(bsp, Bbs, 1 if ml == 15 else 0, {0: 1, 1: 2, 14: 3, 15: 4}[ml], rmj)
                                rmj ^= 1
                            bt, Bbt = (bsp, Bbs) if special else (bint, Bbi)
                            cb0 = 14 if ml == 15 else ml
                            nloc = 6 if ml in (0, 15) else 5

                            def cf(h, ci, cb0=cb0, nloc=nloc, bt=bt):
                                hp, fc = (h % 2) * 64, h // 2
                                if ci < nloc:
                                    c = cb0 + ci
                                    return (kTs[hp:hp + 64, fc, c * 128:(c + 1) * 128], vs[:, c, h, :],
                                            bt[:, h, ci * 128:(ci + 1) * 128])
                                c = ci - nloc
                                return (kcT[hp:hp + 64, fc, c * 128:(c + 1) * 128], vcx[:, c, h, :], None)
                            block(512 + ml * 128, cf, nloc + 2, [Bks, Bvs, Bkc, Bvc, Bbt], ml % 4,
                                  512 + (ml - 3) * 128 if ml % 4 == 3 else None)
                        kb.barrier()

            def phase_sgu(l):
                uTv = uT_s.rearrange("(g e) t -> e g t", e=128)
                gcTv = gcT_s.rearrange("(g e) t -> e g t", e=128)
                ocTv = oT_s[2].rearrange("(g e) t -> e g t", e=128)
                with ExitStack() as st:
                    sb = sbuf_alloc(st)
                    wst = sb("wst", [128, 8, 128], BF16)
                    bsb = sb("bsb", [128, 8, 128], F32)
                    gsg = sb("gsg", [128, 8], F32)
                    Bw, Bb, Bg = Buf(), Buf(), Buf()
                    kb.dma("pool", wst[:], wsT[l], writes=[Bw])
                    kb.dma("sp", bsb[:].rearrange("p a b -> p (a b)"), b_s[l:l + 1, :].partition_broadcast(128), writes=[Bb])
                    kb.dma("sp", gsg[:], g_sguT[l], writes=[Bg])
                    vct = [sb("vct%d" % i, [128, 1024], BF16) for i in range(2)]
                    vn = [sb("vn%d" % i, [128, 1024], BF16) for i in range(2)]
                    stt = [sb("stt%d" % i, [128, 16], F32) for i in range(2)]
                    ut = [sb("ut%d" % i, [128, 8, 512], BF16) for i in range(2)]
                    gt = [sb("gt%d" % i, [128, 8, 512], BF16) for i in range(2)]
                    oc = [sb("oc%d" % i, [128, 8, 512], BF16) for i in range(2)]
                    tmp = [sb("tmpg%d" % i, [128, 8, 128], F32) for i in range(2)]
                    Bvct, Bvn, Bstt = [Buf(), Buf()], [Buf(), Buf()], [Buf(), Buf()]
                    But, Bgt, Boc, Btmp = [Buf(), Buf()], [Buf(), Buf()], [Buf(), Buf()], [Buf(), Buf()]
                    for tt in range(5):
                        ti = tt % 2
                        kb.dma("sp", ut[ti][:], uTv[:, :, tt * 512:(tt + 1) * 512], reads=[SB["u"]], writes=[But[ti]])
                        kb.dma("sp", gt[ti][:], gcTv[:, :, tt * 512:(tt + 1) * 512], reads=[SB["gc"]], writes=[Bgt[ti]])
                        TT("pool", ut[ti][:], ut[ti][:], gt[ti][:], ALU.mult, [Bgt[ti]], [But[ti]])
                        for sc in range(4):
                            c = tt * 4 + sc
                            i = c % 2
                            kb.dma("sp", vct[i][:], vc_s[c * 128:(c + 1) * 128, :], reads=[SB["vc"]], writes=[Bvct[i]])
                            for hf in range(2):
                                kb.op("dve", lambda e: e.bn_stats(out=stt[i][:, hf * 6:(hf + 1) * 6], in_=vct[i][:, hf * 512:(hf + 1) * 512]),
                                      [Bvct[i]], [Bstt[i]])
                            kb.op("dve", lambda e: e.bn_aggr(out=stt[i][:, 12:14], in_=stt[i][:, 0:12]), [], [Bstt[i]])
                            TS("dve", stt[i][:, 14:15], stt[i][:, 13:14], EPS, None, ALU.add, None, [], [Bstt[i]])
                            ACT(stt[i][:, 14:15], stt[i][:, 14:15], AF.Sqrt, [], [Bstt[i]])
                            kb.op("dve", lambda e: e.reciprocal(out=stt[i][:, 15:16], in_=stt[i][:, 14:15]), [], [Bstt[i]])
                            TS("dve", vn[i][:], vct[i][:], stt[i][:, 12:13], stt[i][:, 15:16], ALU.subtract, ALU.mult,
                               [Bvct[i], Bstt[i]], [Bvn[i]])
                            bks = (0, 1) if c % 2 == 0 else (2, 3)
                            for g in range(8):
                                bk = bks[g // 4]
                                MM(PS[bk][:, (g % 4) * 128:(g % 4 + 1) * 128], vn[i][:, g * 128:(g + 1) * 128], wst[:, g, :],
                                   True, True, [Bvn[i], Bw], [PB[bk]], g % 4 == 3)
                            for g in range(8):
                                bk = bks[g // 4]
                                STT(tmp[i][:, g, :], PS[bk][:, (g % 4) * 128:(g % 4 + 1) * 128], gsg[:, g:g + 1], bsb[:, g, :],
                                    ALU.mult, ALU.add, [PB[bk], Bg, Bb], [Btmp[i]])
                            TT("dve", oc[ti][:, :, sc * 128:(sc + 1) * 128], tmp[i][:], ut[ti][:, :, sc * 128:(sc + 1) * 128], ALU.mult,
                               [Btmp[i], But[ti]], [Boc[ti]])
                        kb.dma("sp", ocTv[:, :, tt * 512:(tt + 1) * 512], oc[ti][:], reads=[Boc[ti]], writes=[SB["oc"]])

            def phase_o1(l):
                gmv = gmT_s.rearrange("(br fc p) t -> p br fc t", p=128, br=3)
                mgv = mgT_s.rearrange("(fc p) t -> p fc t", p=128)
                with ExitStack() as st:
                    sb = sbuf_alloc(st)
                    wbr = [sb("wbr%d" % b, [128, 8, D], BF16) for b in range(3)]
                    Bwbr = [Buf() for _ in range(3)]
                    for b in range(3):
                        kb.dma("pool", wbr[b][:], w_br[b][l].rearrange("(kc p) f -> p kc f", p=128), writes=[Bwbr[b]])
                    ot = [[sb("ot%d_%d" % (b, i), [128, 8, 512], BF16) for b in range(3)] for i in range(2)]
                    Bot = [[Buf() for _ in range(3)] for _ in range(2)]
                    gmt = [sb("gmt%d" % i, [128, 3, 512], BF16) for i in range(3)]
                    Bgm = [Buf() for _ in range(3)]
                    acc = [sb("acc%d" % i, [128, 512], F32) for i in range(2)]
                    Bacc = [Buf(), Buf()]
                    mg = [sb("mg%d" % i, [128, 16, 512], BF16) for i in range(2)]
                    Bmg = [Buf(), Buf()]
                    nbk = 0
                    for tt in range(5):
                        ti = tt % 2
                        for b in range(3):
                            kb.dma("sp", ot[ti][b][:], oT_s[b].rearrange("(kc p) t -> p kc t", p=128)[:, :, tt * 512:(tt + 1) * 512],
                                   reads=[SB[("oa", "ob", "oc")[b]]], writes=[Bot[ti][b]])
                        for fc in range(16):
                            gi = (tt * 16 + fc) % 3
                            ai = fc % 2
                            kb.dma("sp", gmt[gi][:], gmv[:, :, fc, tt * 512:(tt + 1) * 512], reads=[SB["gm"]], writes=[Bgm[gi]])
                            for b in range(3):
                                bk = nbk
                                nbk = (nbk + 1) % 7
                                for kc in range(8):
                                    MM(PS[bk][:], wbr[b][:, kc, fc * 128:(fc + 1) * 128], ot[ti][b][:, kc, :], kc == 0, kc == 7,
                                       [Bwbr[b], Bot[ti][b]], [PB[bk]], kc == 7)
                                if b == 0:
                                    TT("dve", acc[ai][:], PS[bk][:], gmt[gi][:, 0, :], ALU.mult, [PB[bk], Bgm[gi]], [Bacc[ai]])
                                elif b == 1:
                                    t2 = TT("dve", PS[bk][:], PS[bk][:], gmt[gi][:, 1, :], ALU.mult, [Bgm[gi]], [PB[bk]])
                                    TT("dve", acc[ai][:], acc[ai][:], PS[bk][:], ALU.add, [PB[bk]], [Bacc[ai]])
                                else:
                                    TT("dve", PS[bk][:], PS[bk][:], gmt[gi][:, 2, :], ALU.mult, [Bgm[gi]], [PB[bk]])
                                    TT("dve", mg[ti][:, fc, :], acc[ai][:], PS[bk][:], ALU.add, [PB[bk], Bacc[ai]], [Bmg[ti]])
                        kb.dma("sp", mgv[:, :, tt * 512:(tt + 1) * 512], mg[ti][:], reads=[Bmg[ti]], writes=[SB["mg"]])

            def phase_o2(l, xsrc, Bx, ydst, By):
                mgv = mgT_s.rearrange("(kc p) t -> p kc t", p=128)
                with ExitStack() as st:
                    sb = sbuf_alloc(st)
                    wo = sb("wo", [128, 16, D], BF16)
                    Bwo = Buf()
                    kb.dma("pool", wo[:], w_out[l].rearrange("(kc p) f -> p kc f", p=128), writes=[Bwo])
                    gpo = sb("gpo", [128, D], F32)
                    G = [sb("G%d" % g, [128, D], F32) for g in range(2)]
                    Bgpo, BG = Buf(), [Buf(), Buf()]
                    kb.dma("sp", gpo[:], g_post[l:l + 1, :].partition_broadcast(128), writes=[Bgpo])
                    for g in range(2):
                        kb.dma("sp", G[g][:], m_s[g:g + 1, 2 * D:3 * D].partition_broadcast(128), reads=[SB["m"]], writes=[BG[g]])
                        TT("dve", G[g][:], G[g][:], gpo[:], ALU.mult, [Bgpo], [BG[g]])
                    mgt = [sb("mgt%d" % i, [128, 16, 128], BF16) for i in range(2)]
                    xt = [sb("xo%d" % i, [128, D], F32) for i in range(2)]
                    zt = [sb("zt%d" % i, [128, D], F32) for i in range(2)]
                    junk = sb("junk2", [128, 512], BF16)
                    ss = [sb("sso%d" % i, [128, 8], F32) for i in range(2)]
                    Bmgt, Bxt, Bzt, Bj, Bss = [Buf(), Buf()], [Buf(), Buf()], [Buf(), Buf()], Buf(), [Buf(), Buf()]
                    for ts in range(20):
                        i = ts % 2
                        g = 0 if ts < 4 else 1
                        kb.dma("sp", mgt[i][:], mgv[:, :, ts * 128:(ts + 1) * 128], reads=[SB["mg"]], writes=[Bmgt[i]])
                        kb.dma("sp", xt[i][:], xsrc[ts * 128:(ts + 1) * 128, :], reads=[Bx], writes=[Bxt[i]])
                        bks = (0, 1, 2, 3) if i == 0 else (4, 5, 6, 0)
                        for cb in range(4):
                            bk = bks[cb]
                            for kc in range(16):
                                MM(PS[bk][:], mgt[i][:, kc, :], wo[:, kc, cb * 512:(cb + 1) * 512], kc == 0, kc == 15,
                                   [Bmgt[i], Bwo], [PB[bk]], kc == 15)
                            CP("dve", zt[i][:, cb * 512:(cb + 1) * 512], PS[bk][:], [PB[bk]], [Bzt[i]])
                            ACT(junk[:], zt[i][:, cb * 512:(cb + 1) * 512], AF.Square, [Bzt[i]], [Bj, Bss[i]], accum_out=ss[i][:, cb:cb + 1])
                        kb.op("dve", lambda e: e.tensor_reduce(out=ss[i][:, 4:5], in_=ss[i][:, 0:4], axis=AX.X, op=ALU.add), [], [Bss[i]])
                        TS("dve", ss[i][:, 5:6], ss[i][:, 4:5], 1.0 / D, EPS, ALU.mult, ALU.add, [], [Bss[i]])
                        ACT(ss[i][:, 6:7], ss[i][:, 5:6], AF.Sqrt, [], [Bss[i]])
                        kb.op("dve", lambda e: e.reciprocal(out=ss[i][:, 7:8], in_=ss[i][:, 6:7]), [], [Bss[i]])
                        STT(zt[i][:], zt[i][:], ss[i][:, 7:8], G[g][:], ALU.mult, ALU.mult, [Bss[i], BG[g]], [Bzt[i]])
                        TT("dve", zt[i][:], zt[i][:], xt[i][:], ALU.add, [Bxt[i]], [Bzt[i]])
                        kb.dma("sp", ydst[ts * 128:(ts + 1) * 128, :], zt[i][:], reads=[Bzt[i]], writes=[By])

            def phase_ssm(l):
                xbv = xbcT_s.rearrange("(fc p) t -> p fc t", p=128)
                obTv = oT_s[1].rearrange("(fc p) t -> p fc t", p=128)
                triu_b, tril_b, ones_b = cstb[:, 1, :], cstb[:, 2, :], cstb[:, 5, :]
                sgt_f, slt_f = cst[:, 3, :], cst[:, 4, :]
                one_col = cst[:, 5, 0:1]
                with ExitStack() as st:
                    sb = sbuf_alloc(st)
                    cw = sb("cw", [128, 16, 3], F32)
                    cbs = sb("cbs", [128, 16], F32)
                    dtb = sb("dtb", [128, 32], F32)
                    abc = sb("abc", [128, 32], F32)
                    dsk = sb("dsk", [128, 16], F32)
                    gss = sb("gss", [128, 1024], F32)
                    CT = sb("CTall", [128, 4, NTOK], BF16)
                    ea = sb("ea", [128, 20, 32], F32)
                    cd = sb("cd", [128, 20, 32], F32)
                    Bcw, Bdtb, Babc, Bdsk, Bgss, BCT, Bea, Bcd = (Buf() for _ in range(8))
                    kb.dma("sp", cw[:], cwT[l], writes=[Bcw])
                    kb.dma("sp", cbs[:], cbT[l], writes=[Bcw])
                    kb.dma("sp", dtb[:], dt_bias[l:l + 1, :].partition_broadcast(128), writes=[Bdtb])
                    kb.dma("sp", abc[:], a_log[l:l + 1, :].partition_broadcast(128), writes=[Babc])
                    ACT(abc[:], abc[:], AF.Exp, [], [Babc])
                    TS("dve", abc[:], abc[:], -1.0, None, ALU.mult, None, [], [Babc])
                    kb.dma("sp", dsk[:], d_skip[l:l + 1, :].partition_broadcast(128), writes=[Bdsk])
                    dtall = sb("dtall", [128, 20, 32], F32)
                    Bdtall = Buf()
                    kb.dma("sp", dtall[:].rearrange("p a b -> p (a b)"), dt_s[:, :], reads=[SB["dt"]], writes=[Bdtall])
                    TT("dve", dtall[:], dtall[:], dtb[:].unsqueeze(1).broadcast_to([128, 20, 32]), ALU.add, [Bdtb], [Bdtall])
                    ACT(dtall[:], dtall[:], AF.Exp, [], [Bdtall])
                    ACT(dtall[:], dtall[:], AF.Ln, [B_cst], [Bdtall], bias=one_col)
                    dt2 = sb("dt2", [128, 2, 20, 16], F32)
                    adt2 = sb("adt2", [128, 2, 20, 16], F32)
                    abk = sb("abk", [128, 2, 20, 16], F32)
                    ahi = sb("ahi", [128, 2, 20, 16], BF16)
                    alo = sb("alo", [128, 2, 20, 16], BF16)
                    ea2 = sb("ea2", [128, 2, 20, 16], F32)
                    cd2 = sb("cd2", [128, 2, 20, 16], F32)
                    w2 = sb("w2", [128, 2, 20, 16], F32)
                    Bsc = Buf()
                    for d in range(2):
                        CP("dve", dt2[:, d], dtall[:, :, d * 16:(d + 1) * 16], [Bdtall], [Bsc])
                        TT("dve", adt2[:, d], dt2[:, d], abc[:, d * 16:(d + 1) * 16].unsqueeze(1).broadcast_to([128, 20, 16]), ALU.mult,
                           [Babc], [Bsc])
                    CP("dve", ahi[:], adt2[:], [], [Bsc])
                    CP("dve", abk[:], ahi[:], [], [Bsc])
                    TT("dve", alo[:], adt2[:], abk[:], ALU.subtract, [], [Bsc])
                    for d in range(2):
                        V = triu_b if d == 0 else tril_b
                        for x, a_ in enumerate((ahi, alo)):
                            MM(PS[d][:, 0:320], V, a_[:, d].rearrange("p c h -> p (c h)"), x == 0, x == 1, [Bsc, B_cst], [PB[d]], x == 1)
                            MM(PS[2 + d][:, 0:320], ones_b, a_[:, d].rearrange("p c h -> p (c h)"), x == 0, x == 1, [Bsc, B_cst], [PB[2 + d]], x == 1)
                    for d in range(2):
                        f2 = lambda t: t[:, d].rearrange("p c h -> p (c h)")
                        ACT(f2(ea2), PS[d][:, 0:320], AF.Exp, [PB[d]], [Bsc])
                        ACT(f2(cd2), PS[2 + d][:, 0:320], AF.Exp, [PB[2 + d]], [Bsc])
                        CP("dve", f2(abk), PS[d][:, 0:320], [PB[d]], [Bsc])
                        TT("dve", f2(abk), PS[2 + d][:, 0:320], f2(abk), ALU.subtract, [PB[2 + d]], [Bsc])
                    ACT(abk[:], abk[:], AF.Exp, [], [Bsc])
                    TT("dve", w2[:], abk[:], dt2[:], ALU.mult, [], [Bsc])
                    Bea = Bsc
                    Bcd = Bsc
                    kb.dma("sp", gss[:], g_ssm[l:l + 1, :].partition_broadcast(128), writes=[Bgss])
                    SSTOP = int(os.environ.get("KDBG_SSTOP", "99"))
                    PBC = os.environ.get("KDBG_PBC", "dve")
                    if SSTOP <= 1:
                        kb.barrier()
                        return
                    with ExitStack() as st1:
                        sb1 = sbuf_alloc(st1)
                        xin = sb1("xin", [128, 16, 514], BF16)
                        hst = sb1("hst", [128, 16, 64], BF16)
                        Bhst = Buf()
                        xc = sb1("xc", [128, 12, 512], BF16)
                        acc = [sb1("cacc%d" % i, [128, 512], F32) for i in range(2)]
                        xs_tok = [sb1("xstok%d" % i, [128, 1024], BF16) for i in range(2)]
                        B_tok = [sb1("Btok%d" % i, [128, 4, 128], BF16) for i in range(2)]
                        sm = [sb1("sm%d" % i, [128, 8, 32], F32) for i in range(2)]
                        ahl = [sb1("ahl%d" % i, [128, 2, 32], BF16) for i in range(2)]
                        xdt = [sb1("xdt%d" % i, [128, 2, 1024], BF16) for i in range(2)]
                        xdd = [sb1("xdd%d" % i, [128, 2, 1024], BF16) for i in range(2)]
                        xsk = [sb1("xsk%d" % i, [128, 1024], F32) for i in range(2)]
                        CBm = [sb1("CBm%d" % i, [128, 2, 4, 128], BF16) for i in range(2)]
                        U = [sb1("U%d" % i, [128, 16, 128], BF16) for i in range(4)]
                        Lt = [sb1("Lt%d" % i, [128, 4, 128], BF16) for i in range(4)]
                        Mt = [sb1("Mt%d" % i, [128, 4, 128], BF16) for i in range(4)]
                        yas = [sb1("yas%d" % i, [128, 1024], F32) for i in range(2)]
                        sts = [sb1("sts%d" % i, [128, 1024], F32) for i in range(2)]
                        Bxin, Bxc = Buf(), Buf()
                        Bacc = [Buf(), Buf()]
                        Bxs, BBt, Bsm, Bahl = [Buf(), Buf()], [Buf(), Buf()], [Buf(), Buf()], [Buf(), Buf()]
                        Bxdt, Bxdd, Bxsk, BCBm = [Buf(), Buf()], [Buf(), Buf()], [Buf(), Buf()], [Buf(), Buf()]
                        BU = [Buf() for _ in range(4)]
                        BLt = [Buf() for _ in range(4)]
                        BMt = [Buf() for _ in range(4)]
                        Byas, Bsts = [Buf(), Buf()], [Buf(), Buf()]
                        kctr = dict(lm=0)
                        C1STOP = int(os.environ.get("KDBG_C1STOP", "99"))

                        def conv_tile(t0, n, left, right, lsrc=None, rsrc=None):
                            oth = REG[1 - slot]
                            if not left:
                                if lsrc is None:
                                    kb.op("pool", lambda e: e.memset(xin[:, :, 0:1], 0.0), [], [Bxin])
                                else:
                                    kb.dma("sp", hst[:], lsrc[0], reads=[oth.SB["xbc"]], writes=[Bhst])
                                    CP("dve", xin[:, :, 0:1], hst[:, :, lsrc[1]:lsrc[1] + 1], [Bhst], [Bxin])
                            if not right:
                                if rsrc is None:
                                    kb.op("pool", lambda e: e.memset(xin[:, :, n + 1:n + 2], 0.0), [], [Bxin])
                                else:
                                    kb.dma("sp", hst[:], rsrc[0], reads=[oth.SB["xbc"]], writes=[Bhst])
                                    CP("dve", xin[:, :, n + 1:n + 2], hst[:, :, rsrc[1]:rsrc[1] + 1], [Bhst], [Bxin])
                            a0, a1 = (0 if left else 1), (n + 2 if right else n + 1)
                            kb.dma("sp", xin[:, :, a0:a1], xbv[:, :, t0 - 1 + a0:t0 - 1 + a1], reads=[SB["xbc"]], writes=[Bxin])
                            CV = int(os.environ.get("KDBG_CV", "9"))
                            for fc in range(16 if CV > 1 else 0):
                                a = acc[fc % 2]
                                Ba = Bacc[fc % 2]
                                TS("dve", a[:, 0:n], xin[:, fc, 0:n], cw[:, fc, 0:1], cbs[:, fc:fc + 1], ALU.mult, ALU.add,
                                   [Bxin, Bcw], [Ba])
                                STT(a[:, 0:n], xin[:, fc, 1:n + 1], cw[:, fc, 1:2], a[:, 0:n], ALU.mult, ALU.add, [Bxin], [Ba])
                                STT(a[:, 0:n], xin[:, fc, 2:n + 2], cw[:, fc, 2:3], a[:, 0:n], ALU.mult, ALU.add, [Bxin], [Ba])
                                if CV <= 2:
                                    continue
                                if fc < 12:
                                    ACT(xc[:, fc, 0:n], a[:, 0:n], AF.Silu, [Ba], [Bxc])
                                else:
                                    ACT(CT[:, fc - 12, t0:t0 + n], a[:, 0:n], AF.Silu, [Ba], [BCT])

                        def chunk1(c, off):
                            if os.environ.get("KDBG_CBAR"):
                                kb.barrier()
                            i = c % 2
                            tok = c * 128
                            for fc in range(8):
                                TR(PSb(6)[:, fc * 128:(fc + 1) * 128], xc[:, fc, off:off + 128], identb, [Bxc, B_cst], [PB[6]], fc == 7)
                            CP("dve", xs_tok[i][:], PSb(6), [PB[6]], [Bxs[i]])
                            for g in range(4):
                                TR(PSb(6)[:, g * 128:(g + 1) * 128], xc[:, 8 + g, off:off + 128], identb, [Bxc, B_cst], [PB[6]], g == 3)
                            CP("dve", B_tok[i][:].rearrange("p a b -> p (a b)"), PSb(6)[:, 0:512], [PB[6]], [BBt[i]])
                            if c >= 1 and C1STOP <= 1:
                                return
                            if c >= 1 and C1STOP <= 3:
                                return
                            xv = xs_tok[i][:].rearrange("p (h q) -> p h q", q=64)
                            for d in range(2):
                                TT("dve", xdt[i][:, d, :].rearrange("p (h q) -> p h q", q=64), xv,
                                   dt2[:, d, c, :].unsqueeze(2).broadcast_to([128, 16, 64]), ALU.mult,
                                   [Bxs[i], Bsc], [Bxdt[i]])
                                TT(PBC, xdd[i][:, d, :].rearrange("p (h q) -> p h q", q=64), xv,
                                   w2[:, d, c, :].unsqueeze(2).broadcast_to([128, 16, 64]), ALU.mult,
                                   [Bxs[i], Bsc], [Bxdd[i]])
                            TT(PBC, xsk[i][:].rearrange("p (h q) -> p h q", q=64), xv,
                               dsk[:].unsqueeze(2).broadcast_to([128, 16, 64]), ALU.mult, [Bxs[i], Bdsk], [Bxsk[i]])
                            if c >= 1 and C1STOP <= 4:
                                return
                            for g in range(4):
                                MM(PS[4][:, g * 128:(g + 1) * 128], xc[:, 8 + g, off:off + 128], CT[:, g, tok:tok + 128], True, True,
                                   [Bxc, BCT], [PB[4]], g == 3)
                            P4 = PS[4][:].rearrange("p (a b) -> p a b", a=4)
                            TT("dve", CBm[i][:, 0], P4, triu_b.unsqueeze(1).broadcast_to([128, 4, 128]), ALU.mult, [PB[4], B_cst], [BCBm[i]])
                            TT("dve", CBm[i][:, 1], P4, tril_b.unsqueeze(1).broadcast_to([128, 4, 128]), ALU.mult, [PB[4], B_cst], [BCBm[i]])
                            if c >= 1 and C1STOP <= 5:
                                return
                            for d in range(2):
                                mk = sgt_f if d == 0 else slt_f
                                for x in range(2):
                                    TT(PBC if x == 0 else "dve", U[d * 2 + x][:], mk.unsqueeze(1).broadcast_to([128, 16, 128]),
                                       (ahi, alo)[x][:, d, c, :].unsqueeze(2).broadcast_to([128, 16, 128]), ALU.mult,
                                       [Bsc, B_cst], [BU[d * 2 + x]])
                            if c >= 1 and C1STOP <= 6:
                                return
                            for g in range(4):
                                lm = []
                                for d in range(2):
                                    bk = 2 + d
                                    V = triu_b if d == 0 else tril_b
                                    for hh in range(4):
                                        h = 4 * g + hh
                                        MM(PS[bk][:, hh * 128:(hh + 1) * 128], U[d * 2][:, h, :], V, True, False,
                                           [BU[d * 2], B_cst], [PB[bk]], False)
                                        MM(PS[bk][:, hh * 128:(hh + 1) * 128], U[d * 2 + 1][:, h, :], V, False, True,
                                           [BU[d * 2 + 1], B_cst], [PB[bk]], hh == 3)
                                    k = kctr["lm"] % 4
                                    kctr["lm"] += 1
                                    ACT(Lt[k][:].rearrange("p a b -> p (a b)"), PS[bk][:], AF.Exp, [PB[bk]], [BLt[k]])
                                    TT("dve", Mt[k][:], Lt[k][:], CBm[i][:, d, g, :].unsqueeze(1).broadcast_to([128, 4, 128]), ALU.mult,
                                       [BLt[k], BCBm[i]], [BMt[k]])
                                    lm.append(k)
                                for hh in range(4):
                                    h = 4 * g + hh
                                    yb = 0 if h < 8 else 1
                                    yo = (h % 8) * 64
                                    for d in range(2):
                                        MM(PS[yb][:, yo:yo + 64], Mt[lm[d]][:, hh, :], xdt[i][:, d, h * 64:(h + 1) * 64], d == 0, d == 1,
                                           [BMt[lm[d]], Bxdt[i]], [PB[yb]], (d == 1 and hh == 3))
                            if c >= 1 and C1STOP <= 7:
                                return
                            for yb in range(2):
                                TT("dve", yas[i][:, yb * 512:(yb + 1) * 512], PS[yb][:], xsk[i][:, yb * 512:(yb + 1) * 512], ALU.add,
                                   [PB[yb], Bxsk[i]], [Byas[i]])
                            kb.dma("sp", ya_s[tok:tok + 128, :], yas[i][:], reads=[Byas[i]], writes=[SB["ya"]])
                            if c >= 1 and C1STOP <= 8:
                                return
                            for d in range(2):
                                for gp in range(2):
                                    bk = 4 + gp
                                    for gg in range(2):
                                        g = 2 * gp + gg
                                        MM(PS[bk][:, gg * 256:(gg + 1) * 256], B_tok[i][:, g, :], xdd[i][:, d, g * 256:(g + 1) * 256],
                                           True, True, [BBt[i], Bxdd[i]], [PB[bk]], gg == 1)
                                    CP("dve", sts[d][:, gp * 512:(gp + 1) * 512], PS[bk][:], [PB[bk]], [Bsts[d]])
                                kb.dma("sp", stc_s[c, d], sts[d][:], reads=[Bsts[d]], writes=[SB["stc"]])

                        for sq in range(2):
                            conv_tile(sq * 256, 256, False, False)
                            if SSTOP <= 2:
                                break
                            for cc in range(2):
                                chunk1(sq * 2 + cc, cc * 128)
                                if SSTOP <= 3:
                                    break
                            if SSTOP <= 3:
                                break
                        NCH = int(os.environ.get("KDBG_NCH", "99"))
                        for tt in range(1, 5 if SSTOP > 3 else 0):
                            if (tt - 1) * 4 >= NCH:
                                break
                            oxbv = REG[1 - slot].xbcT_s.rearrange("(fc p) t -> p fc t", p=128)
                            conv_tile(tt * 512, 512, tt > 1, tt < 4,
                                      lsrc=(oxbv[:, :, 2496:2560], 63) if (slot == 1 and tt == 1) else None,
                                      rsrc=(oxbv[:, :, 512:576], 0) if (slot == 0 and tt == 4) else None)
                            for cc in range(4):
                                if (tt - 1) * 4 + cc >= NCH:
                                    break
                                chunk1(tt * 4 + cc, cc * 128)
                    kb.dma("sp", CT_s[:, :], CT[:].rearrange("p g t -> p (g t)"), reads=[BCT], writes=[SB["ct"]])
                    kb.dma("sp", ea_s[:, :], ea2[:].rearrange("p d c h -> p (d c h)"), reads=[Bsc], writes=[SB["ea"]])
                    kb.dma("sp", cd_s[:, :], cd2[:].rearrange("p d c h -> p (d c h)"), reads=[Bsc], writes=[SB["cd"]])
                kb.barrier()

            ns = types.SimpleNamespace(**{k: v for k, v in locals().items() if not k.startswith("_")})
            REG[slot] = ns
            return ns

        make_slot(0)
        make_slot(1)

        def ssm_pass2(l):
            triu_b = cstb[:, 1, :]
            with ExitStack() as st2:
                sb2 = sbuf_alloc(st2)
                gss = sb2("gss2", [128, 1024], F32)
                Bgss = Buf()
                kb.dma("sp", gss[:], g_ssm[l:l + 1, :].partition_broadcast(128), writes=[Bgss])
                ea2 = [sb2("ea2_%d" % s_, [128, 2, 20, 16], F32) for s_ in range(2)]
                cd2 = [sb2("cd2_%d" % s_, [128, 2, 20, 16], F32) for s_ in range(2)]
                Bea, Bcd = Buf(), Buf()
                for s_ in range(2):
                    kb.dma("sp", ea2[s_][:].rearrange("p d c h -> p (d c h)"), REG[s_].ea_s[:, :], reads=[REG[s_].SB["ea"]], writes=[Bea])
                    kb.dma("sp", cd2[s_][:].rearrange("p d c h -> p (d c h)"), REG[s_].cd_s[:, :], reads=[REG[s_].SB["cd"]], writes=[Bcd])
                H = [sb2("H%d" % d, [128, 1024], F32) for d in range(2)]
                BH = [Buf(), Buf()]
                stl = [sb2("stl%d" % i, [128, 1024], F32) for i in range(2)]
                Bstl = [Buf(), Buf()]
                hpb = [sb2("hpb%d" % i, [128, 1024], BF16) for i in range(2)]
                Bhpb = [Buf(), Buf()]
                hpf = [sb2("hpf%d" % i, [128, 1024], BF16) for i in range(2)]
                Bhpf = [Buf(), Buf()]
                ctl = [sb2("ctl%d" % i, [128, 4, 128], BF16) for i in range(2)]
                Bctl = [Buf(), Buf()]
                s0 = sb2("s0", [128, 8, 128], F32)
                Bs0 = Buf()
                fin = sb2("fin", [128, 8, 128], F32)
                Bfin = Buf()
                yal = [sb2("yal%d" % i, [128, 1024], F32) for i in range(2)]
                Byal = [Buf(), Buf()]
                zsl = [sb2("zsl%d" % i, [128, 1024], BF16) for i in range(2)]
                Bzsl = [Buf(), Buf()]
                t2 = [sb2("t2_%d" % i, [128, 1024], F32) for i in range(2)]
                Bt2 = [Buf(), Buf()]
                junk = sb2("junk3", [128, 1024], BF16)
                Bj = Buf()
                ssq = [sb2("ssq%d" % i, [128, 4], F32) for i in range(2)]
                Bssq = [Buf(), Buf()]
                obt = [sb2("obt%d" % i, [128, 1024], BF16) for i in range(2)]
                Bobt = [Buf(), Buf()]
                obT = sb2("obTst", [128, 8, 512], BF16)
                BobT = Buf()
                Bout = Buf()
                cn = dict(st=0, hp=0, f=0)

                def init_state(d, sample):
                    if not sample:
                        kb.op("dve", lambda e: e.memset(H[d][:], 0.0), [], [BH[d]])
                        return
                    kb.dma("sp", s0[:], st0[l, d].rearrange("(a p) n -> p a n", p=128), writes=[Bs0])
                    for half in range(2):
                        bk = 4 + half
                        for a in range(4):
                            TR(PS[bk][:, a * 128:(a + 1) * 128], s0[:, half * 4 + a, :], identf, [Bs0, B_cst], [PB[bk]], a == 3)
                        CP("dve", H[d][:, half * 512:(half + 1) * 512], PS[bk][:], [PB[bk]], [BH[d]])

                def step_state(d, sl_, c):
                    S_ = REG[sl_]
                    j = cn["st"] % 2
                    cn["st"] += 1
                    kb.dma("sp", stl[j][:], S_.stc_s[c, d], reads=[S_.SB["stc"]], writes=[Bstl[j]])
                    Hv = H[d][:].rearrange("p (h q) -> p h q", q=64)
                    TT("dve", Hv, Hv, cd2[sl_][:, d, c, :].unsqueeze(2).broadcast_to([128, 16, 64]), ALU.mult, [Bcd], [BH[d]])
                    TT("dve", H[d][:], H[d][:], stl[j][:], ALU.add, [Bstl[j]], [BH[d]])

                def emit_state(d, dst):
                    for half in range(2):
                        bk = 4 + half
                        for a in range(4):
                            k8 = half * 4 + a
                            TR(PS[bk][:, a * 128:(a + 1) * 128], H[d][:, k8 * 128:(k8 + 1) * 128], identf, [BH[d], B_cst], [PB[bk]], a == 3)
                        CP("dve", fin[:, half * 4:(half + 1) * 4, :], PS[bk][:].rearrange("p (a b) -> p a b", a=4), [PB[bk]], [Bfin])
                    kb.dma("sp", dst.rearrange("(a p) n -> p a n", p=128), fin[:], reads=[Bfin], writes=[Bout])

                def bwd_visit(sl_, c):
                    S_ = REG[sl_]
                    j = cn["hp"] % 2
                    cn["hp"] += 1
                    CP("dve", hpb[j][:], H[1][:], [BH[1]], [Bhpb[j]])
                    kb.dma("sp", S_.hpb_s[c], hpb[j][:], reads=[Bhpb[j]], writes=[S_.SB["hpb"]])
                    step_state(1, sl_, c)

                def fwd_visit(sl_, c):
                    S_ = REG[sl_]
                    i = cn["f"] % 2
                    cn["f"] += 1
                    tok = c * 128
                    CP("dve", hpf[i][:], H[0][:], [BH[0]], [Bhpf[i]])
                    kb.dma("sp", hpb[i][:], S_.hpb_s[c], reads=[S_.SB["hpb"]], writes=[Bhpb[i]])
                    kb.dma("sp", yal[i][:], S_.ya_s[tok:tok + 128, :], reads=[S_.SB["ya"]], writes=[Byal[i]])
                    kb.dma("sp", zsl[i][:], S_.zs_s[tok:tok + 128, :], reads=[S_.SB["z"]], writes=[Bzsl[i]])
                    kb.dma("sp", ctl[i][:], S_.CT_s.rearrange("p (g t) -> p g t", g=4)[:, :, tok:tok + 128], reads=[S_.SB["ct"]],
                           writes=[Bctl[i]])
                    for d in range(2):
                        hp_ = hpf[i] if d == 0 else hpb[i]
                        Bhp = Bhpf[i] if d == 0 else Bhpb[i]
                        for g in range(4):
                            bk = 2 * d + g // 2
                            MM(PS[bk][:, (g % 2) * 256:(g % 2 + 1) * 256], ctl[i][:, g, :], hp_[:, g * 256:(g + 1) * 256],
                               True, True, [Bctl[i], Bhp], [PB[bk]], g % 2 == 1)
                    for d in range(2):
                        for hb in range(2):
                            bk = 2 * d + hb
                            src = PS[bk][:].rearrange("p (h q) -> p h q", q=64)
                            dstv = t2[i][:, hb * 512:(hb + 1) * 512].rearrange("p (h q) -> p h q", q=64)
                            eav = ea2[sl_][:, d, c, hb * 8:hb * 8 + 8].unsqueeze(2).broadcast_to([128, 8, 64])
                            if d == 0:
                                TT("dve", dstv, src, eav, ALU.mult, [PB[bk], Bea], [Bt2[i]])
                            else:
                                TT("dve", src, src, eav, ALU.mult, [Bea], [PB[bk]])
                                TT("dve", t2[i][:, hb * 512:(hb + 1) * 512], t2[i][:, hb * 512:(hb + 1) * 512], PS[bk][:], ALU.add,
                                   [PB[bk]], [Bt2[i]])
                    TT("dve", t2[i][:], t2[i][:], yal[i][:], ALU.add, [Byal[i]], [Bt2[i]])
                    TT("dve", t2[i][:], t2[i][:], zsl[i][:], ALU.mult, [Bzsl[i]], [Bt2[i]])
                    ACT(junk[:], t2[i][:], AF.Square, [Bt2[i]], [Bj, Bssq[i]], accum_out=ssq[i][:, 0:1])
                    TS("dve", ssq[i][:, 1:2], ssq[i][:, 0:1], 1.0 / 1024, EPS, ALU.mult, ALU.add, [], [Bssq[i]])
                    ACT(ssq[i][:, 2:3], ssq[i][:, 1:2], AF.Sqrt, [], [Bssq[i]])
                    kb.op("dve", lambda e: e.reciprocal(out=ssq[i][:, 3:4], in_=ssq[i][:, 2:3]), [], [Bssq[i]])
                    STT(obt[i][:], t2[i][:], ssq[i][:, 3:4], gss[:], ALU.mult, ALU.mult, [Bt2[i], Bssq[i], Bgss], [Bobt[i]])
                    for fc in range(8):
                        TR(PSb(6)[:, fc * 128:(fc + 1) * 128], obt[i][:, fc * 128:(fc + 1) * 128], identb, [Bobt[i], B_cst], [PB[6]], fc == 7)
                    CP("dve", obT[:, :, (c % 4) * 128:(c % 4 + 1) * 128], PSb(6).rearrange("p (a b) -> p a b", a=8), [PB[6]], [BobT])
                    if c % 4 == 3:
                        obTv = S_.oT_s[1].rearrange("(fc p) t -> p fc t", p=128)
                        kb.dma("sp", obTv[:, :, (c - 3) * 128:(c + 1) * 128], obT[:], reads=[BobT], writes=[S_.SB["ob"]])
                    step_state(0, sl_, c)

                for sl_ in range(2):
                    for sq in range(2):
                        init_state(1, False)
                        for c in (sq * 2 + 1, sq * 2):
                            bwd_visit(sl_, c)
                        emit_state(1, REG[sl_].sbo[l, sq])
                init_state(1, True)
                for sl_ in (1, 0):
                    for c in range(19, 3, -1):
                        bwd_visit(sl_, c)
                for sl_ in range(2):
                    for sq in range(2):
                        init_state(0, False)
                        for c in (sq * 2, sq * 2 + 1):
                            fwd_visit(sl_, c)
                        emit_state(0, REG[sl_].sfo[l, sq])
                init_state(0, True)
                for sl_ in (0, 1):
                    for c in range(4, 20):
                        fwd_visit(sl_, c)
            kb.barrier()

        for l in range(n_layers):
            REG[0].phase_mod(l)
            kb.barrier()
            for sl_ in range(2):
                S_ = REG[sl_]
                xsrc = S_.x_in if l == 0 else S_.y1_s
                Bx = Buf() if l == 0 else S_.SB["y1"]
                with ExitStack() as lst:
                    hT = lst.enter_context(nc.sbuf_tensor("hT%d_%d" % (l, sl_), [128, 16, NTOK], BF16))
                    BhTs = [Buf("hT%d" % i) for i in range(20)]
                    S_.phase_norm(l, xsrc, Bx, hT, BhTs)
                    kb.barrier()
                    S_.phase_proj(l, hT, Buf("hTro"))
                kb.barrier()
            for nm in ("phase_attn", "phase_sgu", "phase_ssm"):
                for sl_ in range(2):
                    if nm in skip:
                        continue
                    getattr(REG[sl_], nm)(l)
                    kb.barrier()
            if "pass2" not in skip:
                ssm_pass2(l)
            for sl_ in range(2):
                S_ = REG[sl_]
                xsrc = S_.x_in if l == 0 else S_.y1_s
                Bx = Buf() if l == 0 else S_.SB["y1"]
                ydst = S_.y_out if l == n_layers - 1 else S_.y1_s
                By = S_.SB["yout"] if l == n_layers - 1 else S_.SB["y1"]
                if "phase_o1" not in skip:
                    S_.phase_o1(l)
                    kb.barrier()
                if "phase_o2" not in skip:
                    S_.phase_o2(l, xsrc, Bx, ydst, By)
                    kb.barrier()

        kb.barrier()
    return nc


def _consts():
    s = np.arange(128)[:, None]
    i = np.arange(128)[None, :]
    c = np.zeros((128, 6, 128), np.float32)
    c[:, 0] = (s == i)
    c[:, 1] = (s <= i)
    c[:, 2] = (s >= i)
    c[:, 3] = (s > i)
    c[:, 4] = (s < i)
    c[:, 5] = 1.0
    return c


def _bias_F(rpb_l):
    j = np.arange(64)[None, :]
    col = np.arange(64)[:, None]
    cs = np.clip(j - 8, 0, 48)
    valid = (col >= cs) & (col < cs + 16)
    idx = np.clip(col - j + 15, 0, 30)
    T = np.where(valid[None, None], rpb_l[:, :, idx], np.float32(NEG)).astype(np.float32)
    F = np.empty((2, 2, 64, 16, 6, 2, 64), np.float32)
    for kind in range(2):
        for c in range(6):
            for a in range(2):
                for e in range(2):
                    ri = 2 * c + a - e + (3 if kind == 0 else 1)
                    F[kind, a, :, :, c, e, :] = np.transpose(T[:, ri], (1, 0, 2))
    return F.reshape(2, 128, 16 * 6 * 128)


def _row_masks(half):
    out = np.zeros((5, 2, 64, 6, 2, 64), np.float32)
    slots = [8, 0, 1, 14, 15]
    for si, ml in enumerate(slots):
        kbg = 32 * half + (2 * ml if ml != 15 else 28) - 4
        for c in range(6):
            for a in range(2):
                kr = kbg + 2 * c + a
                for e in range(2):
                    qr = 32 * half + 2 * ml + e
                    rs = min(max(qr - 4, 0), 56)
                    ok = (0 <= kr < 64) and (rs <= kr < rs + 8)
                    out[si, a, :, c, e, :] = 0.0 if ok else NEG
    return out.reshape(5, 128, 6 * 128)


def prep_shared(inp):
    f = lambda a: np.ascontiguousarray(a, dtype=np.float32)
    sh = {}
    for k in ("w_mod", "b_mod", "g_pre", "g_post", "w_in", "g_ssm", "w_br_a", "w_br_b", "w_br_c", "w_out", "d_skip"):
        sh[k] = f(inp[k])
    sh["Fb"] = np.stack([_bias_F(np.asarray(inp["rpb"][l], np.float32)) for l in range(DEPTH)])
    sh["cwT"] = f(np.asarray(inp["conv_w"]).reshape(DEPTH, 16, 128, 3).transpose(0, 2, 1, 3))
    sh["cbT"] = f(np.asarray(inp["conv_b"]).reshape(DEPTH, 16, 128).transpose(0, 2, 1))
    sh["dt_bias"] = f(np.asarray(inp["dt_bias"]).reshape(DEPTH, 32))
    sh["a_log"] = f(np.asarray(inp["a_log"]).reshape(DEPTH, 32))
    sh["wsT"] = f(np.asarray(inp["w_s"]).transpose(0, 3, 1, 2))
    sh["b_s"] = f(np.asarray(inp["b_s"]).reshape(DEPTH, 1024))
    sh["g_sguT"] = f(np.asarray(inp["g_sgu"]).reshape(DEPTH, 8, 128).transpose(0, 2, 1))
    sh["consts"] = _consts()
    return sh


def prep_core(inp, sh, core):
    f = lambda a: np.ascontiguousarray(a, dtype=np.float32)
    b = core
    m = dict(sh)
    xs_ = []
    for slot in range(2):
        xp = np.asarray(inp["x_prompt"])[4 * core + 2 * slot:4 * core + 2 * slot + 2].reshape(512, D)
        xs = np.asarray(inp["x_sample"])[b, slot * NST:(slot + 1) * NST]
        xs_.append(np.concatenate([xp, xs], 0))
    m["x_in"] = f(np.stack(xs_, 0))
    cvec = np.stack([np.asarray(inp["c_ctx"]), np.asarray(inp["c"])[b]], 0)
    m["cvT"] = f(cvec.reshape(2, 16, 128).transpose(2, 1, 0))
    m["ck"] = f(np.asarray(inp["cache_k"])[b].reshape(DEPTH, 256, 1024))
    m["cv"] = f(np.asarray(inp["cache_v"])[b].reshape(DEPTH, 256, 1024))
    m["st0"] = f(np.stack([np.asarray(inp["state_ssm_fwd"])[b].reshape(DEPTH, 1024, 128),
                           np.asarray(inp["state_ssm_bwd"])[b].reshape(DEPTH, 1024, 128)], 1))
    m["rm"] = np.stack([_row_masks(0), _row_masks(1)], 0)
    return m


_NC_CACHE = {}
N_ACTIVE = 4


def kernel(**inputs):
    if "nc" not in _NC_CACHE:
        _NC_CACHE["nc"] = build()
    nc = _NC_CACHE["nc"]
    sh = prep_shared(inputs)
    in_maps = [prep_core(inputs, sh, c) for c in range(N_ACTIVE)]
    res = run_bass_kernel_spmd(nc, in_maps, core_ids=list(range(N_ACTIVE)))
    R = res.results
    y_p = np.empty((16, 256, D), np.float32)
    y_s = np.empty((4, 4096, D), np.float32)
    nk = np.empty((16, DEPTH, 256, 16, 64), np.float32)
    nv = np.empty((16, DEPTH, 256, 16, 64), np.float32)
    nf = np.empty((16, DEPTH, 16, 64, 128), np.float32)
    nb_ = np.empty((16, DEPTH, 16, 64, 128), np.float32)
    for c in range(N_ACTIVE):
        r = R[c]
        for slot in range(2):
            yo = r["y_out"][slot]
            y_s[c, slot * NST:(slot + 1) * NST] = yo[512:]
            for s in range(2):
                q = 4 * c + 2 * slot + s
                y_p[q] = yo[s * 256:(s + 1) * 256]
                for l in range(DEPTH):
                    nk[q, l] = r["ko"][slot, l, s * 256:(s + 1) * 256].reshape(256, 16, 64)
                    nv[q, l] = r["vo"][slot, l, s * 256:(s + 1) * 256].reshape(256, 16, 64)
                    nf[q, l] = r["sfo"][slot, l, s].reshape(16, 64, 128)
                    nb_[q, l] = r["sbo"][slot, l, s].reshape(16, 64, 128)
    return (y_p, y_s, nk, nv, nf, nb_)
```

```python
import os
import types
import numpy as np
from contextlib import ExitStack
import concourse.bass as bass
import concourse.mybir as mybir
from concourse.bass_utils import run_bass_kernel_spmd

F32 = mybir.dt.float32
BF16 = mybir.dt.bfloat16
AF = mybir.ActivationFunctionType
ALU = mybir.AluOpType
AX = mybir.AxisListType

D = 2048
NTOK = 2560
NPT = 512
NST = 2048
NIN = 16416
DEPTH = 2
EPS = 1e-6
NEG = -30000.0
OFF = dict(q=0, k=1024, v=2048, ga=3072, xbc=4096, z=6144, dt=7168, u=7200, vc=8224, gc=9248, gm=10272)
N_CORES = 8


class Buf:
    __slots__ = ("w", "r", "name")

    def __init__(self, name=""):
        self.w = None
        self.r = {}
        self.name = name


class KB:
    def __init__(self, nc, es, nds=28):
        self.nc = nc
        self.E = {"pe": nc.tensor, "act": nc.scalar, "dve": nc.vector, "pool": nc.gpsimd, "sp": nc.sync}
        self.sem = {}
        self.cnt = {}
        for k in ("pe", "act", "dve", "pool"):
            self.sem[k] = es.enter_context(nc.semaphore("s_" + k))
            self.cnt[k] = 0
        for i in range(nds):
            self.sem[("d", i)] = es.enter_context(nc.semaphore("sd%d" % i))
            self.cnt[("d", i)] = 0
        self.nds = nds
        self.dnext = {"sp": 0, "pool": 0, "act": 0}
        self.seen = {e: {} for e in self.E}

    def wait(self, eng, toks):
        seen = self.seen[eng]
        for t in toks:
            k, v = t
            if v <= 0 or seen.get(k, 0) >= v:
                continue
            if k == "pe" and eng == "pe":
                continue
            self.E[eng].wait_ge(self.sem[k], v)
            seen[k] = v

    @staticmethod
    def deps(reads, writes):
        toks = []
        for b in reads:
            if b.w is not None:
                toks.append(b.w)
        for b in writes:
            if b.w is not None:
                toks.append(b.w)
            toks.extend(b.r.items())
        return toks

    @staticmethod
    def mark(tok, reads, writes):
        k, v = tok
        for b in reads:
            if b.r.get(k, 0) < v:
                b.r[k] = v
        for b in writes:
            b.w = tok
            b.r = {}

    def op(self, eng, fn, reads=(), writes=(), inc=True):
        if eng == "pool" and not os.environ.get("KDBG_POOLC"):
            eng = "dve"
        self.wait(eng, self.deps(reads, writes))
        ins = fn(self.E[eng])
        if inc:
            self.cnt[eng] += 1
            ins.then_inc(self.sem[eng], 1)
            tok = (eng, self.cnt[eng])
        else:
            tok = (eng, self.cnt[eng] + 1)
        self.mark(tok, reads, writes)
        return tok

    def dma(self, q, out, in_, reads=(), writes=()):
        lo, n = {"sp": (0, int(os.environ.get("KDBG_NSP", "16"))), "pool": (16, 8), "act": (24, 4)}[q]
        j = self.dnext[q]
        self.dnext[q] = (j + 1) % n
        i = lo + j
        k = ("d", i)
        toks = self.deps(reads, writes)
        toks.append((k, self.cnt[k]))
        self.wait(q, toks)
        self.cnt[k] += 16
        self.E[q].dma_start(out=out, in_=in_).then_inc(self.sem[k], 16)
        tok = (k, self.cnt[k])
        self.mark(tok, reads, writes)
        return tok

    def barrier(self):
        toks = [(k, v) for k, v in self.cnt.items() if v > 0]
        for e in self.E:
            self.wait(e, toks)


def build(n_layers=DEPTH, stop=None, dbg=(), dbg_in=(), start=None, skip=()):
    nc = bass.Bass("TRN2", target_bir_lowering=False)

    def din(name, shape):
        return nc.dram_tensor(name, list(shape), F32, kind="ExternalInput").ap()

    def dout(name, shape):
        return nc.dram_tensor(name, list(shape), F32, kind="ExternalOutput").ap()

    def dscr(name, shape, dt):
        kind = "ExternalOutput" if name in dbg else ("ExternalInput" if name in dbg_in else "Internal")
        return nc.dram_tensor(name, list(shape), dt, kind=kind).ap()

    x_in_all = din("x_in", [2, NTOK, D])
    cvT = din("cvT", [128, 16, 2])
    ck = din("ck", [DEPTH, 256, 1024])
    cv = din("cv", [DEPTH, 256, 1024])
    st0 = din("st0", [DEPTH, 2, 1024, 128])
    w_mod = din("w_mod", [DEPTH, D, 3 * D])
    b_mod = din("b_mod", [DEPTH, 3 * D])
    g_pre = din("g_pre", [DEPTH, D])
    g_post = din("g_post", [DEPTH, D])
    w_in = din("w_in", [DEPTH, D, NIN])
    Fb = din("Fb", [DEPTH, 2, 128, 16 * 6 * 128])
    rm_all = din("rm", [2, 5, 128, 6 * 128])
    cwT = din("cwT", [DEPTH, 128, 16, 3])
    cbT = din("cbT", [DEPTH, 128, 16])
    dt_bias = din("dt_bias", [DEPTH, 32])
    a_log = din("a_log", [DEPTH, 32])
    d_skip = din("d_skip", [DEPTH, 16])
    g_ssm = din("g_ssm", [DEPTH, 1024])
    wsT = din("wsT", [DEPTH, 128, 8, 128])
    b_s = din("b_s", [DEPTH, 1024])
    g_sguT = din("g_sguT", [DEPTH, 128, 8])
    w_br = [din("w_br_a", [DEPTH, 1024, D]), din("w_br_b", [DEPTH, 1024, D]), din("w_br_c", [DEPTH, 1024, D])]
    w_out = din("w_out", [DEPTH, D, D])
    consts = din("consts", [128, 6, 128])
    y_out_all = dout("y_out", [2, NTOK, D])
    ko_all = dout("ko", [2, DEPTH, NPT, 1024])
    vo_all = dout("vo", [2, DEPTH, NPT, 1024])
    sfo_all = dout("sfo", [2, DEPTH, 2, 1024, 128])
    m_s = dscr("m_s", [2, 3 * D], F32)
    sbo_all = dout("sbo", [2, DEPTH, 2, 1024, 128])
    es = ExitStack()
    with es:
        kb = KB(nc, es)
        PS = [es.enter_context(nc.psum_tensor("ps%d" % i, [128, 512], F32)) for i in range(8)]
        PB = [Buf("ps%d" % i) for i in range(8)]

        def PSb(i):
            return PS[i][:].bitcast(BF16)

        cst = es.enter_context(nc.sbuf_tensor("cst", [128, 6, 128], F32))
        cstb = es.enter_context(nc.sbuf_tensor("cstb", [128, 6, 128], BF16))
        B_cst = Buf("cst")
        kb.dma("sp", cst[:], consts[:, :, :], writes=[B_cst])
        kb.op("dve", lambda e: e.tensor_copy(out=cstb[:], in_=cst[:]), reads=[B_cst], writes=[B_cst])
        identb = cstb[:, 0, :]
        identf = cst[:, 0, :]

        uid = [0]

        def sbuf_alloc(stack):
            def f(name, shape, dt):
                uid[0] += 1
                return stack.enter_context(nc.sbuf_tensor("%s_%d" % (name, uid[0]), list(shape), dt))
            return f

        def MM(out, lhsT, rhs, start, stop, R, W, inc):
            return kb.op("pe", lambda e: e.matmul(out, lhsT=lhsT, rhs=rhs, start=start, stop=stop), R, W, inc)

        def TR(out, in_, ident, R, W, inc=True):
            return kb.op("pe", lambda e: e.transpose(out, in_, ident), R, W, inc)

        def ACT(out, in_, func, R, W, bias=None, scale=None, accum_out=None):
            kw = {}
            if bias is not None:
                kw["bias"] = bias
            if scale is not None:
                kw["scale"] = scale
            if accum_out is not None:
                kw["accum_out"] = accum_out
            return kb.op("act", lambda e: e.activation(out=out, in_=in_, func=func, **kw), R, W)

        def TT(eng, out, in0, in1, op, R, W):
            return kb.op(eng, lambda e: e.tensor_tensor(out=out, in0=in0, in1=in1, op=op), R, W)

        def TS(eng, out, in0, s1, s2, op0, op1, R, W):
            if op1 is None:
                return kb.op(eng, lambda e: e.tensor_scalar(out=out, in0=in0, scalar1=s1, scalar2=None, op0=op0), R, W)
            return kb.op(eng, lambda e: e.tensor_scalar(out=out, in0=in0, scalar1=s1, scalar2=s2, op0=op0, op1=op1), R, W)

        def STT(out, in0, scalar, in1, op0, op1, R, W):
            return kb.op("dve", lambda e: e.scalar_tensor_tensor(out=out, in0=in0, scalar=scalar, in1=in1, op0=op0, op1=op1), R, W)

        def CP(eng, out, in_, R, W):
            if eng == "act" and os.environ.get("KDBG_NOACTCP"):
                eng = "dve"
            if eng == "act":
                return kb.op("act", lambda e: e.activation(out=out, in_=in_, func=AF.Identity), R, W)
            return kb.op(eng, lambda e: e.tensor_copy(out=out, in_=in_), R, W)

        SBm = Buf("m")
        REG = {}

        def make_slot(slot):
            x_in = x_in_all[slot]
            y_out = y_out_all[slot]
            ko, vo, sfo, sbo = ko_all[slot], vo_all[slot], sfo_all[slot], sbo_all[slot]
            rm = rm_all[slot]
            y1_s = dscr("s%d_" % slot + "y1_s", [NTOK, D], F32)
            qT_s = dscr("s%d_" % slot + "qT_s", [1024, NTOK], BF16)
            kT_s = dscr("s%d_" % slot + "kT_s", [1024, NTOK], BF16)
            v_s = dscr("s%d_" % slot + "v_s", [NTOK, 1024], BF16)
            ga_s = dscr("s%d_" % slot + "ga_s", [NTOK, 1024], BF16)
            xbcT_s = dscr("s%d_" % slot + "xbcT_s", [2048, NTOK], BF16)
            zs_s = dscr("s%d_" % slot + "zs_s", [NTOK, 1024], BF16)
            dt_s = dscr("s%d_" % slot + "dt_s", [128, 20 * 32], F32)
            uT_s = dscr("s%d_" % slot + "uT_s", [1024, NTOK], BF16)
            vc_s = dscr("s%d_" % slot + "vc_s", [NTOK, 1024], BF16)
            gcT_s = dscr("s%d_" % slot + "gcT_s", [1024, NTOK], BF16)
            gmT_s = dscr("s%d_" % slot + "gmT_s", [3 * D, NTOK], BF16)
            oT_s = [dscr("s%d_" % slot + "oaT_s", [1024, NTOK], BF16), dscr("s%d_" % slot + "obT_s", [1024, NTOK], BF16), dscr("s%d_" % slot + "ocT_s", [1024, NTOK], BF16)]
            mgT_s = dscr("s%d_" % slot + "mgT_s", [D, NTOK], BF16)
            ya_s = dscr("s%d_" % slot + "ya_s", [NTOK, 1024], F32)
            stc_s = dscr("s%d_" % slot + "stc_s", [20, 2, 128, 1024], F32)
            hpb_s = dscr("s%d_" % slot + "hpb_s", [20, 128, 1024], BF16)
            CT_s = dscr("s%d_" % slot + "CT_s", [128, 4 * NTOK], BF16)
            ea_s = dscr("s%d_" % slot + "ea_s", [128, 640], F32)
            cd_s = dscr("s%d_" % slot + "cd_s", [128, 640], F32)
            SB = {n: Buf(n) for n in ("m", "q", "k", "v", "ga", "xbc", "z", "dt", "u", "vc", "gc", "gm", "oa", "ob",
                                      "oc", "mg", "ya", "stc", "hpb", "y1", "ko", "vo", "sfo", "sbo", "yout", "ct", "ea", "cd")}
            SB["m"] = SBm

            def phase_mod(l):
                with ExitStack() as st:
                    sb = sbuf_alloc(st)
                    cvt = sb("cvt", [128, 16, 2], F32)
                    scT = sb("scT", [128, 16, 2], BF16)
                    bm = sb("bm", [2, 3 * D], F32)
                    mrow = sb("mrow", [2, 3 * D], F32)
                    wb = [sb("wm%d" % i, [128, 16, 512], BF16) for i in range(3)]
                    Bw = [Buf() for _ in range(3)]
                    Bc, Bs, Bb, Bm = Buf(), Buf(), Buf(), Buf()
                    kb.dma("sp", cvt[:], cvT[:, :, :], writes=[Bc])
                    ACT(scT[:], cvt[:], AF.Silu, [Bc], [Bs])
                    kb.dma("sp", bm[:], b_mod[l:l + 1, :].partition_broadcast(2), writes=[Bb])
                    wv = w_mod[l].rearrange("(kc p) f -> p kc f", p=128)
                    for cb in range(12):
                        w = wb[cb % 3]
                        kb.dma("pool", w[:], wv[:, :, cb * 512:(cb + 1) * 512], writes=[Bw[cb % 3]])
                        for kc in range(16):
                            MM(PS[0][0:2, :], scT[:, kc, :], w[:, kc, :], kc == 0, kc == 15, [Bs, Bw[cb % 3]], [PB[0]], kc == 15)
                        TT("dve", mrow[:, cb * 512:(cb + 1) * 512], PS[0][0:2, :], bm[:, cb * 512:(cb + 1) * 512], ALU.add,
                           [PB[0], Bb], [Bm])
                    kb.dma("sp", m_s[:, :], mrow[:], reads=[Bm], writes=[SB["m"]])

            def phase_norm(l, xsrc, Bx, hT, BhT):
                with ExitStack() as st:
                    sb = sbuf_alloc(st)
                    gp = sb("gp", [128, D], F32)
                    A = [sb("A%d" % g, [128, D], F32) for g in range(2)]
                    Bs_ = [sb("Bs%d" % g, [128, D], F32) for g in range(2)]
                    xt = [sb("xt%d" % i, [128, D], F32) for i in range(2)]
                    hx = [sb("hx%d" % i, [128, D], BF16) for i in range(2)]
                    junk = sb("junk", [128, D], BF16)
                    ss = [sb("ss%d" % i, [128, 4], F32) for i in range(2)]
                    Bgp, BA, BBs = Buf(), [Buf(), Buf()], [Buf(), Buf()]
                    Bxt, Bhx, Bj, Bss = [Buf(), Buf()], [Buf(), Buf()], Buf(), [Buf(), Buf()]
                    kb.dma("sp", gp[:], g_pre[l:l + 1, :].partition_broadcast(128), writes=[Bgp])
                    for g in range(2):
                        kb.dma("sp", A[g][:], m_s[g:g + 1, D:2 * D].partition_broadcast(128), reads=[SB["m"]], writes=[BA[g]])
                        kb.dma("sp", Bs_[g][:], m_s[g:g + 1, 0:D].partition_broadcast(128), reads=[SB["m"]], writes=[BBs[g]])
                        STT(A[g][:], A[g][:], 1.0, gp[:], ALU.add, ALU.mult, [Bgp], [BA[g]])
                    for ts in range(20):
                        g = 0 if ts < 4 else 1
                        i = ts % 2
                        kb.dma("sp", xt[i][:], xsrc[ts * 128:(ts + 1) * 128, :], reads=[Bx], writes=[Bxt[i]])
                        ACT(junk[:], xt[i][:], AF.Square, [Bxt[i]], [Bj, Bss[i]], accum_out=ss[i][:, 0:1])
                        TS("dve", ss[i][:, 1:2], ss[i][:, 0:1], 1.0 / D, EPS, ALU.mult, ALU.add, [], [Bss[i]])
                        ACT(ss[i][:, 2:3], ss[i][:, 1:2], AF.Sqrt, [], [Bss[i]])
                        kb.op("dve", lambda e: e.reciprocal(out=ss[i][:, 3:4], in_=ss[i][:, 2:3]), [], [Bss[i]])
                        STT(xt[i][:], xt[i][:], ss[i][:, 3:4], A[g][:], ALU.mult, ALU.mult, [Bss[i], BA[g]], [Bxt[i]])
                        TT("dve", hx[i][:], xt[i][:], Bs_[g][:], ALU.add, [Bxt[i], BBs[g]], [Bhx[i]])
                        for half in range(2):
                            bk = 2 * i + half
                            for j in range(8):
                                kc = half * 8 + j
                                TR(PSb(bk)[:, j * 128:(j + 1) * 128], hx[i][:, kc * 128:(kc + 1) * 128], identb,
                                   [Bhx[i], B_cst], [PB[bk]], j == 7)
                            CP("act" if half == 0 else "dve",
                               hT[:, half * 8:(half + 1) * 8, ts * 128:(ts + 1) * 128],
                               PSb(bk).rearrange("p (a b) -> p a b", a=8), [PB[bk]], [BhT[ts]])

            def phase_proj(l, hT, BhT):
                with ExitStack() as st:
                    sb = sbuf_alloc(st)
                    wb = [sb("wp%d" % i, [128, 16, 512], BF16) for i in range(3)]
                    Bw = [Buf() for _ in range(3)]
                    wdt = sb("wdt", [128, 16, 32], BF16)
                    Bwdt = Buf()
                    sfm = [sb("sfm%d" % i, [128, NTOK], BF16) for i in range(3)]
                    Bsfm = [Buf() for _ in range(3)]
                    stm = [sb("stm%d" % i, [128, 512], BF16) for i in range(4)]
                    Bstm = [Buf() for _ in range(4)]
                    s32 = [sb("s32%d" % i, [128, 512], F32) for i in range(2)]
                    Bs32 = [Buf() for _ in range(2)]
                    dts = sb("dts", [128, 20, 32], F32)
                    Bdts = Buf()
                    wv = w_in[l].rearrange("(kc p) f -> p kc f", p=128)
                    blocks = []

                    def addF(c0, n, scr, key, ev):
                        for b in range(n // 512):
                            blocks.append(("F", c0 + b * 512, scr, key, b * 512, ev))

                    def addT(c0, n, scr, key, ev):
                        for b in range(n // 512):
                            blocks.append(("T", c0 + b * 512, scr, key, b * 512, ev))
                    addF(OFF["q"], 1024, qT_s, "q", "qs")
                    addF(OFF["k"], 1024, kT_s, "k", "cp")
                    addT(OFF["v"], 1024, v_s, "v", "cp")
                    addT(OFF["ga"], 1024, ga_s, "ga", "silu")
                    addF(OFF["xbc"], 2048, xbcT_s, "xbc", "cp")
                    addT(OFF["z"], 1024, zs_s, "z", "silu")
                    addF(OFF["u"], 1024, uT_s, "u", "cp")
                    addT(OFF["vc"], 1024, vc_s, "vc", "cp")
                    addF(OFF["gc"], 1024, gcT_s, "gc", "silu")
                    addF(OFF["gm"], 6144, gmT_s, "gm", "sig")
                    nb = len(blocks)
                    if os.environ.get("KDBG_NB"):
                        blocks = blocks[:int(os.environ["KDBG_NB"])]
                        nb = len(blocks)
                    state = dict(bank=0, fm=0, tm=0, s32=0, ev=0)

                    def load(bi):
                        c0 = blocks[bi][1]
                        kb.dma("pool", wb[bi % 3][:], wv[:, :, c0:c0 + 512], writes=[Bw[bi % 3]])

                    def nbank():
                        b = state["bank"]
                        state["bank"] = (b + 1) % int(os.environ.get("KDBG_NBANK", "7"))
                        return b

                    def evac(ev, out, bank, W):
                        if ev == "silu":
                            ACT(out, PS[bank][:], AF.Silu, [PB[bank]], W)
                        elif ev == "sig":
                            ACT(out, PS[bank][:], AF.Sigmoid, [PB[bank]], W)
                        elif ev == "qs":
                            TS("dve", out, PS[bank][:], 0.125, None, ALU.mult, None, [PB[bank]], W)
                        else:
                            CP("dve", out, PS[bank][:], [PB[bank]], W)

                    if not os.environ.get("KDBG_NODT"):
                        kb.dma("pool", wdt[:], wv[:, :, OFF["dt"]:OFF["dt"] + 32], writes=[Bwdt])
                    load(0)
                    load(1)
                    for bi in range(nb):
                        lay, c0, scr, key, off, ev = blocks[bi]
                        if bi + 2 < nb:
                            load(bi + 2)
                        w = wb[bi % 3]
                        BW = Bw[bi % 3]
                        if lay == "F":
                            for fcl in range(4):
                                si = state["fm"]
                                state["fm"] = (si + 1) % 3
                                for tt in range(5):
                                    bk = nbank()
                                    for kc in range(16):
                                        MM(PS[bk][:], w[:, kc, fcl * 128:(fcl + 1) * 128], hT[:, kc, tt * 512:(tt + 1) * 512],
                                           kc == 0, kc == 15, [BW, BhT], [PB[bk]], kc == 15)
                                    evac(ev, sfm[si][:, tt * 512:(tt + 1) * 512], bk, [Bsfm[si]])
                                r0 = off + fcl * 128
                                kb.dma("sp", scr[r0:r0 + 128, :], sfm[si][:], reads=[Bsfm[si]], writes=[SB[key]])
                            if key == "k":
                                for ts in range(4):
                                    bk = nbank()
                                    for kc in range(16):
                                        MM(PS[bk][:], hT[:, kc, ts * 128:(ts + 1) * 128], w[:, kc, :], kc == 0, kc == 15,
                                           [BW, BhT], [PB[bk]], kc == 15)
                                    j = state["s32"]
                                    state["s32"] = j ^ 1
                                    CP("dve", s32[j][:], PS[bk][:], [PB[bk]], [Bs32[j]])
                                    kb.dma("sp", ko[l, ts * 128:(ts + 1) * 128, off:off + 512], s32[j][:], reads=[Bs32[j]],
                                           writes=[SB["ko"]])
                        else:
                            for ts in range(20):
                                bk = nbank()
                                for kc in range(16):
                                    MM(PS[bk][:], hT[:, kc, ts * 128:(ts + 1) * 128], w[:, kc, :], kc == 0, kc == 15,
                                       [BW, BhT], [PB[bk]], kc == 15)
                                si = state["tm"]
                                state["tm"] = (si + 1) % 4
                                if key == "v" and ts < 4:
                                    j = state["s32"]
                                    state["s32"] = j ^ 1
                                    CP("dve", s32[j][:], PS[bk][:], [PB[bk]], [Bs32[j]])
                                    kb.dma("sp", vo[l, ts * 128:(ts + 1) * 128, off:off + 512], s32[j][:], reads=[Bs32[j]],
                                           writes=[SB["vo"]])
                                evac(ev, stm[si][:], bk, [Bstm[si]])
                                kb.dma("sp", scr[ts * 128:(ts + 1) * 128, off:off + 512], stm[si][:], reads=[Bstm[si]],
                                       writes=[SB[key]])
                        if key == "z" and off == 512 and not os.environ.get("KDBG_NODT"):
                            for ts in range(20):
                                bk = nbank()
                                for kc in range(16):
                                    MM(PS[bk][:, 0:32], hT[:, kc, ts * 128:(ts + 1) * 128], wdt[:, kc, :], kc == 0, kc == 15,
                                       [Bwdt, BhT], [PB[bk]], kc == 15)
                                CP("dve", dts[:, ts, :], PS[bk][:, 0:32], [PB[bk]], [Bdts])
                            kb.dma("sp", dt_s[:, :], dts[:].rearrange("p t c -> p (t c)"), reads=[Bdts], writes=[SB["dt"]])

            def phase_attn(l):
                kTv = kT_s.rearrange("(fc p) t -> p fc t", p=128)
                qTv = qT_s.rearrange("(fc p) t -> p fc t", p=128)
                oaTv = oT_s[0].rearrange("(fc p) t -> p fc t", p=128)
                with ExitStack() as st:
                    sb = sbuf_alloc(st)
                    qt = [sb("qt%d" % i, [128, 8, 128], BF16) for i in range(2)]
                    sga = [sb("sga%d" % i, [128, 1024], BF16) for i in range(2)]
                    PT = [sb("PT%d" % i, [128, 1024], BF16) for i in range(3)]
                    oa = [sb("oa%d" % i, [128, 1024], BF16) for i in range(2)]
                    rden = [sb("rden%d" % i, [128, 4], F32) for i in range(2)]
                    oaT = sb("oaT", [128, 8, 512], BF16)
                    Bqt, Bsga = [Buf(), Buf()], [Buf(), Buf()]
                    BPT, Boa, Brd, BoaT = [Buf() for _ in range(3)], [Buf(), Buf()], [Buf(), Buf()], Buf()
                    cnt = dict(q=0, pt=0, s=0, o=0, oa=0)

                    def block(tok0, chunks_fn, nch, Rk, oslot, flush):
                        qi = cnt["q"] % 2
                        cnt["q"] += 1
                        kb.dma("sp", qt[qi][:], qTv[:, :, tok0:tok0 + 128], reads=[SB["q"]], writes=[Bqt[qi]])
                        kb.dma("sp", sga[qi][:], ga_s[tok0:tok0 + 128, :], reads=[SB["ga"]], writes=[Bsga[qi]])
                        oi = cnt["oa"] % 2
                        cnt["oa"] += 1
                        pend = {}

                        def qk(h):
                            hp, fc = (h % 2) * 64, h // 2
                            sr = cnt["s"] % 2
                            cnt["s"] += 1
                            pend[h] = sr
                            for ci in range(nch):
                                bank = 2 * sr + ci // 4
                                o_ = (ci % 4) * 128
                                kap, vap, bap = chunks_fn(h, ci)
                                lastb = (ci % 4 == 3) or (ci == nch - 1)
                                MM(PS[bank][:, o_:o_ + 128], kap, qt[qi][hp:hp + 64, fc, :], True, bap is None,
                                   Rk + [Bqt[qi]], [PB[bank]], lastb and bap is None)
                                if bap is not None:
                                    MM(PS[bank][:, o_:o_ + 128], identb, bap, False, True, Rk + [B_cst], [PB[bank]], lastb)

                        def pv(h):
                            sr = pend.pop(h)
                            pi = cnt["pt"] % 3
                            cnt["pt"] += 1
                            for b in range((nch + 3) // 4):
                                ncol = min(nch - 4 * b, 4) * 128
                                ACT(PT[pi][:, b * 512:b * 512 + ncol], PS[2 * sr + b][:, 0:ncol], AF.Exp, [PB[2 * sr + b]], [BPT[pi]])
                            hh = h % 4
                            ob = 4 + ((cnt["o"] // 4) % 2)
                            cnt["o"] += 1
                            Ov = PS[ob][:, 0:260].rearrange("p (a b) -> p a b", a=4)
                            for ci in range(nch):
                                kap, vap, bap = chunks_fn(h, ci)
                                MM(Ov[:, hh, :], PT[pi][:, ci * 128:(ci + 1) * 128], vap, ci == 0, ci == nch - 1,
                                   Rk + [BPT[pi]], [PB[ob]], ci == nch - 1)
                            if hh == 3:
                                ri = (h // 4) % 2
                                kb.op("dve", lambda e: e.reciprocal(out=rden[ri][:], in_=Ov[:, :, 64]), [PB[ob]], [Brd[ri]])
                                for k4 in range(4):
                                    h2 = h - 3 + k4
                                    STT(oa[oi][:, h2 * 64:(h2 + 1) * 64], Ov[:, k4, 0:64], rden[ri][:, k4:k4 + 1],
                                        sga[qi][:, h2 * 64:(h2 + 1) * 64], ALU.mult, ALU.mult,
                                        [PB[ob], Brd[ri], Bsga[qi]], [Boa[oi]])
                        qk(0)
                        for h in range(16):
                            if h + 1 < 16:
                                qk(h + 1)
                            pv(h)
                        for fc in range(8):
                            TR(PSb(6)[:, fc * 128:(fc + 1) * 128], oa[oi][:, fc * 128:(fc + 1) * 128], identb,
                               [Boa[oi], B_cst], [PB[6]], fc == 7)
                        CP("dve", oaT[:, :, oslot * 128:(oslot + 1) * 128], PSb(6).rearrange("p (a b) -> p a b", a=8),
                           [PB[6]], [BoaT])
                        if flush is not None:
                            kb.dma("sp", oaTv[:, :, flush:flush + 512], oaT[:], reads=[BoaT], writes=[SB["oa"]])

                    with ExitStack() as st2:
                        sb2 = sbuf_alloc(st2)
                        kTp = sb2("kTp", [128, 8, 512], BF16)
                        vp = sb2("vp", [128, 4, 16, 65], BF16)
                        Bkp, Bvp = Buf(), Buf()
                        kb.dma("sp", kTp[:], kTv[:, :, 0:512], reads=[SB["k"]], writes=[Bkp])
                        kb.op("pool", lambda e: e.memset(vp[:, :, :, 64:65], 1.0), [], [Bvp])
                        for c in range(4):
                            kb.dma("sp", vp[:, c, :, 0:64], v_s[c * 128:(c + 1) * 128, :].rearrange("p (h d) -> p h d", d=64),
                                   reads=[SB["v"]], writes=[Bvp])
                        for sq in range(2):
                            for t in range(2):
                                def cf(h, ci, sq=sq):
                                    hp, fc = (h % 2) * 64, h // 2
                                    return (kTp[hp:hp + 64, fc, sq * 256 + ci * 128: sq * 256 + (ci + 1) * 128],
                                            vp[:, sq * 2 + ci, h, :], None)
                                pslot = sq * 2 + t
                                block(sq * 256 + t * 128, cf, 2, [Bkp, Bvp], pslot, 0 if pslot == 3 else None)
                        kb.barrier()
                    with ExitStack() as st2:
                        sb2 = sbuf_alloc(st2)
                        kTs = sb2("kTs", [128, 8, 2560], BF16)
                        vs = sb2("vs", [128, 20, 16, 65], BF16)
                        kcT = sb2("kcT", [128, 8, 256], BF16)
                        vcx = sb2("vcx", [128, 2, 16, 65], BF16)
                        bint = sb2("bint", [128, 16, 768], BF16)
                        bsp = sb2("bsp", [128, 16, 768], BF16)
                        rmt = [sb2("rmt%d" % i, [128, 768], BF16) for i in range(2)]
                        Bks, Bvs, Bkc, Bvc, Bbi, Bbs, Brm = Buf(), Buf(), Buf(), Buf(), Buf(), Buf(), [Buf(), Buf()]
                        oth = REG[1 - slot]
                        okTv = oth.kT_s.rearrange("(fc p) t -> p fc t", p=128)
                        kb.op("pool", lambda e: e.memset(vs[:, 0:2, :, :], 0.0), [], [Bvs])
                        kb.op("pool", lambda e: e.memset(vs[:, 18:20, :, :], 0.0), [], [Bvs])
                        kb.op("pool", lambda e: e.memset(vs[:, :, :, 64:65], 1.0), [], [Bvs])
                        if slot == 1:
                            kb.dma("sp", kTs[:, :, 0:256], okTv[:, :, 2304:2560], reads=[oth.SB["k"]], writes=[Bks])
                            for c in range(2):
                                kb.dma("sp", vs[:, c, :, 0:64],
                                       oth.v_s[2304 + c * 128:2304 + (c + 1) * 128, :].rearrange("p (h d) -> p h d", d=64),
                                       reads=[oth.SB["v"]], writes=[Bvs])
                            kb.op("pool", lambda e: e.memset(kTs[:, :, 2304:2560], 0.0), [], [Bks])
                        else:
                            kb.op("pool", lambda e: e.memset(kTs[:, :, 0:256], 0.0), [], [Bks])
                            kb.dma("sp", kTs[:, :, 2304:2560], okTv[:, :, 512:768], reads=[oth.SB["k"]], writes=[Bks])
                            for c in range(2):
                                kb.dma("sp", vs[:, 18 + c, :, 0:64],
                                       oth.v_s[512 + c * 128:512 + (c + 1) * 128, :].rearrange("p (h d) -> p h d", d=64),
                                       reads=[oth.SB["v"]], writes=[Bvs])
                        kb.dma("sp", kTs[:, :, 256:2304], kTv[:, :, 512:2560], reads=[SB["k"]], writes=[Bks])
                        for c in range(16):
                            kb.dma("sp", vs[:, 2 + c, :, 0:64],
                                   v_s[512 + c * 128:512 + (c + 1) * 128, :].rearrange("p (h d) -> p h d", d=64),
                                   reads=[SB["v"]], writes=[Bvs])
                        kb.op("pool", lambda e: e.memset(vcx[:, :, :, 64:65], 1.0), [], [Bvc])
                        with ExitStack() as st3:
                            ckb = st3.enter_context(nc.sbuf_tensor("ckb%d_%d" % (l, slot), [128, 2, 1024], BF16))
                            Bck = Buf()
                            kb.dma("pool", ckb[:], ck[l].rearrange("(c p) f -> p c f", p=128), writes=[Bck])
                            for c in range(2):
                                kb.dma("pool", vcx[:, c, :, 0:64], cv[l, c * 128:(c + 1) * 128, :].rearrange("p (h d) -> p h d", d=64),
                                       writes=[Bvc])
                                for fc in range(8):
                                    TR(PSb(6)[:, fc * 128:(fc + 1) * 128], ckb[:, c, fc * 128:(fc + 1) * 128], identb,
                                       [Bck, B_cst], [PB[6]], fc == 7)
                                CP("dve", kcT[:, :, c * 128:(c + 1) * 128], PSb(6).rearrange("p (a b) -> p a b", a=8),
                                   [PB[6]], [Bkc])
                            kb.barrier()

                        def load_bias(dst, Bdst, kind, ridx, j):
                            kb.dma("pool", dst[:].rearrange("p a b -> p (a b)"), Fb[l, kind], writes=[Bdst])
                            kb.dma("pool", rmt[j][:], rm[ridx], writes=[Brm[j]])
                            TT("dve", dst[:], dst[:], rmt[j][:].unsqueeze(1).broadcast_to([128, 16, 768]), ALU.add,
                               [Brm[j]], [Bdst])
                        load_bias(bint, Bbi, 0, 0, 0)
                        rmj = 1
                        for ml in range(16):
                            special = ml in (0, 1, 14, 15)
                            if special:
                                load_bias(bsp, Bbs, 1 if ml == 15 else 0, {0: 1, 1: 2, 14: 3, 15: 4}[ml], rmj)
                                rmj ^= 1
                            bt, Bbt = (bsp, Bbs) if special else (bint, Bbi)
                            cb0 = 14 if ml == 15 else ml
                            nloc = 6 if ml in (0, 15) else 5

                            def cf(h, ci, cb0=cb0, nloc=nloc, bt=bt):
                                hp, fc = (h % 2) * 64, h // 2
                                if ci < nloc:
                                    c = cb0 + ci
                                    return (kTs[hp:hp + 64, fc, c * 128:(c + 1) * 128], vs[:, c, h, :],
                                            bt[:, h, ci * 128:(ci + 1) * 128])
                                c = ci - nloc
                                return (kcT[hp:hp + 64, fc, c * 128:(c + 1) * 128], vcx[:, c, h, :], None)
                            block(512 + ml * 128, cf, nloc + 2, [Bks, Bvs, Bkc, Bvc, Bbt], ml % 4,
                                  512 + (ml - 3) * 128 if ml % 4 == 3 else None)
                        kb.barrier()

            def phase_sgu(l):
                uTv = uT_s.rearrange("(g e) t -> e g t", e=128)
                gcTv = gcT_s.rearrange("(g e) t -> e g t", e=128)
                ocTv = oT_s[2].rearrange("(g e) t -> e g t", e=128)
                with ExitStack() as st:
                    sb = sbuf_alloc(st)
                    wst = sb("wst", [128, 8, 128], BF16)
                    bsb = sb("bsb", [128, 8, 128], F32)
                    gsg = sb("gsg", [128, 8], F32)
                    Bw, Bb, Bg = Buf(), Buf(), Buf()
                    kb.dma("pool", wst[:], wsT[l], writes=[Bw])
                    kb.dma("sp", bsb[:].rearrange("p a b -> p (a b)"), b_s[l:l + 1, :].partition_broadcast(128), writes=[Bb])
                    kb.dma("sp", gsg[:], g_sguT[l], writes=[Bg])
                    vct = [sb("vct%d" % i, [128, 1024], BF16) for i in range(2)]
                    vn = [sb("vn%d" % i, [128, 1024], BF16) for i in range(2)]
                    stt = [sb("stt%d" % i, [128, 16], F32) for i in range(2)]
                    ut = [sb("ut%d" % i, [128, 8, 512], BF16) for i in range(2)]
                    gt = [sb("gt%d" % i, [128, 8, 512], BF16) for i in range(2)]
                    oc = [sb("oc%d" % i, [128, 8, 512], BF16) for i in range(2)]
                    tmp = [sb("tmpg%d" % i, [128, 8, 128], F32) for i in range(2)]
                    Bvct, Bvn, Bstt = [Buf(), Buf()], [Buf(), Buf()], [Buf(), Buf()]
                    But, Bgt, Boc, Btmp = [Buf(), Buf()], [Buf(), Buf()], [Buf(), Buf()], [Buf(), Buf()]
                    for tt in range(5):
                        ti = tt % 2
                        kb.dma("sp", ut[ti][:], uTv[:, :, tt * 512:(tt + 1) * 512], reads=[SB["u"]], writes=[But[ti]])
                        kb.dma("sp", gt[ti][:], gcTv[:, :, tt * 512:(tt + 1) * 512], reads=[SB["gc"]], writes=[Bgt[ti]])
                        TT("pool", ut[ti][:], ut[ti][:], gt[ti][:], ALU.mult, [Bgt[ti]], [But[ti]])
                        for sc in range(4):
                            c = tt * 4 + sc
                            i = c % 2
                            kb.dma("sp", vct[i][:], vc_s[c * 128:(c + 1) * 128, :], reads=[SB["vc"]], writes=[Bvct[i]])
                            for hf in range(2):
                                kb.op("dve", lambda e: e.bn_stats(out=stt[i][:, hf * 6:(hf + 1) * 6], in_=vct[i][:, hf * 512:(hf + 1) * 512]),
                                      [Bvct[i]], [Bstt[i]])
                            kb.op("dve", lambda e: e.bn_aggr(out=stt[i][:, 12:14], in_=stt[i][:, 0:12]), [], [Bstt[i]])
                            TS("dve", stt[i][:, 14:15], stt[i][:, 13:14], EPS, None, ALU.add, None, [], [Bstt[i]])
                            ACT(stt[i][:, 14:15], stt[i][:, 14:15], AF.Sqrt, [], [Bstt[i]])
                            kb.op("dve", lambda e: e.reciprocal(out=stt[i][:, 15:16], in_=stt[i][:, 14:15]), [], [Bstt[i]])
                            TS("dve", vn[i][:], vct[i][:], stt[i][:, 12:13], stt[i][:, 15:16], ALU.subtract, ALU.mult,
                               [Bvct[i], Bstt[i]], [Bvn[i]])
                            bks = (0, 1) if c % 2 == 0 else (2, 3)
                            for g in range(8):
                                bk = bks[g // 4]
                                MM(PS[bk][:, (g % 4) * 128:(g % 4 + 1) * 128], vn[i][:, g * 128:(g + 1) * 128], wst[:, g, :],
                                   True, True, [Bvn[i], Bw], [PB[bk]], g % 4 == 3)
                            for g in range(8):
                                bk = bks[g // 4]
                                STT(tmp[i][:, g, :], PS[bk][:, (g % 4) * 128:(g % 4 + 1) * 128], gsg[:, g:g + 1], bsb[:, g, :],
                                    ALU.mult, ALU.add, [PB[bk], Bg, Bb], [Btmp[i]])
                            TT("dve", oc[ti][:, :, sc * 128:(sc + 1) * 128], tmp[i][:], ut[ti][:, :, sc * 128:(sc + 1) * 128], ALU.mult,
                               [Btmp[i], But[ti]], [Boc[ti]])
                        kb.dma("sp", ocTv[:, :, tt * 512:(tt + 1) * 512], oc[ti][:], reads=[Boc[ti]], writes=[SB["oc"]])

            def phase_o1(l):
                gmv = gmT_s.rearrange("(br fc p) t -> p br fc t", p=128, br=3)
                mgv = mgT_s.rearrange("(fc p) t -> p fc t", p=128)
                with ExitStack() as st:
                    sb = sbuf_alloc(st)
                    wbr = [sb("wbr%d" % b, [128, 8, D], BF16) for b in range(3)]
                    Bwbr = [Buf() for _ in range(3)]
                    for b in range(3):
                        kb.dma("pool", wbr[b][:], w_br[b][l].rearrange("(kc p) f -> p kc f", p=128), writes=[Bwbr[b]])
                    ot = [[sb("ot%d_%d" % (b, i), [128, 8, 512], BF16) for b in range(3)] for i in range(2)]
                    Bot = [[Buf() for _ in range(3)] for _ in range(2)]
                    gmt = [sb("gmt%d" % i, [128, 3, 512], BF16) for i in range(3)]
                    Bgm = [Buf() for _ in range(3)]
                    acc = [sb("acc%d" % i, [128, 512], F32) for i in range(2)]
                    Bacc = [Buf(), Buf()]
                    mg = [sb("mg%d" % i, [128, 16, 512], BF16) for i in range(2)]
                    Bmg = [Buf(), Buf()]
                    nbk = 0
                    for tt in range(5):
                        ti = tt % 2
                        for b in range(3):
                            kb.dma("sp", ot[ti][b][:], oT_s[b].rearrange("(kc p) t -> p kc t", p=128)[:, :, tt * 512:(tt + 1) * 512],
                                   reads=[SB[("oa", "ob", "oc")[b]]], writes=[Bot[ti][b]])
                        for fc in range(16):
                            gi = (tt * 16 + fc) % 3
                            ai = fc % 2
                            kb.dma("sp", gmt[gi][:], gmv[:, :, fc, tt * 512:(tt + 1) * 512], reads=[SB["gm"]], writes=[Bgm[gi]])
                            for b in range(3):
                                bk = nbk
                                nbk = (nbk + 1) % 7
                                for kc in range(8):
                                    MM(PS[bk][:], wbr[b][:, kc, fc * 128:(fc + 1) * 128], ot[ti][b][:, kc, :], kc == 0, kc == 7,
                                       [Bwbr[b], Bot[ti][b]], [PB[bk]], kc == 7)
                                if b == 0:
                                    TT("dve", acc[ai][:], PS[bk][:], gmt[gi][:, 0, :], ALU.mult, [PB[bk], Bgm[gi]], [Bacc[ai]])
                                elif b == 1:
                                    t2 = TT("dve", PS[bk][:], PS[bk][:], gmt[gi][:, 1, :], ALU.mult, [Bgm[gi]], [PB[bk]])
                                    TT("dve", acc[ai][:], acc[ai][:], PS[bk][:], ALU.add, [PB[bk]], [Bacc[ai]])
                                else:
                                    TT("dve", PS[bk][:], PS[bk][:], gmt[gi][:, 2, :], ALU.mult, [Bgm[gi]], [PB[bk]])
                                    TT("dve", mg[ti][:, fc, :], acc[ai][:], PS[bk][:], ALU.add, [PB[bk], Bacc[ai]], [Bmg[ti]])
                        kb.dma("sp", mgv[:, :, tt * 512:(tt + 1) * 512], mg[ti][:], reads=[Bmg[ti]], writes=[SB["mg"]])

            def phase_o2(l, xsrc, Bx, ydst, By):
                mgv = mgT_s.rearrange("(kc p) t -> p kc t", p=128)
                with ExitStack() as st:
                    sb = sbuf_alloc(st)
                    wo = sb("wo", [128, 16, D], BF16)
                    Bwo = Buf()
                    kb.dma("pool", wo[:], w_out[l].rearrange("(kc p) f -> p kc f", p=128), writes=[Bwo])
                    gpo = sb("gpo", [128, D], F32)
                    G = [sb("G%d" % g, [128, D], F32) for g in range(2)]
                    Bgpo, BG = Buf(), [Buf(), Buf()]
                    kb.dma("sp", gpo[:], g_post[l:l + 1, :].partition_broadcast(128), writes=[Bgpo])
                    for g in range(2):
                        kb.dma("sp", G[g][:], m_s[g:g + 1, 2 * D:3 * D].partition_broadcast(128), reads=[SB["m"]], writes=[BG[g]])
                        TT("dve", G[g][:], G[g][:], gpo[:], ALU.mult, [Bgpo], [BG[g]])
                    mgt = [sb("mgt%d" % i, [128, 16, 128], BF16) for i in range(2)]
                    xt = [sb("xo%d" % i, [128, D], F32) for i in range(2)]
                    zt = [sb("zt%d" % i, [128, D], F32) for i in range(2)]
                    junk = sb("junk2", [128, 512], BF16)
                    ss = [sb("sso%d" % i, [128, 8], F32) for i in range(2)]
                    Bmgt, Bxt, Bzt, Bj, Bss = [Buf(), Buf()], [Buf(), Buf()], [Buf(), Buf()], Buf(), [Buf(), Buf()]
                    for ts in range(20):
                        i = ts % 2
                        g = 0 if ts < 4 else 1
                        kb.dma("sp", mgt[i][:], mgv[:, :, ts * 128:(ts + 1) * 128], reads=[SB["mg"]], writes=[Bmgt[i]])
                        kb.dma("sp", xt[i][:], xsrc[ts * 128:(ts + 1) * 128, :], reads=[Bx], writes=[Bxt[i]])
                        bks = (0, 1, 2, 3) if i == 0 else (4, 5, 6, 0)
                        for cb in range(4):
                            bk = bks[cb]
                            for kc in range(16):
                                MM(PS[bk][:], mgt[i][:, kc, :], wo[:, kc, cb * 512:(cb + 1) * 512], kc == 0, kc == 15,
                                   [Bmgt[i], Bwo], [PB[bk]], kc == 15)
                            CP("dve", zt[i][:, cb * 512:(cb + 1) * 512], PS[bk][:], [PB[bk]], [Bzt[i]])
                            ACT(junk[:], zt[i][:, cb * 512:(cb + 1) * 512], AF.Square, [Bzt[i]], [Bj, Bss[i]], accum_out=ss[i][:, cb:cb + 1])
                        kb.op("dve", lambda e: e.tensor_reduce(out=ss[i][:, 4:5], in_=ss[i][:, 0:4], axis=AX.X, op=ALU.add), [], [Bss[i]])
                        TS("dve", ss[i][:, 5:6], ss[i][:, 4:5], 1.0 / D, EPS, ALU.mult, ALU.add, [], [Bss[i]])
                        ACT(ss[i][:, 6:7], ss[i][:, 5:6], AF.Sqrt, [], [Bss[i]])
                        kb.op("dve", lambda e: e.reciprocal(out=ss[i][:, 7:8], in_=ss[i][:, 6:7]), [], [Bss[i]])
                        STT(zt[i][:], zt[i][:], ss[i][:, 7:8], G[g][:], ALU.mult, ALU.mult, [Bss[i], BG[g]], [Bzt[i]])
                        TT("dve", zt[i][:], zt[i][:], xt[i][:], ALU.add, [Bxt[i]], [Bzt[i]])
                        kb.dma("sp", ydst[ts * 128:(ts + 1) * 128, :], zt[i][:], reads=[Bzt[i]], writes=[By])

            def phase_ssm(l):
                xbv = xbcT_s.rearrange("(fc p) t -> p fc t", p=128)
                obTv = oT_s[1].rearrange("(fc p) t -> p fc t", p=128)
                triu_b, tril_b, ones_b = cstb[:, 1, :], cstb[:, 2, :], cstb[:, 5, :]
                sgt_f, slt_f = cst[:, 3, :], cst[:, 4, :]
                one_col = cst[:, 5, 0:1]
                with ExitStack() as st:
                    sb = sbuf_alloc(st)
                    cw = sb("cw", [128, 16, 3], F32)
                    cbs = sb("cbs", [128, 16], F32)
                    dtb = sb("dtb", [128, 32], F32)
                    abc = sb("abc", [128, 32], F32)
                    dsk = sb("dsk", [128, 16], F32)
                    gss = sb("gss", [128, 1024], F32)
                    CT = sb("CTall", [128, 4, NTOK], BF16)
                    ea = sb("ea", [128, 20, 32], F32)
                    cd = sb("cd", [128, 20, 32], F32)
                    Bcw, Bdtb, Babc, Bdsk, Bgss, BCT, Bea, Bcd = (Buf() for _ in range(8))
                    kb.dma("sp", cw[:], cwT[l], writes=[Bcw])
                    kb.dma("sp", cbs[:], cbT[l], writes=[Bcw])
                    kb.dma("sp", dtb[:], dt_bias[l:l + 1, :].partition_broadcast(128), writes=[Bdtb])
                    kb.dma("sp", abc[:], a_log[l:l + 1, :].partition_broadcast(128), writes=[Babc])
                    ACT(abc[:], abc[:], AF.Exp, [], [Babc])
                    TS("dve", abc[:], abc[:], -1.0, None, ALU.mult, None, [], [Babc])
                    kb.dma("sp", dsk[:], d_skip[l:l + 1, :].partition_broadcast(128), writes=[Bdsk])
                    dtall = sb("dtall", [128, 20, 32], F32)
                    Bdtall = Buf()
                    kb.dma("sp", dtall[:].rearrange("p a b -> p (a b)"), dt_s[:, :], reads=[SB["dt"]], writes=[Bdtall])
                    TT("dve", dtall[:], dtall[:], dtb[:].unsqueeze(1).broadcast_to([128, 20, 32]), ALU.add, [Bdtb], [Bdtall])
                    ACT(dtall[:], dtall[:], AF.Exp, [], [Bdtall])
                    ACT(dtall[:], dtall[:], AF.Ln, [B_cst], [Bdtall], bias=one_col)
                    dt2 = sb("dt2", [128, 2, 20, 16], F32)
                    adt2 = sb("adt2", [128, 2, 20, 16], F32)
                    abk = sb("abk", [128, 2, 20, 16], F32)
                    ahi = sb("ahi", [128, 2, 20, 16], BF16)
                    alo = sb("alo", [128, 2, 20, 16], BF16)
                    ea2 = sb("ea2", [128, 2, 20, 16], F32)
                    cd2 = sb("cd2", [128, 2, 20, 16], F32)
                    w2 = sb("w2", [128, 2, 20, 16], F32)
                    Bsc = Buf()
                    for d in range(2):
                        CP("dve", dt2[:, d], dtall[:, :, d * 16:(d + 1) * 16], [Bdtall], [Bsc])
                        TT("dve", adt2[:, d], dt2[:, d], abc[:, d * 16:(d + 1) * 16].unsqueeze(1).broadcast_to([128, 20, 16]), ALU.mult,
                           [Babc], [Bsc])
                    CP("dve", ahi[:], adt2[:], [], [Bsc])
                    CP("dve", abk[:], ahi[:], [], [Bsc])
                    TT("dve", alo[:], adt2[:], abk[:], ALU.subtract, [], [Bsc])
                    for d in range(2):
                        V = triu_b if d == 0 else tril_b
                        for x, a_ in enumerate((ahi, alo)):
                            MM(PS[d][:, 0:320], V, a_[:, d].rearrange("p c h -> p (c h)"), x == 0, x == 1, [Bsc, B_cst], [PB[d]], x == 1)
                            MM(PS[2 + d][:, 0:320], ones_b, a_[:, d].rearrange("p c h -> p (c h)"), x == 0, x == 1, [Bsc, B_cst], [PB[2 + d]], x == 1)
                    for d in range(2):
                        f2 = lambda t: t[:, d].rearrange("p c h -> p (c h)")
                        ACT(f2(ea2), PS[d][:, 0:320], AF.Exp, [PB[d]], [Bsc])
                        ACT(f2(cd2), PS[2 + d][:, 0:320], AF.Exp, [PB[2 + d]], [Bsc])
                        CP("dve", f2(abk), PS[d][:, 0:320], [PB[d]], [Bsc])
                        TT("dve", f2(abk), PS[2 + d][:, 0:320], f2(abk), ALU.subtract, [PB[2 + d]], [Bsc])
                    ACT(abk[:], abk[:], AF.Exp, [], [Bsc])
                    TT("dve", w2[:], abk[:], dt2[:], ALU.mult, [], [Bsc])
                    Bea = Bsc
                    Bcd = Bsc
                    kb.dma("sp", gss[:], g_ssm[l:l + 1, :].partition_broadcast(128), writes=[Bgss])
                    SSTOP = int(os.environ.get("KDBG_SSTOP", "99"))
                    PBC = os.environ.get("KDBG_PBC", "dve")
                    if SSTOP <= 1:
                        kb.barrier()
                        return
                    with ExitStack() as st1:
                        sb1 = sbuf_alloc(st1)
                        xin = sb1("xin", [128, 16, 514], BF16)
                        hst = sb1("hst", [128, 16, 64], BF16)
                        Bhst = Buf()
                        xc = sb1("xc", [128, 12, 512], BF16)
                        acc = [sb1("cacc%d" % i, [128, 512], F32) for i in range(2)]
                        xs_tok = [sb1("xstok%d" % i, [128, 1024], BF16) for i in range(2)]
                        B_tok = [sb1("Btok%d" % i, [128, 4, 128], BF16) for i in range(2)]
                        sm = [sb1("sm%d" % i, [128, 8, 32], F32) for i in range(2)]
                        ahl = [sb1("ahl%d" % i, [128, 2, 32], BF16) for i in range(2)]
                        xdt = [sb1("xdt%d" % i, [128, 2, 1024], BF16) for i in range(2)]
                        xdd = [sb1("xdd%d" % i, [128, 2, 1024], BF16) for i in range(2)]
                        xsk = [sb1("xsk%d" % i, [128, 1024], F32) for i in range(2)]
                        CBm = [sb1("CBm%d" % i, [128, 2, 4, 128], BF16) for i in range(2)]
                        U = [sb1("U%d" % i, [128, 16, 128], BF16) for i in range(8)]
                        Lt = [sb1("Lt%d" % i, [128, 4, 128], BF16) for i in range(4)]
                        Mt = [sb1("Mt%d" % i, [128, 4, 128], BF16) for i in range(4)]
                        yas = [sb1("yas%d" % i, [128, 1024], F32) for i in range(2)]
                        sts = [sb1("sts%d" % i, [128, 1024], F32) for i in range(2)]
                        Bxin, Bxc = Buf(), Buf()
                        Bacc = [Buf(), Buf()]
                        Bxs, BBt, Bsm, Bahl = [Buf(), Buf()], [Buf(), Buf()], [Buf(), Buf()], [Buf(), Buf()]
                        Bxdt, Bxdd, Bxsk, BCBm = [Buf(), Buf()], [Buf(), Buf()], [Buf(), Buf()], [Buf(), Buf()]
                        BU = [Buf() for _ in range(8)]
                        BLt = [Buf() for _ in range(4)]
                        BMt = [Buf() for _ in range(4)]
                        Byas, Bsts = [Buf(), Buf()], [Buf(), Buf()]
                        kctr = dict(lm=0)
                        C1STOP = int(os.environ.get("KDBG_C1STOP", "99"))

                        def conv_tile(t0, n, left, right, lsrc=None, rsrc=None):
                            oth = REG[1 - slot]
                            if not left:
                                if lsrc is None:
                                    kb.op("pool", lambda e: e.memset(xin[:, :, 0:1], 0.0), [], [Bxin])
                                else:
                                    kb.dma("sp", hst[:], lsrc[0], reads=[oth.SB["xbc"]], writes=[Bhst])
                                    CP("dve", xin[:, :, 0:1], hst[:, :, lsrc[1]:lsrc[1] + 1], [Bhst], [Bxin])
                            if not right:
                                if rsrc is None:
                                    kb.op("pool", lambda e: e.memset(xin[:, :, n + 1:n + 2], 0.0), [], [Bxin])
                                else:
                                    kb.dma("sp", hst[:], rsrc[0], reads=[oth.SB["xbc"]], writes=[Bhst])
                                    CP("dve", xin[:, :, n + 1:n + 2], hst[:, :, rsrc[1]:rsrc[1] + 1], [Bhst], [Bxin])
                            a0, a1 = (0 if left else 1), (n + 2 if right else n + 1)
                            kb.dma("sp", xin[:, :, a0:a1], xbv[:, :, t0 - 1 + a0:t0 - 1 + a1], reads=[SB["xbc"]], writes=[Bxin])
                            CV = int(os.environ.get("KDBG_CV", "9"))
                            for fc in range(16 if CV > 1 else 0):
                                a = acc[fc % 2]
                                Ba = Bacc[fc % 2]
                                TS("dve", a[:, 0:n], xin[:, fc, 0:n], cw[:, fc, 0:1], cbs[:, fc:fc + 1], ALU.mult, ALU.add,
                                   [Bxin, Bcw], [Ba])
                                STT(a[:, 0:n], xin[:, fc, 1:n + 1], cw[:, fc, 1:2], a[:, 0:n], ALU.mult, ALU.add, [Bxin], [Ba])
                                STT(a[:, 0:n], xin[:, fc, 2:n + 2], cw[:, fc, 2:3], a[:, 0:n], ALU.mult, ALU.add, [Bxin], [Ba])
                                if CV <= 2:
                                    continue
                                if fc < 12:
                                    ACT(xc[:, fc, 0:n], a[:, 0:n], AF.Silu, [Ba], [Bxc])
                                else:
                                    ACT(CT[:, fc - 12, t0:t0 + n], a[:, 0:n], AF.Silu, [Ba], [BCT])

                        def chunk1(c, off):
                            if os.environ.get("KDBG_CBAR"):
                                kb.barrier()
                            i = c % 2
                            tok = c * 128
                            for fc in range(8):
                                TR(PSb(6)[:, fc * 128:(fc + 1) * 128], xc[:, fc, off:off + 128], identb, [Bxc, B_cst], [PB[6]], fc == 7)
                            CP("dve", xs_tok[i][:], PSb(6), [PB[6]], [Bxs[i]])
                            for g in range(4):
                                TR(PSb(6)[:, g * 128:(g + 1) * 128], xc[:, 8 + g, off:off + 128], identb, [Bxc, B_cst], [PB[6]], g == 3)
                            CP("dve", B_tok[i][:].rearrange("p a b -> p (a b)"), PSb(6)[:, 0:512], [PB[6]], [BBt[i]])
                            if c >= 1 and C1STOP <= 1:
                                return
                            if c >= 1 and C1STOP <= 3:
                                return
                            xv = xs_tok[i][:].rearrange("p (h q) -> p h q", q=64)
                            for d in range(2):
                                TT("dve", xdt[i][:, d, :].rearrange("p (h q) -> p h q", q=64), xv,
                                   dt2[:, d, c, :].unsqueeze(2).broadcast_to([128, 16, 64]), ALU.mult,
                                   [Bxs[i], Bsc], [Bxdt[i]])
                                TT(PBC, xdd[i][:, d, :].rearrange("p (h q) -> p h q", q=64), xv,
                                   w2[:, d, c, :].unsqueeze(2).broadcast_to([128, 16, 64]), ALU.mult,
                                   [Bxs[i], Bsc], [Bxdd[i]])
                            TT(PBC, xsk[i][:].rearrange("p (h q) -> p h q", q=64), xv,
                               dsk[:].unsqueeze(2).broadcast_to([128, 16, 64]), ALU.mult, [Bxs[i], Bdsk], [Bxsk[i]])
                            if c >= 1 and C1STOP <= 4:
                                return
                            for g in range(4):
                                MM(PS[4][:, g * 128:(g + 1) * 128], xc[:, 8 + g, off:off + 128], CT[:, g, tok:tok + 128], True, True,
                                   [Bxc, BCT], [PB[4]], g == 3)
                            P4 = PS[4][:].rearrange("p (a b) -> p a b", a=4)
                            TT("dve", CBm[i][:, 0], P4, triu_b.unsqueeze(1).broadcast_to([128, 4, 128]), ALU.mult, [PB[4], B_cst], [BCBm[i]])
                            TT("dve", CBm[i][:, 1], P4, tril_b.unsqueeze(1).broadcast_to([128, 4, 128]), ALU.mult, [PB[4], B_cst], [BCBm[i]])
                            if c >= 1 and C1STOP <= 5:
                                return
                            for d in range(2):
                                mk = sgt_f if d == 0 else slt_f
                                for x in range(2):
                                    TT(PBC if x == 0 else "dve", U[i * 4 + d * 2 + x][:], mk.unsqueeze(1).broadcast_to([128, 16, 128]),
                                       (ahi, alo)[x][:, d, c, :].unsqueeze(2).broadcast_to([128, 16, 128]), ALU.mult,
                                       [Bsc, B_cst], [BU[i * 4 + d * 2 + x]])
                            if c >= 1 and C1STOP <= 6:
                                return
                            for g in range(4):
                                lm = []
                                for d in range(2):
                                    bk = 2 + d
                                    V = triu_b if d == 0 else tril_b
                                    for hh in range(4):
                                        h = 4 * g + hh
                                        MM(PS[bk][:, hh * 128:(hh + 1) * 128], U[i * 4 + d * 2][:, h, :], V, True, False,
                                           [BU[i * 4 + d * 2], B_cst], [PB[bk]], False)
                                        MM(PS[bk][:, hh * 128:(hh + 1) * 128], U[i * 4 + d * 2 + 1][:, h, :], V, False, True,
                                           [BU[i * 4 + d * 2 + 1], B_cst], [PB[bk]], hh == 3)
                                    k = kctr["lm"] % 4
                                    kctr["lm"] += 1
                                    ACT(Lt[k][:].rearrange("p a b -> p (a b)"), PS[bk][:], AF.Exp, [PB[bk]], [BLt[k]])
                                    TT("dve", Mt[k][:], Lt[k][:], CBm[i][:, d, g, :].unsqueeze(1).broadcast_to([128, 4, 128]), ALU.mult,
                                       [BLt[k], BCBm[i]], [BMt[k]])
                                    lm.append(k)
                                for hh in range(4):
                                    h = 4 * g + hh
                                    yb = 0 if h < 8 else 1
                                    yo = (h % 8) * 64
                                    for d in range(2):
                                        MM(PS[yb][:, yo:yo + 64], Mt[lm[d]][:, hh, :], xdt[i][:, d, h * 64:(h + 1) * 64], d == 0, d == 1,
                                           [BMt[lm[d]], Bxdt[i]], [PB[yb]], (d == 1 and hh == 3))
                            if c >= 1 and C1STOP <= 7:
                                return
                            for yb in range(2):
                                TT("dve", yas[i][:, yb * 512:(yb + 1) * 512], PS[yb][:], xsk[i][:, yb * 512:(yb + 1) * 512], ALU.add,
                                   [PB[yb], Bxsk[i]], [Byas[i]])
                            kb.dma("pool", ya_s[tok:tok + 128, :], yas[i][:], reads=[Byas[i]], writes=[SB["ya"]])
                            if c >= 1 and C1STOP <= 8:
                                return
                            for d in range(2):
                                for gp in range(2):
                                    bk = 4 + gp
                                    for gg in range(2):
                                        g = 2 * gp + gg
                                        MM(PS[bk][:, gg * 256:(gg + 1) * 256], B_tok[i][:, g, :], xdd[i][:, d, g * 256:(g + 1) * 256],
                                           True, True, [BBt[i], Bxdd[i]], [PB[bk]], gg == 1)
                                    CP("dve", sts[d][:, gp * 512:(gp + 1) * 512], PS[bk][:], [PB[bk]], [Bsts[d]])
                                kb.dma("pool", stc_s[c, d], sts[d][:], reads=[Bsts[d]], writes=[SB["stc"]])

                        for sq in range(2):
                            conv_tile(sq * 256, 256, False, False)
                            if SSTOP <= 2:
                                break
                            for cc in range(2):
                                chunk1(sq * 2 + cc, cc * 128)
                                if SSTOP <= 3:
                                    break
                            if SSTOP <= 3:
                                break
                        NCH = int(os.environ.get("KDBG_NCH", "99"))
                        for tt in range(1, 5 if SSTOP > 3 else 0):
                            if (tt - 1) * 4 >= NCH:
                                break
                            oxbv = REG[1 - slot].xbcT_s.rearrange("(fc p) t -> p fc t", p=128)
                            conv_tile(tt * 512, 512, tt > 1, tt < 4,
                                      lsrc=(oxbv[:, :, 2496:2560], 63) if (slot == 1 and tt == 1) else None,
                                      rsrc=(oxbv[:, :, 512:576], 0) if (slot == 0 and tt == 4) else None)
                            for cc in range(4):
                                if (tt - 1) * 4 + cc >= NCH:
                                    break
                                chunk1(tt * 4 + cc, cc * 128)
                    kb.dma("sp", CT_s[:, :], CT[:].rearrange("p g t -> p (g t)"), reads=[BCT], writes=[SB["ct"]])
                    kb.dma("sp", ea_s[:, :], ea2[:].rearrange("p d c h -> p (d c h)"), reads=[Bsc], writes=[SB["ea"]])
                    kb.dma("sp", cd_s[:, :], cd2[:].rearrange("p d c h -> p (d c h)"), reads=[Bsc], writes=[SB["cd"]])
                kb.barrier()

            ns = types.SimpleNamespace(**{k: v for k, v in locals().items() if not k.startswith("_")})
            REG[slot] = ns
            return ns

        make_slot(0)
        make_slot(1)

        def ssm_pass2(l):
            triu_b = cstb[:, 1, :]
            with ExitStack() as st2:
                sb2 = sbuf_alloc(st2)
                gss = sb2("gss2", [128, 1024], F32)
                Bgss = Buf()
                kb.dma("sp", gss[:], g_ssm[l:l + 1, :].partition_broadcast(128), writes=[Bgss])
                ea2 = [sb2("ea2_%d" % s_, [128, 2, 20, 16], F32) for s_ in range(2)]
                cd2 = [sb2("cd2_%d" % s_, [128, 2, 20, 16], F32) for s_ in range(2)]
                Bea, Bcd = Buf(), Buf()
                for s_ in range(2):
                    kb.dma("sp", ea2[s_][:].rearrange("p d c h -> p (d c h)"), REG[s_].ea_s[:, :], reads=[REG[s_].SB["ea"]], writes=[Bea])
                    kb.dma("sp", cd2[s_][:].rearrange("p d c h -> p (d c h)"), REG[s_].cd_s[:, :], reads=[REG[s_].SB["cd"]], writes=[Bcd])
                H = [sb2("H%d" % d, [128, 1024], F32) for d in range(2)]
                BH = [Buf(), Buf()]
                stl = [sb2("stl%d" % i, [128, 1024], F32) for i in range(2)]
                Bstl = [Buf(), Buf()]
                hpb = [sb2("hpb%d" % i, [128, 1024], BF16) for i in range(2)]
                Bhpb = [Buf(), Buf()]
                hpf = [sb2("hpf%d" % i, [128, 1024], BF16) for i in range(2)]
                Bhpf = [Buf(), Buf()]
                ctl = [sb2("ctl%d" % i, [128, 4, 128], BF16) for i in range(2)]
                Bctl = [Buf(), Buf()]
                s0 = sb2("s0", [128, 8, 128], F32)
                Bs0 = Buf()
                fin = sb2("fin", [128, 8, 128], F32)
                Bfin = Buf()
                yal = [sb2("yal%d" % i, [128, 1024], F32) for i in range(2)]
                Byal = [Buf(), Buf()]
                zsl = [sb2("zsl%d" % i, [128, 1024], BF16) for i in range(2)]
                Bzsl = [Buf(), Buf()]
                t2 = [sb2("t2_%d" % i, [128, 1024], F32) for i in range(2)]
                Bt2 = [Buf(), Buf()]
                junk = sb2("junk3", [128, 1024], BF16)
                Bj = Buf()
                ssq = [sb2("ssq%d" % i, [128, 4], F32) for i in range(2)]
                Bssq = [Buf(), Buf()]
                obt = [sb2("obt%d" % i, [128, 1024], BF16) for i in range(2)]
                Bobt = [Buf(), Buf()]
                obT = sb2("obTst", [128, 8, 512], BF16)
                BobT = Buf()
                Bout = Buf()
                cn = dict(st=0, hp=0, f=0)

                def init_state(d, sample):
                    if not sample:
                        kb.op("dve", lambda e: e.memset(H[d][:], 0.0), [], [BH[d]])
                        return
                    kb.dma("sp", s0[:], st0[l, d].rearrange("(a p) n -> p a n", p=128), writes=[Bs0])
                    for half in range(2):
                        bk = 4 + half
                        for a in range(4):
                            TR(PS[bk][:, a * 128:(a + 1) * 128], s0[:, half * 4 + a, :], identf, [Bs0, B_cst], [PB[bk]], a == 3)
                        CP("dve", H[d][:, half * 512:(half + 1) * 512], PS[bk][:], [PB[bk]], [BH[d]])

                def prefetch_state(d, sl_, c):
                    S_ = REG[sl_]
                    j = cn["st"] % 2
                    kb.dma("sp", stl[j][:], S_.stc_s[c, d], reads=[S_.SB["stc"]], writes=[Bstl[j]])

                def step_state(d, sl_, c):
                    S_ = REG[sl_]
                    j = cn["st"] % 2
                    cn["st"] += 1
                    Hv = H[d][:].rearrange("p (h q) -> p h q", q=64)
                    TT("dve", Hv, Hv, cd2[sl_][:, d, c, :].unsqueeze(2).broadcast_to([128, 16, 64]), ALU.mult, [Bcd], [BH[d]])
                    TT("dve", H[d][:], H[d][:], stl[j][:], ALU.add, [Bstl[j]], [BH[d]])

                def emit_state(d, dst):
                    for half in range(2):
                        bk = 4 + half
                        for a in range(4):
                            k8 = half * 4 + a
                            TR(PS[bk][:, a * 128:(a + 1) * 128], H[d][:, k8 * 128:(k8 + 1) * 128], identf, [BH[d], B_cst], [PB[bk]], a == 3)
                        CP("dve", fin[:, half * 4:(half + 1) * 4, :], PS[bk][:].rearrange("p (a b) -> p a b", a=4), [PB[bk]], [Bfin])
                    kb.dma("pool", dst.rearrange("(a p) n -> p a n", p=128), fin[:], reads=[Bfin], writes=[Bout])

                def bwd_visit(sl_, c):
                    S_ = REG[sl_]
                    prefetch_state(1, sl_, c)
                    j = cn["hp"] % 2
                    cn["hp"] += 1
                    CP("dve", hpb[j][:], H[1][:], [BH[1]], [Bhpb[j]])
                    kb.dma("pool", S_.hpb_s[c], hpb[j][:], reads=[Bhpb[j]], writes=[S_.SB["hpb"]])
                    step_state(1, sl_, c)

                def fwd_visit(sl_, c):
                    S_ = REG[sl_]
                    prefetch_state(0, sl_, c)
                    i = cn["f"] % 2
                    cn["f"] += 1
                    tok = c * 128
                    CP("dve", hpf[i][:], H[0][:], [BH[0]], [Bhpf[i]])
                    kb.dma("sp", hpb[i][:], S_.hpb_s[c], reads=[S_.SB["hpb"]], writes=[Bhpb[i]])
                    kb.dma("sp", yal[i][:], S_.ya_s[tok:tok + 128, :], reads=[S_.SB["ya"]], writes=[Byal[i]])
                    kb.dma("sp", zsl[i][:], S_.zs_s[tok:tok + 128, :], reads=[S_.SB["z"]], writes=[Bzsl[i]])
                    kb.dma("sp", ctl[i][:], S_.CT_s.rearrange("p (g t) -> p g t", g=4)[:, :, tok:tok + 128], reads=[S_.SB["ct"]],
                           writes=[Bctl[i]])
                    for d in range(2):
                        hp_ = hpf[i] if d == 0 else hpb[i]
                        Bhp = Bhpf[i] if d == 0 else Bhpb[i]
                        for g in range(4):
                            bk = 2 * d + g // 2
                            MM(PS[bk][:, (g % 2) * 256:(g % 2 + 1) * 256], ctl[i][:, g, :], hp_[:, g * 256:(g + 1) * 256],
                               True, True, [Bctl[i], Bhp], [PB[bk]], g % 2 == 1)
                    for d in range(2):
                        for hb in range(2):
                            bk = 2 * d + hb
                            src = PS[bk][:].rearrange("p (h q) -> p h q", q=64)
                            dstv = t2[i][:, hb * 512:(hb + 1) * 512].rearrange("p (h q) -> p h q", q=64)
                            eav = ea2[sl_][:, d, c, hb * 8:hb * 8 + 8].unsqueeze(2).broadcast_to([128, 8, 64])
                            if d == 0:
                                TT("dve", dstv, src, eav, ALU.mult, [PB[bk], Bea], [Bt2[i]])
                            else:
                                TT("dve", src, src, eav, ALU.mult, [Bea], [PB[bk]])
                                TT("dve", t2[i][:, hb * 512:(hb + 1) * 512], t2[i][:, hb * 512:(hb + 1) * 512], PS[bk][:], ALU.add,
                                   [PB[bk]], [Bt2[i]])
                    TT("dve", t2[i][:], t2[i][:], yal[i][:], ALU.add, [Byal[i]], [Bt2[i]])
                    TT("dve", t2[i][:], t2[i][:], zsl[i][:], ALU.mult, [Bzsl[i]], [Bt2[i]])
                    ACT(junk[:], t2[i][:], AF.Square, [Bt2[i]], [Bj, Bssq[i]], accum_out=ssq[i][:, 0:1])
                    TS("dve", ssq[i][:, 1:2], ssq[i][:, 0:1], 1.0 / 1024, EPS, ALU.mult, ALU.add, [], [Bssq[i]])
                    ACT(ssq[i][:, 2:3], ssq[i][:, 1:2], AF.Sqrt, [], [Bssq[i]])
                    kb.op("dve", lambda e: e.reciprocal(out=ssq[i][:, 3:4], in_=ssq[i][:, 2:3]), [], [Bssq[i]])
                    STT(obt[i][:], t2[i][:], ssq[i][:, 3:4], gss[:], ALU.mult, ALU.mult, [Bt2[i], Bssq[i], Bgss], [Bobt[i]])
                    for fc in range(8):
                        TR(PSb(6)[:, fc * 128:(fc + 1) * 128], obt[i][:, fc * 128:(fc + 1) * 128], identb, [Bobt[i], B_cst], [PB[6]], fc == 7)
                    CP("dve", obT[:, :, (c % 4) * 128:(c % 4 + 1) * 128], PSb(6).rearrange("p (a b) -> p a b", a=8), [PB[6]], [BobT])
                    if c % 4 == 3:
                        obTv = S_.oT_s[1].rearrange("(fc p) t -> p fc t", p=128)
                        kb.dma("pool", obTv[:, :, (c - 3) * 128:(c + 1) * 128], obT[:], reads=[BobT], writes=[S_.SB["ob"]])
                    step_state(0, sl_, c)

                for sl_ in range(2):
                    for sq in range(2):
                        init_state(1, False)
                        for c in (sq * 2 + 1, sq * 2):
                            bwd_visit(sl_, c)
                        emit_state(1, REG[sl_].sbo[l, sq])
                init_state(1, True)
                for sl_ in (1, 0):
                    for c in range(19, 3, -1):
                        bwd_visit(sl_, c)
                for sl_ in range(2):
                    for sq in range(2):
                        init_state(0, False)
                        for c in (sq * 2, sq * 2 + 1):
                            fwd_visit(sl_, c)
                        emit_state(0, REG[sl_].sfo[l, sq])
                init_state(0, True)
                for sl_ in (0, 1):
                    for c in range(4, 20):
                        fwd_visit(sl_, c)
            kb.barrier()

        for l in range(n_layers):
            REG[0].phase_mod(l)
            kb.barrier()
            for sl_ in range(2):
                S_ = REG[sl_]
                xsrc = S_.x_in if l == 0 else S_.y1_s
                Bx = Buf() if l == 0 else S_.SB["y1"]
                with ExitStack() as lst:
                    hT = lst.enter_context(nc.sbuf_tensor("hT%d_%d" % (l, sl_), [128, 16, NTOK], BF16))
                    BhTs = [Buf("hT%d" % i) for i in range(20)]
                    S_.phase_norm(l, xsrc, Bx, hT, BhTs)
                    kb.barrier()
                    S_.phase_proj(l, hT, Buf("hTro"))
                kb.barrier()
            for nm in ("phase_attn", "phase_sgu", "phase_ssm"):
                for sl_ in range(2):
                    getattr(REG[sl_], nm)(l)
                    kb.barrier()
            ssm_pass2(l)
            for sl_ in range(2):
                S_ = REG[sl_]
                xsrc = S_.x_in if l == 0 else S_.y1_s
                Bx = Buf() if l == 0 else S_.SB["y1"]
                ydst = S_.y_out if l == n_layers - 1 else S_.y1_s
                By = S_.SB["yout"] if l == n_layers - 1 else S_.SB["y1"]
                S_.phase_o1(l)
                kb.barrier()
                S_.phase_o2(l, xsrc, Bx, ydst, By)
                kb.barrier()

        kb.barrier()
    return nc


def _consts():
    s = np.arange(128)[:, None]
    i = np.arange(128)[None, :]
    c = np.zeros((128, 6, 128), np.float32)
    c[:, 0] = (s == i)
    c[:, 1] = (s <= i)
    c[:, 2] = (s >= i)
    c[:, 3] = (s > i)
    c[:, 4] = (s < i)
    c[:, 5] = 1.0
    return c


def _bias_F(rpb_l):
    j = np.arange(64)[None, :]
    col = np.arange(64)[:, None]
    cs = np.clip(j - 8, 0, 48)
    valid = (col >= cs) & (col < cs + 16)
    idx = np.clip(col - j + 15, 0, 30)
    T = np.where(valid[None, None], rpb_l[:, :, idx], np.float32(NEG)).astype(np.float32)
    F = np.empty((2, 2, 64, 16, 6, 2, 64), np.float32)
    for kind in range(2):
        for c in range(6):
            for a in range(2):
                for e in range(2):
                    ri = 2 * c + a - e + (3 if kind == 0 else 1)
                    F[kind, a, :, :, c, e, :] = np.transpose(T[:, ri], (1, 0, 2))
    return F.reshape(2, 128, 16 * 6 * 128)


def _row_masks(half):
    out = np.zeros((5, 2, 64, 6, 2, 64), np.float32)
    slots = [8, 0, 1, 14, 15]
    for si, ml in enumerate(slots):
        kbg = 32 * half + (2 * ml if ml != 15 else 28) - 4
        for c in range(6):
            for a in range(2):
                kr = kbg + 2 * c + a
                for e in range(2):
                    qr = 32 * half + 2 * ml + e
                    rs = min(max(qr - 4, 0), 56)
                    ok = (0 <= kr < 64) and (rs <= kr < rs + 8)
                    out[si, a, :, c, e, :] = 0.0 if ok else NEG
    return out.reshape(5, 128, 6 * 128)


def prep_shared(inp):
    f = lambda a: np.ascontiguousarray(a, dtype=np.float32)
    sh = {}
    for k in ("w_mod", "b_mod", "g_pre", "g_post", "w_in", "g_ssm", "w_br_a", "w_br_b", "w_br_c", "w_out", "d_skip"):
        sh[k] = f(inp[k])
    sh["Fb"] = np.stack([_bias_F(np.asarray(inp["rpb"][l], np.float32)) for l in range(DEPTH)])
    sh["cwT"] = f(np.asarray(inp["conv_w"]).reshape(DEPTH, 16, 128, 3).transpose(0, 2, 1, 3))
    sh["cbT"] = f(np.asarray(inp["conv_b"]).reshape(DEPTH, 16, 128).transpose(0, 2, 1))
    sh["dt_bias"] = f(np.asarray(inp["dt_bias"]).reshape(DEPTH, 32))
    sh["a_log"] = f(np.asarray(inp["a_log"]).reshape(DEPTH, 32))
    sh["wsT"] = f(np.asarray(inp["w_s"]).transpose(0, 3, 1, 2))
    sh["b_s"] = f(np.asarray(inp["b_s"]).reshape(DEPTH, 1024))
    sh["g_sguT"] = f(np.asarray(inp["g_sgu"]).reshape(DEPTH, 8, 128).transpose(0, 2, 1))
    sh["consts"] = _consts()
    return sh


def prep_core(inp, sh, core):
    f = lambda a: np.ascontiguousarray(a, dtype=np.float32)
    b = core
    m = dict(sh)
    xs_ = []
    for slot in range(2):
        xp = np.asarray(inp["x_prompt"])[4 * core + 2 * slot:4 * core + 2 * slot + 2].reshape(512, D)
        xs = np.asarray(inp["x_sample"])[b, slot * NST:(slot + 1) * NST]
        xs_.append(np.concatenate([xp, xs], 0))
    m["x_in"] = f(np.stack(xs_, 0))
    cvec = np.stack([np.asarray(inp["c_ctx"]), np.asarray(inp["c"])[b]], 0)
    m["cvT"] = f(cvec.reshape(2, 16, 128).transpose(2, 1, 0))
    m["ck"] = f(np.asarray(inp["cache_k"])[b].reshape(DEPTH, 256, 1024))
    m["cv"] = f(np.asarray(inp["cache_v"])[b].reshape(DEPTH, 256, 1024))
    m["st0"] = f(np.stack([np.asarray(inp["state_ssm_fwd"])[b].reshape(DEPTH, 1024, 128),
                           np.asarray(inp["state_ssm_bwd"])[b].reshape(DEPTH, 1024, 128)], 1))
    m["rm"] = np.stack([_row_masks(0), _row_masks(1)], 0)
    return m


_NC_CACHE = {}
N_ACTIVE = 4


def kernel(**inputs):
    if "nc" not in _NC_CACHE:
        _NC_CACHE["nc"] = build()
    nc = _NC_CACHE["nc"]
    sh = prep_shared(inputs)
    in_maps = [prep_core(inputs, sh, c) for c in range(N_ACTIVE)]
    res = run_bass_kernel_spmd(nc, in_maps, core_ids=list(range(N_ACTIVE)))
    R = res.results
    y_p = np.empty((16, 256, D), np.float32)
    y_s = np.empty((4, 4096, D), np.float32)
    nk = np.empty((16, DEPTH, 256, 16, 64), np.float32)
    nv = np.empty((16, DEPTH, 256, 16, 64), np.float32)
    nf = np.empty((16, DEPTH, 16, 64, 128), np.float32)
    nb_ = np.empty((16, DEPTH, 16, 64, 128), np.float32)
    for c in range(N_ACTIVE):
        r = R[c]
        for slot in range(2):
            yo = r["y_out"][slot]
            y_s[c, slot * NST:(slot + 1) * NST] = yo[512:]
            for s in range(2):
                q = 4 * c + 2 * slot + s
                y_p[q] = yo[s * 256:(s + 1) * 256]
                for l in range(DEPTH):
                    nk[q, l] = r["ko"][slot, l, s * 256:(s + 1) * 256].reshape(256, 16, 64)
                    nv[q, l] = r["vo"][slot, l, s * 256:(s + 1) * 256].reshape(256, 16, 64)
                    nf[q, l] = r["sfo"][slot, l, s].reshape(16, 64, 128)
                    nb_[q, l] = r["sbo"][slot, l, s].reshape(16, 64, 128)
    return (y_p, y_s, nk, nv, nf, nb_)
```

```python
import os
import types
import numpy as np
from contextlib import ExitStack
import concourse.bass as bass
import concourse.mybir as mybir
from concourse.bass_utils import run_bass_kernel_spmd

F32 = mybir.dt.float32
BF16 = mybir.dt.bfloat16
AF = mybir.ActivationFunctionType
ALU = mybir.AluOpType
AX = mybir.AxisListType

D = 2048
NTOK = 2560
NPT = 512
NST = 2048
NIN = 16416
DEPTH = 2
EPS = 1e-6
NEG = -30000.0
OFF = dict(q=0, k=1024, v=2048, ga=3072, xbc=4096, z=6144, dt=7168, u=7200, vc=8224, gc=9248, gm=10272)
N_CORES = 8


class Buf:
    __slots__ = ("w", "r", "name")

    def __init__(self, name=""):
        self.w = None
        self.r = {}
        self.name = name


class KB:
    def __init__(self, nc, es, nds=28):
        self.nc = nc
        self.E = {"pe": nc.tensor, "act": nc.scalar, "dve": nc.vector, "pool": nc.gpsimd, "sp": nc.sync}
        self.sem = {}
        self.cnt = {}
        for k in ("pe", "act", "dve", "pool"):
            self.sem[k] = es.enter_context(nc.semaphore("s_" + k))
            self.cnt[k] = 0
        for i in range(nds):
            self.sem[("d", i)] = es.enter_context(nc.semaphore("sd%d" % i))
            self.cnt[("d", i)] = 0
        self.nds = nds
        self.dnext = {"sp": 0, "pool": 0, "act": 0}
        self.seen = {e: {} for e in self.E}

    def wait(self, eng, toks):
        seen = self.seen[eng]
        for t in toks:
            k, v = t
            if v <= 0 or seen.get(k, 0) >= v:
                continue
            if k == "pe" and eng == "pe":
                continue
            self.E[eng].wait_ge(self.sem[k], v)
            seen[k] = v

    @staticmethod
    def deps(reads, writes):
        toks = []
        for b in reads:
            if b.w is not None:
                toks.append(b.w)
        for b in writes:
            if b.w is not None:
                toks.append(b.w)
            toks.extend(b.r.items())
        return toks

    @staticmethod
    def mark(tok, reads, writes):
        k, v = tok
        for b in reads:
            if b.r.get(k, 0) < v:
                b.r[k] = v
        for b in writes:
            b.w = tok
            b.r = {}

    def op(self, eng, fn, reads=(), writes=(), inc=True):
        if eng == "pool" and not os.environ.get("KDBG_POOLC"):
            eng = "dve"
        self.wait(eng, self.deps(reads, writes))
        ins = fn(self.E[eng])
        if inc:
            self.cnt[eng] += 1
            ins.then_inc(self.sem[eng], 1)
            tok = (eng, self.cnt[eng])
        else:
            tok = (eng, self.cnt[eng] + 1)
        self.mark(tok, reads, writes)
        return tok

    def dma(self, q, out, in_, reads=(), writes=()):
        lo, n = {"sp": (0, int(os.environ.get("KDBG_NSP", "16"))), "pool": (16, 8), "act": (24, 4)}[q]
        j = self.dnext[q]
        self.dnext[q] = (j + 1) % n
        i = lo + j
        k = ("d", i)
        toks = self.deps(reads, writes)
        toks.append((k, self.cnt[k]))
        self.wait(q, toks)
        self.cnt[k] += 16
        self.E[q].dma_start(out=out, in_=in_).then_inc(self.sem[k], 16)
        tok = (k, self.cnt[k])
        self.mark(tok, reads, writes)
        return tok

    def barrier(self):
        toks = [(k, v) for k, v in self.cnt.items() if v > 0]
        for e in self.E:
            self.wait(e, toks)


def build(n_layers=DEPTH, stop=None, dbg=(), dbg_in=(), start=None, skip=()):
    nc = bass.Bass("TRN2", target_bir_lowering=False)

    def din(name, shape):
        return nc.dram_tensor(name, list(shape), F32, kind="ExternalInput").ap()

    def dout(name, shape):
        return nc.dram_tensor(name, list(shape), F32, kind="ExternalOutput").ap()

    def dscr(name, shape, dt):
        kind = "ExternalOutput" if name in dbg else ("ExternalInput" if name in dbg_in else "Internal")
        return nc.dram_tensor(name, list(shape), dt, kind=kind).ap()

    x_in_all = din("x_in", [2, NTOK, D])
    cvT = din("cvT", [128, 16, 2])
    ck = din("ck", [DEPTH, 256, 1024])
    cv = din("cv", [DEPTH, 256, 1024])
    st0 = din("st0", [DEPTH, 2, 1024, 128])
    w_mod = din("w_mod", [DEPTH, D, 3 * D])
    b_mod = din("b_mod", [DEPTH, 3 * D])
    g_pre = din("g_pre", [DEPTH, D])
    g_post = din("g_post", [DEPTH, D])
    w_in = din("w_in", [DEPTH, D, NIN])
    Fb = din("Fb", [DEPTH, 2, 128, 16 * 6 * 128])
    rm_all = din("rm", [2, 5, 128, 6 * 128])
    cwT = din("cwT", [DEPTH, 128, 16, 3])
    cbT = din("cbT", [DEPTH, 128, 16])
    dt_bias = din("dt_bias", [DEPTH, 32])
    a_log = din("a_log", [DEPTH, 32])
    d_skip = din("d_skip", [DEPTH, 16])
    g_ssm = din("g_ssm", [DEPTH, 1024])
    wsT = din("wsT", [DEPTH, 128, 8, 128])
    b_s = din("b_s", [DEPTH, 1024])
    g_sguT = din("g_sguT", [DEPTH, 128, 8])
    w_br = [din("w_br_a", [DEPTH, 1024, D]), din("w_br_b", [DEPTH, 1024, D]), din("w_br_c", [DEPTH, 1024, D])]
    w_out = din("w_out", [DEPTH, D, D])
    consts = din("consts", [128, 6, 128])
    y_out_all = dout("y_out", [2, NTOK, D])
    ko_all = dout("ko", [2, DEPTH, NPT, 1024])
    vo_all = dout("vo", [2, DEPTH, NPT, 1024])
    sfo_all = dout("sfo", [2, DEPTH, 2, 1024, 128])
    m_s = dscr("m_s", [2, 3 * D], F32)
    sbo_all = dout("sbo", [2, DEPTH, 2, 1024, 128])
    es = ExitStack()
    with es:
        kb = KB(nc, es)
        PS = [es.enter_context(nc.psum_tensor("ps%d" % i, [128, 512], F32)) for i in range(8)]
        PB = [Buf("ps%d" % i) for i in range(8)]

        def PSb(i):
            return PS[i][:].bitcast(BF16)

        cst = es.enter_context(nc.sbuf_tensor("cst", [128, 6, 128], F32))
        cstb = es.enter_context(nc.sbuf_tensor("cstb", [128, 6, 128], BF16))
        B_cst = Buf("cst")
        kb.dma("sp", cst[:], consts[:, :, :], writes=[B_cst])
        kb.op("dve", lambda e: e.tensor_copy(out=cstb[:], in_=cst[:]), reads=[B_cst], writes=[B_cst])
        identb = cstb[:, 0, :]
        identf = cst[:, 0, :]

        uid = [0]

        def sbuf_alloc(stack):
            def f(name, shape, dt):
                uid[0] += 1
                return stack.enter_context(nc.sbuf_tensor("%s_%d" % (name, uid[0]), list(shape), dt))
            return f

        def MM(out, lhsT, rhs, start, stop, R, W, inc):
            return kb.op("pe", lambda e: e.matmul(out, lhsT=lhsT, rhs=rhs, start=start, stop=stop), R, W, inc)

        def TR(out, in_, ident, R, W, inc=True):
            return kb.op("pe", lambda e: e.transpose(out, in_, ident), R, W, inc)

        def ACT(out, in_, func, R, W, bias=None, scale=None, accum_out=None):
            kw = {}
            if bias is not None:
                kw["bias"] = bias
            if scale is not None:
                kw["scale"] = scale
            if accum_out is not None:
                kw["accum_out"] = accum_out
            return kb.op("act", lambda e: e.activation(out=out, in_=in_, func=func, **kw), R, W)

        def TT(eng, out, in0, in1, op, R, W):
            return kb.op(eng, lambda e: e.tensor_tensor(out=out, in0=in0, in1=in1, op=op), R, W)

        def TS(eng, out, in0, s1, s2, op0, op1, R, W):
            if op1 is None:
                return kb.op(eng, lambda e: e.tensor_scalar(out=out, in0=in0, scalar1=s1, scalar2=None, op0=op0), R, W)
            return kb.op(eng, lambda e: e.tensor_scalar(out=out, in0=in0, scalar1=s1, scalar2=s2, op0=op0, op1=op1), R, W)

        def STT(out, in0, scalar, in1, op0, op1, R, W):
            return kb.op("dve", lambda e: e.scalar_tensor_tensor(out=out, in0=in0, scalar=scalar, in1=in1, op0=op0, op1=op1), R, W)

        def CP(eng, out, in_, R, W):
            if eng == "act" and os.environ.get("KDBG_NOACTCP"):
                eng = "dve"
            if eng == "act":
                return kb.op("act", lambda e: e.activation(out=out, in_=in_, func=AF.Identity), R, W)
            return kb.op(eng, lambda e: e.tensor_copy(out=out, in_=in_), R, W)

        SBm = Buf("m")
        REG = {}

        def make_slot(slot):
            x_in = x_in_all[slot]
            y_out = y_out_all[slot]
            ko, vo, sfo, sbo = ko_all[slot], vo_all[slot], sfo_all[slot], sbo_all[slot]
            rm = rm_all[slot]
            y1_s = dscr("s%d_" % slot + "y1_s", [NTOK, D], F32)
            qT_s = dscr("s%d_" % slot + "qT_s", [1024, NTOK], BF16)
            kT_s = dscr("s%d_" % slot + "kT_s", [1024, NTOK], BF16)
            v_s = dscr("s%d_" % slot + "v_s", [NTOK, 1024], BF16)
            ga_s = dscr("s%d_" % slot + "ga_s", [NTOK, 1024], BF16)
            xbcT_s = dscr("s%d_" % slot + "xbcT_s", [2048, NTOK], BF16)
            zs_s = dscr("s%d_" % slot + "zs_s", [NTOK, 1024], BF16)
            dt_s = dscr("s%d_" % slot + "dt_s", [128, 20 * 32], F32)
            uT_s = dscr("s%d_" % slot + "uT_s", [1024, NTOK], BF16)
            vc_s = dscr("s%d_" % slot + "vc_s", [NTOK, 1024], BF16)
            gcT_s = dscr("s%d_" % slot + "gcT_s", [1024, NTOK], BF16)
            gmT_s = dscr("s%d_" % slot + "gmT_s", [3 * D, NTOK], BF16)
            oT_s = [dscr("s%d_" % slot + "oaT_s", [1024, NTOK], BF16), dscr("s%d_" % slot + "obT_s", [1024, NTOK], BF16), dscr("s%d_" % slot + "ocT_s", [1024, NTOK], BF16)]
            mgT_s = dscr("s%d_" % slot + "mgT_s", [D, NTOK], BF16)
            ya_s = dscr("s%d_" % slot + "ya_s", [NTOK, 1024], F32)
            stc_s = dscr("s%d_" % slot + "stc_s", [20, 2, 128, 1024], F32)
            hpb_s = dscr("s%d_" % slot + "hpb_s", [20, 128, 1024], BF16)
            CT_s = dscr("s%d_" % slot + "CT_s", [128, 4 * NTOK], BF16)
            ea_s = dscr("s%d_" % slot + "ea_s", [128, 640], F32)
            cd_s = dscr("s%d_" % slot + "cd_s", [128, 640], F32)
            SB = {n: Buf(n) for n in ("m", "q", "k", "v", "ga", "xbc", "z", "dt", "u", "vc", "gc", "gm", "oa", "ob",
                                      "oc", "mg", "ya", "stc", "hpb", "y1", "ko", "vo", "sfo", "sbo", "yout", "ct", "ea", "cd")}
            SB["m"] = SBm

            def phase_mod(l):
                with ExitStack() as st:
                    sb = sbuf_alloc(st)
                    cvt = sb("cvt", [128, 16, 2], F32)
                    scT = sb("scT", [128, 16, 2], BF16)
                    bm = sb("bm", [2, 3 * D], F32)
                    mrow = sb("mrow", [2, 3 * D], F32)
                    wb = [sb("wm%d" % i, [128, 16, 512], BF16) for i in range(3)]
                    Bw = [Buf() for _ in range(3)]
                    Bc, Bs, Bb, Bm = Buf(), Buf(), Buf(), Buf()
                    kb.dma("sp", cvt[:], cvT[:, :, :], writes=[Bc])
                    ACT(scT[:], cvt[:], AF.Silu, [Bc], [Bs])
                    kb.dma("sp", bm[:], b_mod[l:l + 1, :].partition_broadcast(2), writes=[Bb])
                    wv = w_mod[l].rearrange("(kc p) f -> p kc f", p=128)
                    for cb in range(12):
                        w = wb[cb % 3]
                        kb.dma("pool", w[:], wv[:, :, cb * 512:(cb + 1) * 512], writes=[Bw[cb % 3]])
                        for kc in range(16):
                            MM(PS[0][0:2, :], scT[:, kc, :], w[:, kc, :], kc == 0, kc == 15, [Bs, Bw[cb % 3]], [PB[0]], kc == 15)
                        TT("dve", mrow[:, cb * 512:(cb + 1) * 512], PS[0][0:2, :], bm[:, cb * 512:(cb + 1) * 512], ALU.add,
                           [PB[0], Bb], [Bm])
                    kb.dma("sp", m_s[:, :], mrow[:], reads=[Bm], writes=[SB["m"]])

            def phase_norm(l, xsrc, Bx, hT, BhT):
                with ExitStack() as st:
                    sb = sbuf_alloc(st)
                    gp = sb("gp", [128, D], F32)
                    A = [sb("A%d" % g, [128, D], F32) for g in range(2)]
                    Bs_ = [sb("Bs%d" % g, [128, D], F32) for g in range(2)]
                    xt = [sb("xt%d" % i, [128, D], F32) for i in range(2)]
                    hx = [sb("hx%d" % i, [128, D], BF16) for i in range(2)]
                    junk = sb("junk", [128, D], BF16)
                    ss = [sb("ss%d" % i, [128, 4], F32) for i in range(2)]
                    Bgp, BA, BBs = Buf(), [Buf(), Buf()], [Buf(), Buf()]
                    Bxt, Bhx, Bj, Bss = [Buf(), Buf()], [Buf(), Buf()], Buf(), [Buf(), Buf()]
                    kb.dma("sp", gp[:], g_pre[l:l + 1, :].partition_broadcast(128), writes=[Bgp])
                    for g in range(2):
                        kb.dma("sp", A[g][:], m_s[g:g + 1, D:2 * D].partition_broadcast(128), reads=[SB["m"]], writes=[BA[g]])
                        kb.dma("sp", Bs_[g][:], m_s[g:g + 1, 0:D].partition_broadcast(128), reads=[SB["m"]], writes=[BBs[g]])
                        STT(A[g][:], A[g][:], 1.0, gp[:], ALU.add, ALU.mult, [Bgp], [BA[g]])
                    for ts in range(20):
                        g = 0 if ts < 4 else 1
                        i = ts % 2
                        kb.dma("sp", xt[i][:], xsrc[ts * 128:(ts + 1) * 128, :], reads=[Bx], writes=[Bxt[i]])
                        ACT(junk[:], xt[i][:], AF.Square, [Bxt[i]], [Bj, Bss[i]], accum_out=ss[i][:, 0:1])
                        TS("dve", ss[i][:, 1:2], ss[i][:, 0:1], 1.0 / D, EPS, ALU.mult, ALU.add, [], [Bss[i]])
                        ACT(ss[i][:, 2:3], ss[i][:, 1:2], AF.Sqrt, [], [Bss[i]])
                        kb.op("dve", lambda e: e.reciprocal(out=ss[i][:, 3:4], in_=ss[i][:, 2:3]), [], [Bss[i]])
                        STT(xt[i][:], xt[i][:], ss[i][:, 3:4], A[g][:], ALU.mult, ALU.mult, [Bss[i], BA[g]], [Bxt[i]])
                        TT("dve", hx[i][:], xt[i][:], Bs_[g][:], ALU.add, [Bxt[i], BBs[g]], [Bhx[i]])
                        for half in range(2):
                            bk = 2 * i + half
                            for j in range(8):
                                kc = half * 8 + j
                                TR(PSb(bk)[:, j * 128:(j + 1) * 128], hx[i][:, kc * 128:(kc + 1) * 128], identb,
                                   [Bhx[i], B_cst], [PB[bk]], j == 7)
                            CP("act" if half == 0 else "dve",
                               hT[:, half * 8:(half + 1) * 8, ts * 128:(ts + 1) * 128],
                               PSb(bk).rearrange("p (a b) -> p a b", a=8), [PB[bk]], [BhT[ts]])

            def phase_proj(l, hT, BhT):
                with ExitStack() as st:
                    sb = sbuf_alloc(st)
                    wb = [sb("wp%d" % i, [128, 16, 512], BF16) for i in range(3)]
                    Bw = [Buf() for _ in range(3)]
                    wdt = sb("wdt", [128, 16, 32], BF16)
                    Bwdt = Buf()
                    sfm = [sb("sfm%d" % i, [128, NTOK], BF16) for i in range(3)]
                    Bsfm = [Buf() for _ in range(3)]
                    stm = [sb("stm%d" % i, [128, 512], BF16) for i in range(4)]
                    Bstm = [Buf() for _ in range(4)]
                    s32 = [sb("s32%d" % i, [128, 512], F32) for i in range(2)]
                    Bs32 = [Buf() for _ in range(2)]
                    dts = sb("dts", [128, 20, 32], F32)
                    Bdts = Buf()
                    wv = w_in[l].rearrange("(kc p) f -> p kc f", p=128)
                    blocks = []

                    def addF(c0, n, scr, key, ev):
                        for b in range(n // 512):
                            blocks.append(("F", c0 + b * 512, scr, key, b * 512, ev))

                    def addT(c0, n, scr, key, ev):
                        for b in range(n // 512):
                            blocks.append(("T", c0 + b * 512, scr, key, b * 512, ev))
                    addF(OFF["q"], 1024, qT_s, "q", "qs")
                    addF(OFF["k"], 1024, kT_s, "k", "cp")
                    addT(OFF["v"], 1024, v_s, "v", "cp")
                    addT(OFF["ga"], 1024, ga_s, "ga", "silu")
                    addF(OFF["xbc"], 2048, xbcT_s, "xbc", "cp")
                    addT(OFF["z"], 1024, zs_s, "z", "silu")
                    addF(OFF["u"], 1024, uT_s, "u", "cp")
                    addT(OFF["vc"], 1024, vc_s, "vc", "cp")
                    addF(OFF["gc"], 1024, gcT_s, "gc", "silu")
                    addF(OFF["gm"], 6144, gmT_s, "gm", "sig")
                    nb = len(blocks)
                    if os.environ.get("KDBG_NB"):
                        blocks = blocks[:int(os.environ["KDBG_NB"])]
                        nb = len(blocks)
                    state = dict(bank=0, fm=0, tm=0, s32=0, ev=0)

                    def load(bi):
                        c0 = blocks[bi][1]
                        kb.dma("pool", wb[bi % 3][:], wv[:, :, c0:c0 + 512], writes=[Bw[bi % 3]])

                    def nbank():
                        b = state["bank"]
                        state["bank"] = (b + 1) % int(os.environ.get("KDBG_NBANK", "7"))
                        return b

                    def evac(ev, out, bank, W):
                        if ev == "silu":
                            ACT(out, PS[bank][:], AF.Silu, [PB[bank]], W)
                        elif ev == "sig":
                            ACT(out, PS[bank][:], AF.Sigmoid, [PB[bank]], W)
                        elif ev == "qs":
                            TS("dve", out, PS[bank][:], 0.125, None, ALU.mult, None, [PB[bank]], W)
                        else:
                            CP("dve", out, PS[bank][:], [PB[bank]], W)

                    if not os.environ.get("KDBG_NODT"):
                        kb.dma("pool", wdt[:], wv[:, :, OFF["dt"]:OFF["dt"] + 32], writes=[Bwdt])
                    load(0)
                    load(1)
                    for bi in range(nb):
                        lay, c0, scr, key, off, ev = blocks[bi]
                        if bi + 2 < nb:
                            load(bi + 2)
                        w = wb[bi % 3]
                        BW = Bw[bi % 3]
                        if lay == "F":
                            for fcl in range(4):
                                si = state["fm"]
                                state["fm"] = (si + 1) % 3
                                for tt in range(5):
                                    bk = nbank()
                                    for kc in range(16):
                                        MM(PS[bk][:], w[:, kc, fcl * 128:(fcl + 1) * 128], hT[:, kc, tt * 512:(tt + 1) * 512],
                                           kc == 0, kc == 15, [BW, BhT], [PB[bk]], kc == 15)
                                    evac(ev, sfm[si][:, tt * 512:(tt + 1) * 512], bk, [Bsfm[si]])
                                r0 = off + fcl * 128
                                kb.dma("sp", scr[r0:r0 + 128, :], sfm[si][:], reads=[Bsfm[si]], writes=[SB[key]])
                            if key == "k":
                                for ts in range(4):
                                    bk = nbank()
                                    for kc in range(16):
                                        MM(PS[bk][:], hT[:, kc, ts * 128:(ts + 1) * 128], w[:, kc, :], kc == 0, kc == 15,
                                           [BW, BhT], [PB[bk]], kc == 15)
                                    j = state["s32"]
                                    state["s32"] = j ^ 1
                                    CP("dve", s32[j][:], PS[bk][:], [PB[bk]], [Bs32[j]])
                                    kb.dma("sp", ko[l, ts * 128:(ts + 1) * 128, off:off + 512], s32[j][:], reads=[Bs32[j]],
                                           writes=[SB["ko"]])
                        else:
                            for ts in range(20):
                                bk = nbank()
                                for kc in range(16):
                                    MM(PS[bk][:], hT[:, kc, ts * 128:(ts + 1) * 128], w[:, kc, :], kc == 0, kc == 15,
                                       [BW, BhT], [PB[bk]], kc == 15)
                                si = state["tm"]
                                state["tm"] = (si + 1) % 4
                                if key == "v" and ts < 4:
                                    j = state["s32"]
                                    state["s32"] = j ^ 1
                                    CP("dve", s32[j][:], PS[bk][:], [PB[bk]], [Bs32[j]])
                                    kb.dma("sp", vo[l, ts * 128:(ts + 1) * 128, off:off + 512], s32[j][:], reads=[Bs32[j]],
                                           writes=[SB["vo"]])
                                evac(ev, stm[si][:], bk, [Bstm[si]])
                                kb.dma("sp", scr[ts * 128:(ts + 1) * 128, off:off + 512], stm[si][:], reads=[Bstm[si]],
                                       writes=[SB[key]])
                        if key == "z" and off == 512 and not os.environ.get("KDBG_NODT"):
                            for ts in range(20):
                                bk = nbank()
                                for kc in range(16):
                                    MM(PS[bk][:, 0:32], hT[:, kc, ts * 128:(ts + 1) * 128], wdt[:, kc, :], kc == 0, kc == 15,
                                       [Bwdt, BhT], [PB[bk]], kc == 15)
                                CP("dve", dts[:, ts, :], PS[bk][:, 0:32], [PB[bk]], [Bdts])
                            kb.dma("sp", dt_s[:, :], dts[:].rearrange("p t c -> p (t c)"), reads=[Bdts], writes=[SB["dt"]])

            def phase_attn(l):
                kTv = kT_s.rearrange("(fc p) t -> p fc t", p=128)
                qTv = qT_s.rearrange("(fc p) t -> p fc t", p=128)
                oaTv = oT_s[0].rearrange("(fc p) t -> p fc t", p=128)
                with ExitStack() as st:
                    sb = sbuf_alloc(st)
                    qt = [sb("qt%d" % i, [128, 8, 128], BF16) for i in range(2)]
                    sga = [sb("sga%d" % i, [128, 1024], BF16) for i in range(2)]
                    PT = [sb("PT%d" % i, [128, 1024], BF16) for i in range(3)]
                    oa = [sb("oa%d" % i, [128, 1024], BF16) for i in range(2)]
                    rden = [sb("rden%d" % i, [128, 4], F32) for i in range(2)]
                    oaT = sb("oaT", [128, 8, 512], BF16)
                    Bqt, Bsga = [Buf(), Buf()], [Buf(), Buf()]
                    BPT, Boa, Brd, BoaT = [Buf() for _ in range(3)], [Buf(), Buf()], [Buf(), Buf()], Buf()
                    cnt = dict(q=0, pt=0, s=0, o=0, oa=0)

                    def block(tok0, chunks_fn, nch, Rk, oslot, flush):
                        qi = cnt["q"] % 2
                        cnt["q"] += 1
                        kb.dma("sp", qt[qi][:], qTv[:, :, tok0:tok0 + 128], reads=[SB["q"]], writes=[Bqt[qi]])
                        kb.dma("sp", sga[qi][:], ga_s[tok0:tok0 + 128, :], reads=[SB["ga"]], writes=[Bsga[qi]])
                        oi = cnt["oa"] % 2
                        cnt["oa"] += 1
                        pend = {}

                        def qk(h):
                            hp, fc = (h % 2) * 64, h // 2
                            sr = cnt["s"] % 2
                            cnt["s"] += 1
                            pend[h] = sr
                            for ci in range(nch):
                                bank = 2 * sr + ci // 4
                                o_ = (ci % 4) * 128
                                kap, vap, bap = chunks_fn(h, ci)
                                lastb = (ci % 4 == 3) or (ci == nch - 1)
                                MM(PS[bank][:, o_:o_ + 128], kap, qt[qi][hp:hp + 64, fc, :], True, bap is None,
                                   Rk + [Bqt[qi]], [PB[bank]], lastb and bap is None)
                                if bap is not None:
                                    MM(PS[bank][:, o_:o_ + 128], identb, bap, False, True, Rk + [B_cst], [PB[bank]], lastb)

                        def pv(h):
                            sr = pend.pop(h)
                            pi = cnt["pt"] % 3
                            cnt["pt"] += 1
                            for b in range((nch + 3) // 4):
                                ncol = min(nch - 4 * b, 4) * 128
                                ACT(PT[pi][:, b * 512:b * 512 + ncol], PS[2 * sr + b][:, 0:ncol], AF.Exp, [PB[2 * sr + b]], [BPT[pi]])
                            hh = h % 4
                            ob = 4 + ((cnt["o"] // 4) % 2)
                            cnt["o"] += 1
                            Ov = PS[ob][:, 0:260].rearrange("p (a b) -> p a b", a=4)
                            for ci in range(nch):
                                kap, vap, bap = chunks_fn(h, ci)
                                MM(Ov[:, hh, :], PT[pi][:, ci * 128:(ci + 1) * 128], vap, ci == 0, ci == nch - 1,
                                   Rk + [BPT[pi]], [PB[ob]], ci == nch - 1)
                            if hh == 3:
                                ri = (h // 4) % 2
                                kb.op("dve", lambda e: e.reciprocal(out=rden[ri][:], in_=Ov[:, :, 64]), [PB[ob]], [Brd[ri]])
                                for k4 in range(4):
                                    h2 = h - 3 + k4
                                    STT(oa[oi][:, h2 * 64:(h2 + 1) * 64], Ov[:, k4, 0:64], rden[ri][:, k4:k4 + 1],
                                        sga[qi][:, h2 * 64:(h2 + 1) * 64], ALU.mult, ALU.mult,
                                        [PB[ob], Brd[ri], Bsga[qi]], [Boa[oi]])
                        qk(0)
                        for h in range(16):
                            if h + 1 < 16:
                                qk(h + 1)
                            pv(h)
                        for fc in range(8):
                            TR(PSb(6)[:, fc * 128:(fc + 1) * 128], oa[oi][:, fc * 128:(fc + 1) * 128], identb,
                               [Boa[oi], B_cst], [PB[6]], fc == 7)
                        CP("dve", oaT[:, :, oslot * 128:(oslot + 1) * 128], PSb(6).rearrange("p (a b) -> p a b", a=8),
                           [PB[6]], [BoaT])
                        if flush is not None:
                            kb.dma("pool", oaTv[:, :, flush:flush + 512], oaT[:], reads=[BoaT], writes=[SB["oa"]])

                    with ExitStack() as st2:
                        sb2 = sbuf_alloc(st2)
                        kTp = sb2("kTp", [128, 8, 512], BF16)
                        vp = sb2("vp", [128, 4, 16, 65], BF16)
                        Bkp, Bvp = Buf(), Buf()
                        kb.dma("sp", kTp[:], kTv[:, :, 0:512], reads=[SB["k"]], writes=[Bkp])
                        kb.op("pool", lambda e: e.memset(vp[:, :, :, 64:65], 1.0), [], [Bvp])
                        for c in range(4):
                            kb.dma("sp", vp[:, c, :, 0:64], v_s[c * 128:(c + 1) * 128, :].rearrange("p (h d) -> p h d", d=64),
                                   reads=[SB["v"]], writes=[Bvp])
                        for sq in range(2):
                            for t in range(2):
                                def cf(h, ci, sq=sq):
                                    hp, fc = (h % 2) * 64, h // 2
                                    return (kTp[hp:hp + 64, fc, sq * 256 + ci * 128: sq * 256 + (ci + 1) * 128],
                                            vp[:, sq * 2 + ci, h, :], None)
                                pslot = sq * 2 + t
                                block(sq * 256 + t * 128, cf, 2, [Bkp, Bvp], pslot, 0 if pslot == 3 else None)
                        kb.barrier()
                    with ExitStack() as st2:
                        sb2 = sbuf_alloc(st2)
                        kTs = sb2("kTs", [128, 8, 2560], BF16)
                        vs = sb2("vs", [128, 20, 16, 65], BF16)
                        kcT = sb2("kcT", [128, 8, 256], BF16)
                        vcx = sb2("vcx", [128, 2, 16, 65], BF16)
                        bint = sb2("bint", [128, 16, 768], BF16)
                        bsp = sb2("bsp", [128, 16, 768], BF16)
                        rmt = [sb2("rmt%d" % i, [128, 768], BF16) for i in range(2)]
                        Bks, Bvs, Bkc, Bvc, Bbi, Bbs, Brm = Buf(), Buf(), Buf(), Buf(), Buf(), Buf(), [Buf(), Buf()]
                        oth = REG[1 - slot]
                        okTv = oth.kT_s.rearrange("(fc p) t -> p fc t", p=128)
                        kb.op("pool", lambda e: e.memset(vs[:, 0:2, :, :], 0.0), [], [Bvs])
                        kb.op("pool", lambda e: e.memset(vs[:, 18:20, :, :], 0.0), [], [Bvs])
                        kb.op("pool", lambda e: e.memset(vs[:, :, :, 64:65], 1.0), [], [Bvs])
                        if slot == 1:
                            kb.dma("sp", kTs[:, :, 0:256], okTv[:, :, 2304:2560], reads=[oth.SB["k"]], writes=[Bks])
                            for c in range(2):
                                kb.dma("sp", vs[:, c, :, 0:64],
                                       oth.v_s[2304 + c * 128:2304 + (c + 1) * 128, :].rearrange("p (h d) -> p h d", d=64),
                                       reads=[oth.SB["v"]], writes=[Bvs])
                            kb.op("pool", lambda e: e.memset(kTs[:, :, 2304:2560], 0.0), [], [Bks])
                        else:
                            kb.op("pool", lambda e: e.memset(kTs[:, :, 0:256], 0.0), [], [Bks])
                            kb.dma("sp", kTs[:, :, 2304:2560], okTv[:, :, 512:768], reads=[oth.SB["k"]], writes=[Bks])
                            for c in range(2):
                                kb.dma("sp", vs[:, 18 + c, :, 0:64],
                                       oth.v_s[512 + c * 128:512 + (c + 1) * 128, :].rearrange("p (h d) -> p h d", d=64),
                                       reads=[oth.SB["v"]], writes=[Bvs])
                        kb.dma("sp", kTs[:, :, 256:2304], kTv[:, :, 512:2560], reads=[SB["k"]], writes=[Bks])
                        for c in range(16):
                            kb.dma("sp", vs[:, 2 + c, :, 0:64],
                                   v_s[512 + c * 128:512 + (c + 1) * 128, :].rearrange("p (h d) -> p h d", d=64),
                                   reads=[SB["v"]], writes=[Bvs])
                        kb.op("pool", lambda e: e.memset(vcx[:, :, :, 64:65], 1.0), [], [Bvc])
                        with ExitStack() as st3:
                            ckb = st3.enter_context(nc.sbuf_tensor("ckb%d_%d" % (l, slot), [128, 2, 1024], BF16))
                            Bck = Buf()
                            kb.dma("pool", ckb[:], ck[l].rearrange("(c p) f -> p c f", p=128), writes=[Bck])
                            for c in range(2):
                                kb.dma("pool", vcx[:, c, :, 0:64], cv[l, c * 128:(c + 1) * 128, :].rearrange("p (h d) -> p h d", d=64),
                                       writes=[Bvc])
                                for fc in range(8):
                                    TR(PSb(6)[:, fc * 128:(fc + 1) * 128], ckb[:, c, fc * 128:(fc + 1) * 128], identb,
                                       [Bck, B_cst], [PB[6]], fc == 7)
                                CP("dve", kcT[:, :, c * 128:(c + 1) * 128], PSb(6).rearrange("p (a b) -> p a b", a=8),
                                   [PB[6]], [Bkc])
                            kb.barrier()

                        def load_bias(dst, Bdst, kind, ridx, j):
                            kb.dma("pool", dst[:].rearrange("p a b -> p (a b)"), Fb[l, kind], writes=[Bdst])
                            kb.dma("pool", rmt[j][:], rm[ridx], writes=[Brm[j]])
                            TT("dve", dst[:], dst[:], rmt[j][:].unsqueeze(1).broadcast_to([128, 16, 768]), ALU.add,
                               [Brm[j]], [Bdst])
                        load_bias(bint, Bbi, 0, 0, 0)
                        rmj = 1
                        for ml in range(16):
                            special = ml in (0, 1, 14, 15)
                            if special:
                                load_bias(bsp, Bbs, 1 if ml == 15 else 0, {0: 1, 1: 2, 14: 3, 15: 4}[ml], rmj)
                                rmj ^= 1
                            bt, Bbt = (bsp, Bbs) if special else (bint, Bbi)
                            cb0 = 14 if ml == 15 else ml
                            nloc = 6 if ml in (0, 15) else 5

                            def cf(h, ci, cb0=cb0, nloc=nloc, bt=bt):
                                hp, fc = (h % 2) * 64, h // 2
                                if ci < nloc:
                                    c = cb0 + ci
                                    return (kTs[hp:hp + 64, fc, c * 128:(c + 1) * 128], vs[:, c, h, :],
                                            bt[:, h, ci * 128:(ci + 1) * 128])
                                c = ci - nloc
                                return (kcT[hp:hp + 64, fc, c * 128:(c + 1) * 128], vcx[:, c, h, :], None)
                            block(512 + ml * 128, cf, nloc + 2, [Bks, Bvs, Bkc, Bvc, Bbt], ml % 4,
                                  512 + (ml - 3) * 128 if ml % 4 == 3 else None)
                        kb.barrier()

            def phase_sgu(l):
                uTv = uT_s.rearrange("(g e) t -> e g t", e=128)
                gcTv = gcT_s.rearrange("(g e) t -> e g t", e=128)
                ocTv = oT_s[2].rearrange("(g e) t -> e g t", e=128)
                with ExitStack() as st:
                    sb = sbuf_alloc(st)
                    wst = sb("wst", [128, 8, 128], BF16)
                    bsb = sb("bsb", [128, 8, 128], F32)
                    gsg = sb("gsg", [128, 8], F32)
                    Bw, Bb, Bg = Buf(), Buf(), Buf()
                    kb.dma("pool", wst[:], wsT[l], writes=[Bw])
                    kb.dma("sp", bsb[:].rearrange("p a b -> p (a b)"), b_s[l:l + 1, :].partition_broadcast(128), writes=[Bb])
                    kb.dma("sp", gsg[:], g_sguT[l], writes=[Bg])
                    vct = [sb("vct%d" % i, [128, 1024], BF16) for i in range(2)]
                    vn = [sb("vn%d" % i, [128, 1024], BF16) for i in range(2)]
                    stt = [sb("stt%d" % i, [128, 16], F32) for i in range(2)]
                    ut = [sb("ut%d" % i, [128, 8, 512], BF16) for i in range(2)]
                    gt = [sb("gt%d" % i, [128, 8, 512], BF16) for i in range(2)]
                    oc = [sb("oc%d" % i, [128, 8, 512], BF16) for i in range(2)]
                    tmp = [sb("tmpg%d" % i, [128, 8, 128], F32) for i in range(2)]
                    Bvct, Bvn, Bstt = [Buf(), Buf()], [Buf(), Buf()], [Buf(), Buf()]
                    But, Bgt, Boc, Btmp = [Buf(), Buf()], [Buf(), Buf()], [Buf(), Buf()], [Buf(), Buf()]
                    for tt in range(5):
                        ti = tt % 2
                        kb.dma("sp", ut[ti][:], uTv[:, :, tt * 512:(tt + 1) * 512], reads=[SB["u"]], writes=[But[ti]])
                        kb.dma("sp", gt[ti][:], gcTv[:, :, tt * 512:(tt + 1) * 512], reads=[SB["gc"]], writes=[Bgt[ti]])
                        TT("pool", ut[ti][:], ut[ti][:], gt[ti][:], ALU.mult, [Bgt[ti]], [But[ti]])
                        for sc in range(4):
                            c = tt * 4 + sc
                            i = c % 2
                            kb.dma("sp", vct[i][:], vc_s[c * 128:(c + 1) * 128, :], reads=[SB["vc"]], writes=[Bvct[i]])
                            for hf in range(2):
                                kb.op("dve", lambda e: e.bn_stats(out=stt[i][:, hf * 6:(hf + 1) * 6], in_=vct[i][:, hf * 512:(hf + 1) * 512]),
                                      [Bvct[i]], [Bstt[i]])
                            kb.op("dve", lambda e: e.bn_aggr(out=stt[i][:, 12:14], in_=stt[i][:, 0:12]), [], [Bstt[i]])
                            TS("dve", stt[i][:, 14:15], stt[i][:, 13:14], EPS, None, ALU.add, None, [], [Bstt[i]])
                            ACT(stt[i][:, 14:15], stt[i][:, 14:15], AF.Sqrt, [], [Bstt[i]])
                            kb.op("dve", lambda e: e.reciprocal(out=stt[i][:, 15:16], in_=stt[i][:, 14:15]), [], [Bstt[i]])
                            TS("dve", vn[i][:], vct[i][:], stt[i][:, 12:13], stt[i][:, 15:16], ALU.subtract, ALU.mult,
                               [Bvct[i], Bstt[i]], [Bvn[i]])
                            bks = (0, 1) if c % 2 == 0 else (2, 3)
                            for g in range(8):
                                bk = bks[g // 4]
                                MM(PS[bk][:, (g % 4) * 128:(g % 4 + 1) * 128], vn[i][:, g * 128:(g + 1) * 128], wst[:, g, :],
                                   True, True, [Bvn[i], Bw], [PB[bk]], g % 4 == 3)
                            for g in range(8):
                                bk = bks[g // 4]
                                STT(tmp[i][:, g, :], PS[bk][:, (g % 4) * 128:(g % 4 + 1) * 128], gsg[:, g:g + 1], bsb[:, g, :],
                                    ALU.mult, ALU.add, [PB[bk], Bg, Bb], [Btmp[i]])
                            TT("dve", oc[ti][:, :, sc * 128:(sc + 1) * 128], tmp[i][:], ut[ti][:, :, sc * 128:(sc + 1) * 128], ALU.mult,
                               [Btmp[i], But[ti]], [Boc[ti]])
                        kb.dma("pool", ocTv[:, :, tt * 512:(tt + 1) * 512], oc[ti][:], reads=[Boc[ti]], writes=[SB["oc"]])

            def phase_o1(l):
                gmv = gmT_s.rearrange("(br fc p) t -> p br fc t", p=128, br=3)
                mgv = mgT_s.rearrange("(fc p) t -> p fc t", p=128)
                with ExitStack() as st:
                    sb = sbuf_alloc(st)
                    wbr = [sb("wbr%d" % b, [128, 8, D], BF16) for b in range(3)]
                    Bwbr = [Buf() for _ in range(3)]
                    for b in range(3):
                        kb.dma("pool", wbr[b][:], w_br[b][l].rearrange("(kc p) f -> p kc f", p=128), writes=[Bwbr[b]])
                    ot = [[sb("ot%d_%d" % (b, i), [128, 8, 512], BF16) for b in range(3)] for i in range(2)]
                    Bot = [[Buf() for _ in range(3)] for _ in range(2)]
                    gmt = [sb("gmt%d" % i, [128, 3, 512], BF16) for i in range(3)]
                    Bgm = [Buf() for _ in range(3)]
                    acc = [sb("acc%d" % i, [128, 512], F32) for i in range(2)]
                    Bacc = [Buf(), Buf()]
                    mg = [sb("mg%d" % i, [128, 16, 512], BF16) for i in range(2)]
                    Bmg = [Buf(), Buf()]
                    nbk = 0
                    for tt in range(5):
                        ti = tt % 2
                        for b in range(3):
                            kb.dma("sp", ot[ti][b][:], oT_s[b].rearrange("(kc p) t -> p kc t", p=128)[:, :, tt * 512:(tt + 1) * 512],
                                   reads=[SB[("oa", "ob", "oc")[b]]], writes=[Bot[ti][b]])
                        for fc in range(16):
                            gi = (tt * 16 + fc) % 3
                            ai = fc % 2
                            kb.dma("sp", gmt[gi][:], gmv[:, :, fc, tt * 512:(tt + 1) * 512], reads=[SB["gm"]], writes=[Bgm[gi]])
                            for b in range(3):
                                bk = nbk
                                nbk = (nbk + 1) % 7
                                for kc in range(8):
                                    MM(PS[bk][:], wbr[b][:, kc, fc * 128:(fc + 1) * 128], ot[ti][b][:, kc, :], kc == 0, kc == 7,
                                       [Bwbr[b], Bot[ti][b]], [PB[bk]], kc == 7)
                                if b == 0:
                                    TT("dve", acc[ai][:], PS[bk][:], gmt[gi][:, 0, :], ALU.mult, [PB[bk], Bgm[gi]], [Bacc[ai]])
                                elif b == 1:
                                    t2 = TT("dve", PS[bk][:], PS[bk][:], gmt[gi][:, 1, :], ALU.mult, [Bgm[gi]], [PB[bk]])
                                    TT("dve", acc[ai][:], acc[ai][:], PS[bk][:], ALU.add, [PB[bk]], [Bacc[ai]])
                                else:
                                    TT("dve", PS[bk][:], PS[bk][:], gmt[gi][:, 2, :], ALU.mult, [Bgm[gi]], [PB[bk]])
                                    TT("dve", mg[ti][:, fc, :], acc[ai][:], PS[bk][:], ALU.add, [PB[bk], Bacc[ai]], [Bmg[ti]])
                        kb.dma("pool", mgv[:, :, tt * 512:(tt + 1) * 512], mg[ti][:], reads=[Bmg[ti]], writes=[SB["mg"]])

            def phase_o2(l, xsrc, Bx, ydst, By):
                mgv = mgT_s.rearrange("(kc p) t -> p kc t", p=128)
                with ExitStack() as st:
                    sb = sbuf_alloc(st)
                    wo = sb("wo", [128, 16, D], BF16)
                    Bwo = Buf()
                    kb.dma("pool", wo[:], w_out[l].rearrange("(kc p) f -> p kc f", p=128), writes=[Bwo])
                    gpo = sb("gpo", [128, D], F32)
                    G = [sb("G%d" % g, [128, D], F32) for g in range(2)]
                    Bgpo, BG = Buf(), [Buf(), Buf()]
                    kb.dma("sp", gpo[:], g_post[l:l + 1, :].partition_broadcast(128), writes=[Bgpo])
                    for g in range(2):
                        kb.dma("sp", G[g][:], m_s[g:g + 1, 2 * D:3 * D].partition_broadcast(128), reads=[SB["m"]], writes=[BG[g]])
                        TT("dve", G[g][:], G[g][:], gpo[:], ALU.mult, [Bgpo], [BG[g]])
                    mgt = [sb("mgt%d" % i, [128, 16, 128], BF16) for i in range(2)]
                    xt = [sb("xo%d" % i, [128, D], F32) for i in range(2)]
                    zt = [sb("zt%d" % i, [128, D], F32) for i in range(2)]
                    junk = sb("junk2", [128, 512], BF16)
                    ss = [sb("sso%d" % i, [128, 8], F32) for i in range(2)]
                    Bmgt, Bxt, Bzt, Bj, Bss = [Buf(), Buf()], [Buf(), Buf()], [Buf(), Buf()], Buf(), [Buf(), Buf()]
                    for ts in range(20):
                        i = ts % 2
                        g = 0 if ts < 4 else 1
                        kb.dma("sp", mgt[i][:], mgv[:, :, ts * 128:(ts + 1) * 128], reads=[SB["mg"]], writes=[Bmgt[i]])
                        kb.dma("sp", xt[i][:], xsrc[ts * 128:(ts + 1) * 128, :], reads=[Bx], writes=[Bxt[i]])
                        bks = (0, 1, 2, 3) if i == 0 else (4, 5, 6, 0)
                        for cb in range(4):
                            bk = bks[cb]
                            for kc in range(16):
                                MM(PS[bk][:], mgt[i][:, kc, :], wo[:, kc, cb * 512:(cb + 1) * 512], kc == 0, kc == 15,
                                   [Bmgt[i], Bwo], [PB[bk]], kc == 15)
                            CP("dve", zt[i][:, cb * 512:(cb + 1) * 512], PS[bk][:], [PB[bk]], [Bzt[i]])
                            ACT(junk[:], zt[i][:, cb * 512:(cb + 1) * 512], AF.Square, [Bzt[i]], [Bj, Bss[i]], accum_out=ss[i][:, cb:cb + 1])
                        kb.op("dve", lambda e: e.tensor_reduce(out=ss[i][:, 4:5], in_=ss[i][:, 0:4], axis=AX.X, op=ALU.add), [], [Bss[i]])
                        TS("dve", ss[i][:, 5:6], ss[i][:, 4:5], 1.0 / D, EPS, ALU.mult, ALU.add, [], [Bss[i]])
                        ACT(ss[i][:, 6:7], ss[i][:, 5:6], AF.Sqrt, [], [Bss[i]])
                        kb.op("dve", lambda e: e.reciprocal(out=ss[i][:, 7:8], in_=ss[i][:, 6:7]), [], [Bss[i]])
                        STT(zt[i][:], zt[i][:], ss[i][:, 7:8], G[g][:], ALU.mult, ALU.mult, [Bss[i], BG[g]], [Bzt[i]])
                        TT("dve", zt[i][:], zt[i][:], xt[i][:], ALU.add, [Bxt[i]], [Bzt[i]])
                        kb.dma("pool", ydst[ts * 128:(ts + 1) * 128, :], zt[i][:], reads=[Bzt[i]], writes=[By])

            def phase_ssm(l):
                xbv = xbcT_s.rearrange("(fc p) t -> p fc t", p=128)
                obTv = oT_s[1].rearrange("(fc p) t -> p fc t", p=128)
                triu_b, tril_b, ones_b = cstb[:, 1, :], cstb[:, 2, :], cstb[:, 5, :]
                sgt_f, slt_f = cst[:, 3, :], cst[:, 4, :]
                one_col = cst[:, 5, 0:1]
                with ExitStack() as st:
                    sb = sbuf_alloc(st)
                    cw = sb("cw", [128, 16, 3], F32)
                    cbs = sb("cbs", [128, 16], F32)
                    dtb = sb("dtb", [128, 32], F32)
                    abc = sb("abc", [128, 32], F32)
                    dsk = sb("dsk", [128, 16], F32)
                    gss = sb("gss", [128, 1024], F32)
                    CT = sb("CTall", [128, 4, NTOK], BF16)
                    ea = sb("ea", [128, 20, 32], F32)
                    cd = sb("cd", [128, 20, 32], F32)
                    Bcw, Bdtb, Babc, Bdsk, Bgss, BCT, Bea, Bcd = (Buf() for _ in range(8))
                    kb.dma("sp", cw[:], cwT[l], writes=[Bcw])
                    kb.dma("sp", cbs[:], cbT[l], writes=[Bcw])
                    kb.dma("sp", dtb[:], dt_bias[l:l + 1, :].partition_broadcast(128), writes=[Bdtb])
                    kb.dma("sp", abc[:], a_log[l:l + 1, :].partition_broadcast(128), writes=[Babc])
                    ACT(abc[:], abc[:], AF.Exp, [], [Babc])
                    TS("dve", abc[:], abc[:], -1.0, None, ALU.mult, None, [], [Babc])
                    kb.dma("sp", dsk[:], d_skip[l:l + 1, :].partition_broadcast(128), writes=[Bdsk])
                    dtall = sb("dtall", [128, 20, 32], F32)
                    Bdtall = Buf()
                    kb.dma("sp", dtall[:].rearrange("p a b -> p (a b)"), dt_s[:, :], reads=[SB["dt"]], writes=[Bdtall])
                    TT("dve", dtall[:], dtall[:], dtb[:].unsqueeze(1).broadcast_to([128, 20, 32]), ALU.add, [Bdtb], [Bdtall])
                    ACT(dtall[:], dtall[:], AF.Exp, [], [Bdtall])
                    ACT(dtall[:], dtall[:], AF.Ln, [B_cst], [Bdtall], bias=one_col)
                    dt2 = sb("dt2", [128, 2, 20, 16], F32)
                    adt2 = sb("adt2", [128, 2, 20, 16], F32)
                    abk = sb("abk", [128, 2, 20, 16], F32)
                    ahi = sb("ahi", [128, 2, 20, 16], BF16)
                    alo = sb("alo", [128, 2, 20, 16], BF16)
                    ea2 = sb("ea2", [128, 2, 20, 16], F32)
                    cd2 = sb("cd2", [128, 2, 20, 16], F32)
                    w2 = sb("w2", [128, 2, 20, 16], F32)
                    Bsc = Buf()
                    for d in range(2):
                        CP("dve", dt2[:, d], dtall[:, :, d * 16:(d + 1) * 16], [Bdtall], [Bsc])
                        TT("dve", adt2[:, d], dt2[:, d], abc[:, d * 16:(d + 1) * 16].unsqueeze(1).broadcast_to([128, 20, 16]), ALU.mult,
                           [Babc], [Bsc])
                    CP("dve", ahi[:], adt2[:], [], [Bsc])
                    CP("dve", abk[:], ahi[:], [], [Bsc])
                    TT("dve", alo[:], adt2[:], abk[:], ALU.subtract, [], [Bsc])
                    for d in range(2):
                        V = triu_b if d == 0 else tril_b
                        for x, a_ in enumerate((ahi, alo)):
                            MM(PS[d][:, 0:320], V, a_[:, d].rearrange("p c h -> p (c h)"), x == 0, x == 1, [Bsc, B_cst], [PB[d]], x == 1)
                            MM(PS[2 + d][:, 0:320], ones_b, a_[:, d].rearrange("p c h -> p (c h)"), x == 0, x == 1, [Bsc, B_cst], [PB[2 + d]], x == 1)
                    for d in range(2):
                        f2 = lambda t: t[:, d].rearrange("p c h -> p (c h)")
                        ACT(f2(ea2), PS[d][:, 0:320], AF.Exp, [PB[d]], [Bsc])
                        ACT(f2(cd2), PS[2 + d][:, 0:320], AF.Exp, [PB[2 + d]], [Bsc])
                        CP("dve", f2(abk), PS[d][:, 0:320], [PB[d]], [Bsc])
                        TT("dve", f2(abk), PS[2 + d][:, 0:320], f2(abk), ALU.subtract, [PB[2 + d]], [Bsc])
                    ACT(abk[:], abk[:], AF.Exp, [], [Bsc])
                    TT("dve", w2[:], abk[:], dt2[:], ALU.mult, [], [Bsc])
                    Bea = Bsc
                    Bcd = Bsc
                    kb.dma("sp", gss[:], g_ssm[l:l + 1, :].partition_broadcast(128), writes=[Bgss])
                    SSTOP = int(os.environ.get("KDBG_SSTOP", "99"))
                    PBC = os.environ.get("KDBG_PBC", "dve")
                    if SSTOP <= 1:
                        kb.barrier()
                        return
                    with ExitStack() as st1:
                        sb1 = sbuf_alloc(st1)
                        xin = sb1("xin", [128, 16, 514], BF16)
                        hst = sb1("hst", [128, 16, 64], BF16)
                        Bhst = Buf()
                        xc = sb1("xc", [128, 12, 512], BF16)
                        acc = [sb1("cacc%d" % i, [128, 512], F32) for i in range(2)]
                        xs_tok = [sb1("xstok%d" % i, [128, 1024], BF16) for i in range(2)]
                        B_tok = [sb1("Btok%d" % i, [128, 4, 128], BF16) for i in range(2)]
                        sm = [sb1("sm%d" % i, [128, 8, 32], F32) for i in range(2)]
                        ahl = [sb1("ahl%d" % i, [128, 2, 32], BF16) for i in range(2)]
                        xdt = [sb1("xdt%d" % i, [128, 2, 1024], BF16) for i in range(2)]
                        xdd = [sb1("xdd%d" % i, [128, 2, 1024], BF16) for i in range(2)]
                        xsk = [sb1("xsk%d" % i, [128, 1024], F32) for i in range(2)]
                        CBm = [sb1("CBm%d" % i, [128, 2, 4, 128], BF16) for i in range(2)]
                        U = [sb1("U%d" % i, [128, 16, 128], BF16) for i in range(8)]
                        Lt = [sb1("Lt%d" % i, [128, 4, 128], BF16) for i in range(4)]
                        Mt = [sb1("Mt%d" % i, [128, 4, 128], BF16) for i in range(4)]
                        yas = [sb1("yas%d" % i, [128, 1024], F32) for i in range(2)]
                        sts = [sb1("sts%d" % i, [128, 1024], F32) for i in range(2)]
                        Bxin, Bxc = Buf(), Buf()
                        Bacc = [Buf(), Buf()]
                        Bxs, BBt, Bsm, Bahl = [Buf(), Buf()], [Buf(), Buf()], [Buf(), Buf()], [Buf(), Buf()]
                        Bxdt, Bxdd, Bxsk, BCBm = [Buf(), Buf()], [Buf(), Buf()], [Buf(), Buf()], [Buf(), Buf()]
                        BU = [Buf() for _ in range(8)]
                        BLt = [Buf() for _ in range(4)]
                        BMt = [Buf() for _ in range(4)]
                        Byas, Bsts = [Buf(), Buf()], [Buf(), Buf()]
                        kctr = dict(lm=0)
                        C1STOP = int(os.environ.get("KDBG_C1STOP", "99"))

                        def conv_tile(t0, n, left, right, lsrc=None, rsrc=None):
                            oth = REG[1 - slot]
                            if not left:
                                if lsrc is None:
                                    kb.op("pool", lambda e: e.memset(xin[:, :, 0:1], 0.0), [], [Bxin])
                                else:
                                    kb.dma("sp", hst[:], lsrc[0], reads=[oth.SB["xbc"]], writes=[Bhst])
                                    CP("dve", xin[:, :, 0:1], hst[:, :, lsrc[1]:lsrc[1] + 1], [Bhst], [Bxin])
                            if not right:
                                if rsrc is None:
                                    kb.op("pool", lambda e: e.memset(xin[:, :, n + 1:n + 2], 0.0), [], [Bxin])
                                else:
                                    kb.dma("sp", hst[:], rsrc[0], reads=[oth.SB["xbc"]], writes=[Bhst])
                                    CP("dve", xin[:, :, n + 1:n + 2], hst[:, :, rsrc[1]:rsrc[1] + 1], [Bhst], [Bxin])
                            a0, a1 = (0 if left else 1), (n + 2 if right else n + 1)
                            kb.dma("sp", xin[:, :, a0:a1], xbv[:, :, t0 - 1 + a0:t0 - 1 + a1], reads=[SB["xbc"]], writes=[Bxin])
                            CV = int(os.environ.get("KDBG_CV", "9"))
                            for fc in range(16 if CV > 1 else 0):
                                a = acc[fc % 2]
                                Ba = Bacc[fc % 2]
                                TS("dve", a[:, 0:n], xin[:, fc, 0:n], cw[:, fc, 0:1], cbs[:, fc:fc + 1], ALU.mult, ALU.add,
                                   [Bxin, Bcw], [Ba])
                                STT(a[:, 0:n], xin[:, fc, 1:n + 1], cw[:, fc, 1:2], a[:, 0:n], ALU.mult, ALU.add, [Bxin], [Ba])
                                STT(a[:, 0:n], xin[:, fc, 2:n + 2], cw[:, fc, 2:3], a[:, 0:n], ALU.mult, ALU.add, [Bxin], [Ba])
                                if CV <= 2:
                                    continue
                                if fc < 12:
                                    ACT(xc[:, fc, 0:n], a[:, 0:n], AF.Silu, [Ba], [Bxc])
                                else:
                                    ACT(CT[:, fc - 12, t0:t0 + n], a[:, 0:n], AF.Silu, [Ba], [BCT])

                        def chunk1(c, off):
                            if os.environ.get("KDBG_CBAR"):
                                kb.barrier()
                            i = c % 2
                            tok = c * 128
                            for fc in range(8):
                                TR(PSb(6)[:, fc * 128:(fc + 1) * 128], xc[:, fc, off:off + 128], identb, [Bxc, B_cst], [PB[6]], fc == 7)
                            CP("dve", xs_tok[i][:], PSb(6), [PB[6]], [Bxs[i]])
                            for g in range(4):
                                TR(PSb(6)[:, g * 128:(g + 1) * 128], xc[:, 8 + g, off:off + 128], identb, [Bxc, B_cst], [PB[6]], g == 3)
                            CP("dve", B_tok[i][:].rearrange("p a b -> p (a b)"), PSb(6)[:, 0:512], [PB[6]], [BBt[i]])
                            if c >= 1 and C1STOP <= 1:
                                return
                            if c >= 1 and C1STOP <= 3:
                                return
                            xv = xs_tok[i][:].rearrange("p (h q) -> p h q", q=64)
                            for d in range(2):
                                TT("dve", xdt[i][:, d, :].rearrange("p (h q) -> p h q", q=64), xv,
                                   dt2[:, d, c, :].unsqueeze(2).broadcast_to([128, 16, 64]), ALU.mult,
                                   [Bxs[i], Bsc], [Bxdt[i]])
                                TT(PBC, xdd[i][:, d, :].rearrange("p (h q) -> p h q", q=64), xv,
                                   w2[:, d, c, :].unsqueeze(2).broadcast_to([128, 16, 64]), ALU.mult,
                                   [Bxs[i], Bsc], [Bxdd[i]])
                            TT(PBC, xsk[i][:].rearrange("p (h q) -> p h q", q=64), xv,
                               dsk[:].unsqueeze(2).broadcast_to([128, 16, 64]), ALU.mult, [Bxs[i], Bdsk], [Bxsk[i]])
                            if c >= 1 and C1STOP <= 4:
                                return
                            for g in range(4):
                                MM(PS[4][:, g * 128:(g + 1) * 128], xc[:, 8 + g, off:off + 128], CT[:, g, tok:tok + 128], True, True,
                                   [Bxc, BCT], [PB[4]], g == 3)
                            P4 = PS[4][:].rearrange("p (a b) -> p a b", a=4)
                            TT("dve", CBm[i][:, 0], P4, triu_b.unsqueeze(1).broadcast_to([128, 4, 128]), ALU.mult, [PB[4], B_cst], [BCBm[i]])
                            TT("dve", CBm[i][:, 1], P4, tril_b.unsqueeze(1).broadcast_to([128, 4, 128]), ALU.mult, [PB[4], B_cst], [BCBm[i]])
                            if c >= 1 and C1STOP <= 5:
                                return
                            for d in range(2):
                                mk = sgt_f if d == 0 else slt_f
                                for x in range(2):
                                    TT(PBC if x == 0 else "dve", U[i * 4 + d * 2 + x][:], mk.unsqueeze(1).broadcast_to([128, 16, 128]),
                                       (ahi, alo)[x][:, d, c, :].unsqueeze(2).broadcast_to([128, 16, 128]), ALU.mult,
                                       [Bsc, B_cst], [BU[i * 4 + d * 2 + x]])
                            if c >= 1 and C1STOP <= 6:
                                return
                            for g in range(4):
                                lm = []
                                for d in range(2):
                                    bk = 2 + d
                                    V = triu_b if d == 0 else tril_b
                                    for hh in range(4):
                                        h = 4 * g + hh
                                        MM(PS[bk][:, hh * 128:(hh + 1) * 128], U[i * 4 + d * 2][:, h, :], V, True, False,
                                           [BU[i * 4 + d * 2], B_cst], [PB[bk]], False)
                                        MM(PS[bk][:, hh * 128:(hh + 1) * 128], U[i * 4 + d * 2 + 1][:, h, :], V, False, True,
                                           [BU[i * 4 + d * 2 + 1], B_cst], [PB[bk]], hh == 3)
                                    k = kctr["lm"] % 4
                                    kctr["lm"] += 1
                                    ACT(Lt[k][:].rearrange("p a b -> p (a b)"), PS[bk][:], AF.Exp, [PB[bk]], [BLt[k]])
                                    TT("dve", Mt[k][:], Lt[k][:], CBm[i][:, d, g, :].unsqueeze(1).broadcast_to([128, 4, 128]), ALU.mult,
                                       [BLt[k], BCBm[i]], [BMt[k]])
                                    lm.append(k)
                                for hh in range(4):
                                    h = 4 * g + hh
                                    yb = 0 if h < 8 else 1
                                    yo = (h % 8) * 64
                                    for d in range(2):
                                        MM(PS[yb][:, yo:yo + 64], Mt[lm[d]][:, hh, :], xdt[i][:, d, h * 64:(h + 1) * 64], d == 0, d == 1,
                                           [BMt[lm[d]], Bxdt[i]], [PB[yb]], (d == 1 and hh == 3))
                            if c >= 1 and C1STOP <= 7:
                                return
                            for yb in range(2):
                                TT("dve", yas[i][:, yb * 512:(yb + 1) * 512], PS[yb][:], xsk[i][:, yb * 512:(yb + 1) * 512], ALU.add,
                                   [PB[yb], Bxsk[i]], [Byas[i]])
                            kb.dma("pool", ya_s[tok:tok + 128, :], yas[i][:], reads=[Byas[i]], writes=[SB["ya"]])
                            if c >= 1 and C1STOP <= 8:
                                return
                            for d in range(2):
                                for gp in range(2):
                                    bk = 4 + gp
                                    for gg in range(2):
                                        g = 2 * gp + gg
                                        MM(PS[bk][:, gg * 256:(gg + 1) * 256], B_tok[i][:, g, :], xdd[i][:, d, g * 256:(g + 1) * 256],
                                           True, True, [BBt[i], Bxdd[i]], [PB[bk]], gg == 1)
                                    CP("dve", sts[d][:, gp * 512:(gp + 1) * 512], PS[bk][:], [PB[bk]], [Bsts[d]])
                                kb.dma("pool", stc_s[c, d], sts[d][:], reads=[Bsts[d]], writes=[SB["stc"]])

                        for sq in range(2):
                            conv_tile(sq * 256, 256, False, False)
                            if SSTOP <= 2:
                                break
                            for cc in range(2):
                                chunk1(sq * 2 + cc, cc * 128)
                                if SSTOP <= 3:
                                    break
                            if SSTOP <= 3:
                                break
                        NCH = int(os.environ.get("KDBG_NCH", "99"))
                        for tt in range(1, 5 if SSTOP > 3 else 0):
                            if (tt - 1) * 4 >= NCH:
                                break
                            oxbv = REG[1 - slot].xbcT_s.rearrange("(fc p) t -> p fc t", p=128)
                            conv_tile(tt * 512, 512, tt > 1, tt < 4,
                                      lsrc=(oxbv[:, :, 2496:2560], 63) if (slot == 1 and tt == 1) else None,
                                      rsrc=(oxbv[:, :, 512:576], 0) if (slot == 0 and tt == 4) else None)
                            for cc in range(4):
                                if (tt - 1) * 4 + cc >= NCH:
                                    break
                                chunk1(tt * 4 + cc, cc * 128)
                    kb.dma("sp", CT_s[:, :], CT[:].rearrange("p g t -> p (g t)"), reads=[BCT], writes=[SB["ct"]])
                    kb.dma("sp", ea_s[:, :], ea2[:].rearrange("p d c h -> p (d c h)"), reads=[Bsc], writes=[SB["ea"]])
                    kb.dma("sp", cd_s[:, :], cd2[:].rearrange("p d c h -> p (d c h)"), reads=[Bsc], writes=[SB["cd"]])
                kb.barrier()

            ns = types.SimpleNamespace(**{k: v for k, v in locals().items() if not k.startswith("_")})
            REG[slot] = ns
            return ns

        make_slot(0)
        make_slot(1)

        def ssm_pass2(l):
            triu_b = cstb[:, 1, :]
            with ExitStack() as st2:
                sb2 = sbuf_alloc(st2)
                gss = sb2("gss2", [128, 1024], F32)
                Bgss = Buf()
                kb.dma("sp", gss[:], g_ssm[l:l + 1, :].partition_broadcast(128), writes=[Bgss])
                ea2 = [sb2("ea2_%d" % s_, [128, 2, 20, 16], F32) for s_ in range(2)]
                cd2 = [sb2("cd2_%d" % s_, [128, 2, 20, 16], F32) for s_ in range(2)]
                Bea, Bcd = Buf(), Buf()
                for s_ in range(2):
                    kb.dma("sp", ea2[s_][:].rearrange("p d c h -> p (d c h)"), REG[s_].ea_s[:, :], reads=[REG[s_].SB["ea"]], writes=[Bea])
                    kb.dma("sp", cd2[s_][:].rearrange("p d c h -> p (d c h)"), REG[s_].cd_s[:, :], reads=[REG[s_].SB["cd"]], writes=[Bcd])
                H = [sb2("H%d" % d, [128, 1024], F32) for d in range(2)]
                BH = [Buf(), Buf()]
                stl = [sb2("stl%d" % i, [128, 1024], F32) for i in range(2)]
                Bstl = [Buf(), Buf()]
                hpb = [sb2("hpb%d" % i, [128, 1024], BF16) for i in range(2)]
                Bhpb = [Buf(), Buf()]
                hpf = [sb2("hpf%d" % i, [128, 1024], BF16) for i in range(2)]
                Bhpf = [Buf(), Buf()]
                ctl = [sb2("ctl%d" % i, [128, 4, 128], BF16) for i in range(2)]
                Bctl = [Buf(), Buf()]
                s0 = sb2("s0", [128, 8, 128], F32)
                Bs0 = Buf()
                fin = sb2("fin", [128, 8, 128], F32)
                Bfin = Buf()
                yal = [sb2("yal%d" % i, [128, 1024], F32) for i in range(2)]
                Byal = [Buf(), Buf()]
                zsl = [sb2("zsl%d" % i, [128, 1024], BF16) for i in range(2)]
                Bzsl = [Buf(), Buf()]
                t2 = [sb2("t2_%d" % i, [128, 1024], F32) for i in range(2)]
                Bt2 = [Buf(), Buf()]
                junk = sb2("junk3", [128, 1024], BF16)
                Bj = Buf()
                ssq = [sb2("ssq%d" % i, [128, 4], F32) for i in range(2)]
                Bssq = [Buf(), Buf()]
                obt = [sb2("obt%d" % i, [128, 1024], BF16) for i in range(2)]
                Bobt = [Buf(), Buf()]
                obT = sb2("obTst", [128, 8, 512], BF16)
                BobT = Buf()
                Bout = Buf()
                cn = dict(st=0, hp=0, f=0)

                def init_state(d, sample):
                    if not sample:
                        kb.op("dve", lambda e: e.memset(H[d][:], 0.0), [], [BH[d]])
                        return
                    kb.dma("sp", s0[:], st0[l, d].rearrange("(a p) n -> p a n", p=128), writes=[Bs0])
                    for half in range(2):
                        bk = 4 + half
                        for a in range(4):
                            TR(PS[bk][:, a * 128:(a + 1) * 128], s0[:, half * 4 + a, :], identf, [Bs0, B_cst], [PB[bk]], a == 3)
                        CP("dve", H[d][:, half * 512:(half + 1) * 512], PS[bk][:], [PB[bk]], [BH[d]])

                def prefetch_state(d, sl_, c):
                    S_ = REG[sl_]
                    j = cn["st"] % 2
                    kb.dma("sp", stl[j][:], S_.stc_s[c, d], reads=[S_.SB["stc"]], writes=[Bstl[j]])

                def step_state(d, sl_, c):
                    S_ = REG[sl_]
                    j = cn["st"] % 2
                    cn["st"] += 1
                    Hv = H[d][:].rearrange("p (h q) -> p h q", q=64)
                    TT("dve", Hv, Hv, cd2[sl_][:, d, c, :].unsqueeze(2).broadcast_to([128, 16, 64]), ALU.mult, [Bcd], [BH[d]])
                    TT("dve", H[d][:], H[d][:], stl[j][:], ALU.add, [Bstl[j]], [BH[d]])

                def emit_state(d, dst):
                    for half in range(2):
                        bk = 4 + half
                        for a in range(4):
                            k8 = half * 4 + a
                            TR(PS[bk][:, a * 128:(a + 1) * 128], H[d][:, k8 * 128:(k8 + 1) * 128], identf, [BH[d], B_cst], [PB[bk]], a == 3)
                        CP("dve", fin[:, half * 4:(half + 1) * 4, :], PS[bk][:].rearrange("p (a b) -> p a b", a=4), [PB[bk]], [Bfin])
                    kb.dma("pool", dst.rearrange("(a p) n -> p a n", p=128), fin[:], reads=[Bfin], writes=[Bout])

                def bwd_visit(sl_, c):
                    S_ = REG[sl_]
                    prefetch_state(1, sl_, c)
                    j = cn["hp"] % 2
                    cn["hp"] += 1
                    CP("dve", hpb[j][:], H[1][:], [BH[1]], [Bhpb[j]])
                    kb.dma("pool", S_.hpb_s[c], hpb[j][:], reads=[Bhpb[j]], writes=[S_.SB["hpb"]])
                    step_state(1, sl_, c)

                def fwd_visit(sl_, c):
                    S_ = REG[sl_]
                    prefetch_state(0, sl_, c)
                    i = cn["f"] % 2
                    cn["f"] += 1
                    tok = c * 128
                    CP("dve", hpf[i][:], H[0][:], [BH[0]], [Bhpf[i]])
                    kb.dma("sp", hpb[i][:], S_.hpb_s[c], reads=[S_.SB["hpb"]], writes=[Bhpb[i]])
                    kb.dma("sp", yal[i][:], S_.ya_s[tok:tok + 128, :], reads=[S_.SB["ya"]], writes=[Byal[i]])
                    kb.dma("sp", zsl[i][:], S_.zs_s[tok:tok + 128, :], reads=[S_.SB["z"]], writes=[Bzsl[i]])
                    kb.dma("sp", ctl[i][:], S_.CT_s.rearrange("p (g t) -> p g t", g=4)[:, :, tok:tok + 128], reads=[S_.SB["ct"]],
                           writes=[Bctl[i]])
                    for d in range(2):
                        hp_ = hpf[i] if d == 0 else hpb[i]
                        Bhp = Bhpf[i] if d == 0 else Bhpb[i]
                        for g in range(4):
                            bk = 2 * d + g // 2
                            MM(PS[bk][:, (g % 2) * 256:(g % 2 + 1) * 256], ctl[i][:, g, :], hp_[:, g * 256:(g + 1) * 256],
                               True, True, [Bctl[i], Bhp], [PB[bk]], g % 2 == 1)
                    for d in range(2):
                        for hb in range(2):
                            bk = 2 * d + hb
                            src = PS[bk][:].rearrange("p (h q) -> p h q", q=64)
                            dstv = t2[i][:, hb * 512:(hb + 1) * 512].rearrange("p (h q) -> p h q", q=64)
                            eav = ea2[sl_][:, d, c, hb * 8:hb * 8 + 8].unsqueeze(2).broadcast_to([128, 8, 64])
                            if d == 0:
                                TT("dve", dstv, src, eav, ALU.mult, [PB[bk], Bea], [Bt2[i]])
                            else:
                                TT("dve", src, src, eav, ALU.mult, [Bea], [PB[bk]])
                                TT("dve", t2[i][:, hb * 512:(hb + 1) * 512], t2[i][:, hb * 512:(hb + 1) * 512], PS[bk][:], ALU.add,
                                   [PB[bk]], [Bt2[i]])
                    TT("dve", t2[i][:], t2[i][:], yal[i][:], ALU.add, [Byal[i]], [Bt2[i]])
                    TT("dve", t2[i][:], t2[i][:], zsl[i][:], ALU.mult, [Bzsl[i]], [Bt2[i]])
                    ACT(junk[:], t2[i][:], AF.Square, [Bt2[i]], [Bj, Bssq[i]], accum_out=ssq[i][:, 0:1])
                    TS("dve", ssq[i][:, 1:2], ssq[i][:, 0:1], 1.0 / 1024, EPS, ALU.mult, ALU.add, [], [Bssq[i]])
                    ACT(ssq[i][:, 2:3], ssq[i][:, 1:2], AF.Sqrt, [], [Bssq[i]])
                    kb.op("dve", lambda e: e.reciprocal(out=ssq[i][:, 3:4], in_=ssq[i][:, 2:3]), [], [Bssq[i]])
                    STT(obt[i][:], t2[i][:], ssq[i][:, 3:4], gss[:], ALU.mult, ALU.mult, [Bt2[i], Bssq[i], Bgss], [Bobt[i]])
                    for fc in range(8):
                        TR(PSb(6)[:, fc * 128:(fc + 1) * 128], obt[i][:, fc * 128:(fc + 1) * 128], identb, [Bobt[i], B_cst], [PB[6]], fc == 7)
                    CP("dve", obT[:, :, (c % 4) * 128:(c % 4 + 1) * 128], PSb(6).rearrange("p (a b) -> p a b", a=8), [PB[6]], [BobT])
                    if c % 4 == 3:
                        obTv = S_.oT_s[1].rearrange("(fc p) t -> p fc t", p=128)
                        kb.dma("pool", obTv[:, :, (c - 3) * 128:(c + 1) * 128], obT[:], reads=[BobT], writes=[S_.SB["ob"]])
                    step_state(0, sl_, c)

                for sl_ in range(2):
                    for sq in range(2):
                        init_state(1, False)
                        for c in (sq * 2 + 1, sq * 2):
                            bwd_visit(sl_, c)
                        emit_state(1, REG[sl_].sbo[l, sq])
                init_state(1, True)
                for sl_ in (1, 0):
                    for c in range(19, 3, -1):
                        bwd_visit(sl_, c)
                for sl_ in range(2):
                    for sq in range(2):
                        init_state(0, False)
                        for c in (sq * 2, sq * 2 + 1):
                            fwd_visit(sl_, c)
                        emit_state(0, REG[sl_].sfo[l, sq])
                init_state(0, True)
                for sl_ in (0, 1):
                    for c in range(4, 20):
                        fwd_visit(sl_, c)
            kb.barrier()

        for l in range(n_layers):
            REG[0].phase_mod(l)
            kb.barrier()
            for sl_ in range(2):
                S_ = REG[sl_]
                xsrc = S_.x_in if l == 0 else S_.y1_s
                Bx = Buf() if l == 0 else S_.SB["y1"]
                with ExitStack() as lst:
                    hT = lst.enter_context(nc.sbuf_tensor("hT%d_%d" % (l, sl_), [128, 16, NTOK], BF16))
                    BhTs = [Buf("hT%d" % i) for i in range(20)]
                    S_.phase_norm(l, xsrc, Bx, hT, BhTs)
                    kb.barrier()
                    S_.phase_proj(l, hT, Buf("hTro"))
                kb.barrier()
            for nm in ("phase_attn", "phase_sgu", "phase_ssm"):
                for sl_ in range(2):
                    getattr(REG[sl_], nm)(l)
                    kb.barrier()
            ssm_pass2(l)
            for sl_ in range(2):
                S_ = REG[sl_]
                xsrc = S_.x_in if l == 0 else S_.y1_s
                Bx = Buf() if l == 0 else S_.SB["y1"]
                ydst = S_.y_out if l == n_layers - 1 else S_.y1_s
                By = S_.SB["yout"] if l == n_layers - 1 else S_.SB["y1"]
                S_.phase_o1(l)
                kb.barrier()
                S_.phase_o2(l, xsrc, Bx, ydst, By)
                kb.barrier()

        kb.barrier()
    return nc


def _consts():
    s = np.arange(128)[:, None]
    i = np.arange(128)[None, :]
    c = np.zeros((128, 6, 128), np.float32)
    c[:, 0] = (s == i)
    c[:, 1] = (s <= i)
    c[:, 2] = (s >= i)
    c[:, 3] = (s > i)
    c[:, 4] = (s < i)
    c[:, 5] = 1.0
    return c


def _bias_F(rpb_l):
    j = np.arange(64)[None, :]
    col = np.arange(64)[:, None]
    cs = np.clip(j - 8, 0, 48)
    valid = (col >= cs) & (col < cs + 16)
    idx = np.clip(col - j + 15, 0, 30)
    T = np.where(valid[None, None], rpb_l[:, :, idx], np.float32(NEG)).astype(np.float32)
    F = np.empty((2, 2, 64, 16, 6, 2, 64), np.float32)
    for kind in range(2):
        for c in range(6):
            for a in range(2):
                for e in range(2):
                    ri = 2 * c + a - e + (3 if kind == 0 else 1)
                    F[kind, a, :, :, c, e, :] = np.transpose(T[:, ri], (1, 0, 2))
    return F.reshape(2, 128, 16 * 6 * 128)


def _row_masks(half):
    out = np.zeros((5, 2, 64, 6, 2, 64), np.float32)
    slots = [8, 0, 1, 14, 15]
    for si, ml in enumerate(slots):
        kbg = 32 * half + (2 * ml if ml != 15 else 28) - 4
        for c in range(6):
            for a in range(2):
                kr = kbg + 2 * c + a
                for e in range(2):
                    qr = 32 * half + 2 * ml + e
                    rs = min(max(qr - 4, 0), 56)
                    ok = (0 <= kr < 64) and (rs <= kr < rs + 8)
                    out[si, a, :, c, e, :] = 0.0 if ok else NEG
    return out.reshape(5, 128, 6 * 128)


def prep_shared(inp):
    f = lambda a: np.ascontiguousarray(a, dtype=np.float32)
    sh = {}
    for k in ("w_mod", "b_mod", "g_pre", "g_post", "w_in", "g_ssm", "w_br_a", "w_br_b", "w_br_c", "w_out", "d_skip"):
        sh[k] = f(inp[k])
    sh["Fb"] = np.stack([_bias_F(np.asarray(inp["rpb"][l], np.float32)) for l in range(DEPTH)])
    sh["cwT"] = f(np.asarray(inp["conv_w"]).reshape(DEPTH, 16, 128, 3).transpose(0, 2, 1, 3))
    sh["cbT"] = f(np.asarray(inp["conv_b"]).reshape(DEPTH, 16, 128).transpose(0, 2, 1))
    sh["dt_bias"] = f(np.asarray(inp["dt_bias"]).reshape(DEPTH, 32))
    sh["a_log"] = f(np.asarray(inp["a_log"]).reshape(DEPTH, 32))
    sh["wsT"] = f(np.asarray(inp["w_s"]).transpose(0, 3, 1, 2))
    sh["b_s"] = f(np.asarray(inp["b_s"]).reshape(DEPTH, 1024))
    sh["g_sguT"] = f(np.asarray(inp["g_sgu"]).reshape(DEPTH, 8, 128).transpose(0, 2, 1))
    sh["consts"] = _consts()
    return sh


def prep_core(inp, sh, core):
    f = lambda a: np.ascontiguousarray(a, dtype=np.float32)
    b = core
    m = dict(sh)
    xs_ = []
    for slot in range(2):
        xp = np.asarray(inp["x_prompt"])[4 * core + 2 * slot:4 * core + 2 * slot + 2].reshape(512, D)
        xs = np.asarray(inp["x_sample"])[b, slot * NST:(slot + 1) * NST]
        xs_.append(np.concatenate([xp, xs], 0))
    m["x_in"] = f(np.stack(xs_, 0))
    cvec = np.stack([np.asarray(inp["c_ctx"]), np.asarray(inp["c"])[b]], 0)
    m["cvT"] = f(cvec.reshape(2, 16, 128).transpose(2, 1, 0))
    m["ck"] = f(np.asarray(inp["cache_k"])[b].reshape(DEPTH, 256, 1024))
    m["cv"] = f(np.asarray(inp["cache_v"])[b].reshape(DEPTH, 256, 1024))
    m["st0"] = f(np.stack([np.asarray(inp["state_ssm_fwd"])[b].reshape(DEPTH, 1024, 128),
                           np.asarray(inp["state_ssm_bwd"])[b].reshape(DEPTH, 1024, 128)], 1))
    m["rm"] = np.stack([_row_masks(0), _row_masks(1)], 0)
    return m


_NC_CACHE = {}
N_ACTIVE = 4


def kernel(**inputs):
    if "nc" not in _NC_CACHE:
        _NC_CACHE["nc"] = build()
    nc = _NC_CACHE["nc"]
    sh = prep_shared(inputs)
    in_maps = [prep_core(inputs, sh, c) for c in range(N_ACTIVE)]
    res = run_bass_kernel_spmd(nc, in_maps, core_ids=list(range(N_ACTIVE)))
    R = res.results
    y_p = np.empty((16, 256, D), np.float32)
    y_s = np.empty((4, 4096, D), np.float32)
    nk = np.empty((16, DEPTH, 256, 16, 64), np.float32)
    nv = np.empty((16, DEPTH, 256, 16, 64), np.float32)
    nf = np.empty((16, DEPTH, 16, 64, 128), np.float32)
    nb_ = np.empty((16, DEPTH, 16, 64, 128), np.float32)
    for c in range(N_ACTIVE):
        r = R[c]
        for slot in range(2):
            yo = r["y_out"][slot]
            y_s[c, slot * NST:(slot + 1) * NST] = yo[512:]
            for s in range(2):
                q = 4 * c + 2 * slot + s
                y_p[q] = yo[s * 256:(s + 1) * 256]
                for l in range(DEPTH):
                    nk[q, l] = r["ko"][slot, l, s * 256:(s + 1) * 256].reshape(256, 16, 64)
                    nv[q, l] = r["vo"][slot, l, s * 256:(s + 1) * 256].reshape(256, 16, 64)
                    nf[q, l] = r["sfo"][slot, l, s].reshape(16, 64, 128)
                    nb_[q, l] = r["sbo"][slot, l, s].reshape(16, 64, 128)
    return (y_p, y_s, nk, nv, nf, nb_)
```

```python
import os
import types
import numpy as np
from contextlib import ExitStack
import concourse.bass as bass
import concourse.mybir as mybir
from concourse.bass_utils import run_bass_kernel_spmd

F32 = mybir.dt.float32
BF16 = mybir.dt.bfloat16
AF = mybir.ActivationFunctionType
ALU = mybir.AluOpType
AX = mybir.AxisListType

D = 2048
NTOK = 2560
NPT = 512
NST = 2048
NIN = 16416
DEPTH = 2
EPS = 1e-6
NEG = -30000.0
OFF = dict(q=0, k=1024, v=2048, ga=3072, xbc=4096, z=6144, dt=7168, u=7200, vc=8224, gc=9248, gm=10272)
N_CORES = 8


class Buf:
    __slots__ = ("w", "r", "name")

    def __init__(self, name=""):
        self.w = None
        self.r = {}
        self.name = name


class KB:
    def __init__(self, nc, es, nds=28):
        self.nc = nc
        self.E = {"pe": nc.tensor, "act": nc.scalar, "dve": nc.vector, "pool": nc.gpsimd, "sp": nc.sync}
        self.sem = {}
        self.cnt = {}
        for k in ("pe", "act", "dve", "pool"):
            self.sem[k] = es.enter_context(nc.semaphore("s_" + k))
            self.cnt[k] = 0
        for i in range(nds):
            self.sem[("d", i)] = es.enter_context(nc.semaphore("sd%d" % i))
            self.cnt[("d", i)] = 0
        self.nds = nds
        self.dnext = {"sp": 0, "pool": 0, "act": 0}
        self.seen = {e: {} for e in self.E}

    def wait(self, eng, toks):
        seen = self.seen[eng]
        for t in toks:
            k, v = t
            if v <= 0 or seen.get(k, 0) >= v:
                continue
            if k == "pe" and eng == "pe":
                continue
            self.E[eng].wait_ge(self.sem[k], v)
            seen[k] = v

    @staticmethod
    def deps(reads, writes):
        toks = []
        for b in reads:
            if b.w is not None:
                toks.append(b.w)
        for b in writes:
            if b.w is not None:
                toks.append(b.w)
            toks.extend(b.r.items())
        return toks

    @staticmethod
    def mark(tok, reads, writes):
        k, v = tok
        for b in reads:
            if b.r.get(k, 0) < v:
                b.r[k] = v
        for b in writes:
            b.w = tok
            b.r = {}

    def op(self, eng, fn, reads=(), writes=(), inc=True):
        if eng == "pool" and not os.environ.get("KDBG_POOLC"):
            eng = "dve"
        self.wait(eng, self.deps(reads, writes))
        ins = fn(self.E[eng])
        if inc:
            self.cnt[eng] += 1
            ins.then_inc(self.sem[eng], 1)
            tok = (eng, self.cnt[eng])
        else:
            tok = (eng, self.cnt[eng] + 1)
        self.mark(tok, reads, writes)
        return tok

    def dma(self, q, out, in_, reads=(), writes=()):
        lo, n = {"sp": (0, int(os.environ.get("KDBG_NSP", "16"))), "pool": (16, 8), "act": (24, 4)}[q]
        j = self.dnext[q]
        self.dnext[q] = (j + 1) % n
        i = lo + j
        k = ("d", i)
        toks = self.deps(reads, writes)
        toks.append((k, self.cnt[k]))
        self.wait(q, toks)
        self.cnt[k] += 16
        self.E[q].dma_start(out=out, in_=in_).then_inc(self.sem[k], 16)
        tok = (k, self.cnt[k])
        self.mark(tok, reads, writes)
        return tok

    def barrier(self):
        toks = [(k, v) for k, v in self.cnt.items() if v > 0]
        for e in self.E:
            self.wait(e, toks)


def build(n_layers=DEPTH, stop=None, dbg=(), dbg_in=(), start=None, skip=()):
    nc = bass.Bass("TRN2", target_bir_lowering=False)

    def din(name, shape):
        return nc.dram_tensor(name, list(shape), F32, kind="ExternalInput").ap()

    def dout(name, shape):
        return nc.dram_tensor(name, list(shape), F32, kind="ExternalOutput").ap()

    def dscr(name, shape, dt):
        kind = "ExternalOutput" if name in dbg else ("ExternalInput" if name in dbg_in else "Internal")
        return nc.dram_tensor(name, list(shape), dt, kind=kind).ap()

    x_in_all = din("x_in", [2, NTOK, D])
    cvT = din("cvT", [128, 16, 2])
    ck = din("ck", [DEPTH, 256, 1024])
    cv = din("cv", [DEPTH, 256, 1024])
    st0 = din("st0", [DEPTH, 2, 1024, 128])
    w_mod = din("w_mod", [DEPTH, D, 3 * D])
    b_mod = din("b_mod", [DEPTH, 3 * D])
    g_pre = din("g_pre", [DEPTH, D])
    g_post = din("g_post", [DEPTH, D])
    w_in = din("w_in", [DEPTH, D, NIN])
    Fb = din("Fb", [DEPTH, 2, 128, 16 * 6 * 128])
    rm_all = din("rm", [2, 5, 128, 6 * 128])
    cwT = din("cwT", [DEPTH, 128, 16, 3])
    cbT = din("cbT", [DEPTH, 128, 16])
    dt_bias = din("dt_bias", [DEPTH, 32])
    a_log = din("a_log", [DEPTH, 32])
    d_skip = din("d_skip", [DEPTH, 16])
    g_ssm = din("g_ssm", [DEPTH, 1024])
    wsT = din("wsT", [DEPTH, 128, 8, 128])
    b_s = din("b_s", [DEPTH, 1024])
    g_sguT = din("g_sguT", [DEPTH, 128, 8])
    w_br = [din("w_br_a", [DEPTH, 1024, D]), din("w_br_b", [DEPTH, 1024, D]), din("w_br_c", [DEPTH, 1024, D])]
    w_out = din("w_out", [DEPTH, D, D])
    consts = din("consts", [128, 6, 128])
    y_out_all = dout("y_out", [2, NTOK, D])
    ko_all = dout("ko", [2, DEPTH, NPT, 1024])
    vo_all = dout("vo", [2, DEPTH, NPT, 1024])
    sfo_all = dout("sfo", [2, DEPTH, 2, 1024, 128])
    m_s = dscr("m_s", [2, 3 * D], F32)
    sbo_all = dout("sbo", [2, DEPTH, 2, 1024, 128])
    es = ExitStack()
    with es:
        kb = KB(nc, es)
        PS = [es.enter_context(nc.psum_tensor("ps%d" % i, [128, 512], F32)) for i in range(8)]
        PB = [Buf("ps%d" % i) for i in range(8)]

        def PSb(i):
            return PS[i][:].bitcast(BF16)

        cst = es.enter_context(nc.sbuf_tensor("cst", [128, 6, 128], F32))
        cstb = es.enter_context(nc.sbuf_tensor("cstb", [128, 6, 128], BF16))
        B_cst = Buf("cst")
        kb.dma("sp", cst[:], consts[:, :, :], writes=[B_cst])
        kb.op("dve", lambda e: e.tensor_copy(out=cstb[:], in_=cst[:]), reads=[B_cst], writes=[B_cst])
        identb = cstb[:, 0, :]
        identf = cst[:, 0, :]

        uid = [0]

        def sbuf_alloc(stack):
            def f(name, shape, dt):
                uid[0] += 1
                return stack.enter_context(nc.sbuf_tensor("%s_%d" % (name, uid[0]), list(shape), dt))
            return f

        def MM(out, lhsT, rhs, start, stop, R, W, inc):
            return kb.op("pe", lambda e: e.matmul(out, lhsT=lhsT, rhs=rhs, start=start, stop=stop), R, W, inc)

        def TR(out, in_, ident, R, W, inc=True):
            return kb.op("pe", lambda e: e.transpose(out, in_, ident), R, W, inc)

        def ACT(out, in_, func, R, W, bias=None, scale=None, accum_out=None):
            kw = {}
            if bias is not None:
                kw["bias"] = bias
            if scale is not None:
                kw["scale"] = scale
            if accum_out is not None:
                kw["accum_out"] = accum_out
            return kb.op("act", lambda e: e.activation(out=out, in_=in_, func=func, **kw), R, W)

        def TT(eng, out, in0, in1, op, R, W):
            return kb.op(eng, lambda e: e.tensor_tensor(out=out, in0=in0, in1=in1, op=op), R, W)

        def TS(eng, out, in0, s1, s2, op0, op1, R, W):
            if op1 is None:
                return kb.op(eng, lambda e: e.tensor_scalar(out=out, in0=in0, scalar1=s1, scalar2=None, op0=op0), R, W)
            return kb.op(eng, lambda e: e.tensor_scalar(out=out, in0=in0, scalar1=s1, scalar2=s2, op0=op0, op1=op1), R, W)

        def STT(out, in0, scalar, in1, op0, op1, R, W):
            return kb.op("dve", lambda e: e.scalar_tensor_tensor(out=out, in0=in0, scalar=scalar, in1=in1, op0=op0, op1=op1), R, W)

        def CP(eng, out, in_, R, W):
            if eng == "act" and os.environ.get("KDBG_NOACTCP"):
                eng = "dve"
            if eng == "act":
                return kb.op("act", lambda e: e.activation(out=out, in_=in_, func=AF.Identity), R, W)
            return kb.op(eng, lambda e: e.tensor_copy(out=out, in_=in_), R, W)

        SBm = Buf("m")
        REG = {}

        def make_slot(slot):
            x_in = x_in_all[slot]
            y_out = y_out_all[slot]
            ko, vo, sfo, sbo = ko_all[slot], vo_all[slot], sfo_all[slot], sbo_all[slot]
            rm = rm_all[slot]
            y1_s = dscr("s%d_" % slot + "y1_s", [NTOK, D], F32)
            qT_s = dscr("s%d_" % slot + "qT_s", [1024, NTOK], BF16)
            kT_s = dscr("s%d_" % slot + "kT_s", [1024, NTOK], BF16)
            v_s = dscr("s%d_" % slot + "v_s", [NTOK, 1024], BF16)
            ga_s = dscr("s%d_" % slot + "ga_s", [NTOK, 1024], BF16)
            xbcT_s = dscr("s%d_" % slot + "xbcT_s", [2048, NTOK], BF16)
            zs_s = dscr("s%d_" % slot + "zs_s", [NTOK, 1024], BF16)
            dt_s = dscr("s%d_" % slot + "dt_s", [128, 20 * 32], F32)
            uT_s = dscr("s%d_" % slot + "uT_s", [1024, NTOK], BF16)
            vc_s = dscr("s%d_" % slot + "vc_s", [NTOK, 1024], BF16)
            gcT_s = dscr("s%d_" % slot + "gcT_s", [1024, NTOK], BF16)
            gmT_s = dscr("s%d_" % slot + "gmT_s", [3 * D, NTOK], BF16)
            oT_s = [dscr("s%d_" % slot + "oaT_s", [1024, NTOK], BF16), dscr("s%d_" % slot + "obT_s", [1024, NTOK], BF16), dscr("s%d_" % slot + "ocT_s", [1024, NTOK], BF16)]
            mgT_s = dscr("s%d_" % slot + "mgT_s", [D, NTOK], BF16)
            ya_s = dscr("s%d_" % slot + "ya_s", [NTOK, 1024], F32)
            stc_s = dscr("s%d_" % slot + "stc_s", [20, 2, 128, 1024], F32)
            hpb_s = dscr("s%d_" % slot + "hpb_s", [20, 128, 1024], BF16)
            CT_s = dscr("s%d_" % slot + "CT_s", [128, 4 * NTOK], BF16)
            ea_s = dscr("s%d_" % slot + "ea_s", [128, 640], F32)
            cd_s = dscr("s%d_" % slot + "cd_s", [128, 640], F32)
            SB = {n: Buf(n) for n in ("m", "q", "k", "v", "ga", "xbc", "z", "dt", "u", "vc", "gc", "gm", "oa", "ob",
                                      "oc", "mg", "ya", "stc", "hpb", "y1", "ko", "vo", "sfo", "sbo", "yout", "ct", "ea", "cd")}
            SB["m"] = SBm

            def phase_mod(l):
                with ExitStack() as st:
                    sb = sbuf_alloc(st)
                    cvt = sb("cvt", [128, 16, 2], F32)
                    scT = sb("scT", [128, 16, 2], BF16)
                    bm = sb("bm", [2, 3 * D], F32)
                    mrow = sb("mrow", [2, 3 * D], F32)
                    wb = [sb("wm%d" % i, [128, 16, 512], BF16) for i in range(3)]
                    Bw = [Buf() for _ in range(3)]
                    Bc, Bs, Bb, Bm = Buf(), Buf(), Buf(), Buf()
                    kb.dma("sp", cvt[:], cvT[:, :, :], writes=[Bc])
                    ACT(scT[:], cvt[:], AF.Silu, [Bc], [Bs])
                    kb.dma("sp", bm[:], b_mod[l:l + 1, :].partition_broadcast(2), writes=[Bb])
                    wv = w_mod[l].rearrange("(kc p) f -> p kc f", p=128)
                    for cb in range(12):
                        w = wb[cb % 3]
                        kb.dma("pool", w[:], wv[:, :, cb * 512:(cb + 1) * 512], writes=[Bw[cb % 3]])
                        for kc in range(16):
                            MM(PS[0][0:2, :], scT[:, kc, :], w[:, kc, :], kc == 0, kc == 15, [Bs, Bw[cb % 3]], [PB[0]], kc == 15)
                        TT("dve", mrow[:, cb * 512:(cb + 1) * 512], PS[0][0:2, :], bm[:, cb * 512:(cb + 1) * 512], ALU.add,
                           [PB[0], Bb], [Bm])
                    kb.dma("sp", m_s[:, :], mrow[:], reads=[Bm], writes=[SB["m"]])

            def phase_norm(l, xsrc, Bx, hT, BhT):
                with ExitStack() as st:
                    sb = sbuf_alloc(st)
                    gp = sb("gp", [128, D], F32)
                    A = [sb("A%d" % g, [128, D], F32) for g in range(2)]
                    Bs_ = [sb("Bs%d" % g, [128, D], F32) for g in range(2)]
                    xt = [sb("xt%d" % i, [128, D], F32) for i in range(2)]
                    hx = [sb("hx%d" % i, [128, D], BF16) for i in range(2)]
                    junk = sb("junk", [128, D], BF16)
                    ss = [sb("ss%d" % i, [128, 4], F32) for i in range(2)]
                    Bgp, BA, BBs = Buf(), [Buf(), Buf()], [Buf(), Buf()]
                    Bxt, Bhx, Bj, Bss = [Buf(), Buf()], [Buf(), Buf()], Buf(), [Buf(), Buf()]
                    kb.dma("sp", gp[:], g_pre[l:l + 1, :].partition_broadcast(128), writes=[Bgp])
                    for g in range(2):
                        kb.dma("sp", A[g][:], m_s[g:g + 1, D:2 * D].partition_broadcast(128), reads=[SB["m"]], writes=[BA[g]])
                        kb.dma("sp", Bs_[g][:], m_s[g:g + 1, 0:D].partition_broadcast(128), reads=[SB["m"]], writes=[BBs[g]])
                        STT(A[g][:], A[g][:], 1.0, gp[:], ALU.add, ALU.mult, [Bgp], [BA[g]])
                    for ts in range(20):
                        g = 0 if ts < 4 else 1
                        i = ts % 2
                        kb.dma("sp", xt[i][:], xsrc[ts * 128:(ts + 1) * 128, :], reads=[Bx], writes=[Bxt[i]])
                        ACT(junk[:], xt[i][:], AF.Square, [Bxt[i]], [Bj, Bss[i]], accum_out=ss[i][:, 0:1])
                        TS("dve", ss[i][:, 1:2], ss[i][:, 0:1], 1.0 / D, EPS, ALU.mult, ALU.add, [], [Bss[i]])
                        ACT(ss[i][:, 2:3], ss[i][:, 1:2], AF.Sqrt, [], [Bss[i]])
                        kb.op("dve", lambda e: e.reciprocal(out=ss[i][:, 3:4], in_=ss[i][:, 2:3]), [], [Bss[i]])
                        STT(xt[i][:], xt[i][:], ss[i][:, 3:4], A[g][:], ALU.mult, ALU.mult, [Bss[i], BA[g]], [Bxt[i]])
                        TT("dve", hx[i][:], xt[i][:], Bs_[g][:], ALU.add, [Bxt[i], BBs[g]], [Bhx[i]])
                        for half in range(2):
                            bk = 2 * i + half
                            for j in range(8):
                                kc = half * 8 + j
                                TR(PSb(bk)[:, j * 128:(j + 1) * 128], hx[i][:, kc * 128:(kc + 1) * 128], identb,
                                   [Bhx[i], B_cst], [PB[bk]], j == 7)
                            CP("act" if half == 0 else "dve",
                               hT[:, half * 8:(half + 1) * 8, ts * 128:(ts + 1) * 128],
                               PSb(bk).rearrange("p (a b) -> p a b", a=8), [PB[bk]], [BhT[ts]])

            def phase_proj(l, hT, BhT):
                with ExitStack() as st:
                    sb = sbuf_alloc(st)
                    wb = [sb("wp%d" % i, [128, 16, 512], BF16) for i in range(3)]
                    Bw = [Buf() for _ in range(3)]
                    wdt = sb("wdt", [128, 16, 32], BF16)
                    Bwdt = Buf()
                    sfm = [sb("sfm%d" % i, [128, NTOK], BF16) for i in range(3)]
                    Bsfm = [Buf() for _ in range(3)]
                    stm = [sb("stm%d" % i, [128, 512], BF16) for i in range(4)]
                    Bstm = [Buf() for _ in range(4)]
                    s32 = [sb("s32%d" % i, [128, 512], F32) for i in range(2)]
                    Bs32 = [Buf() for _ in range(2)]
                    dts = sb("dts", [128, 20, 32], F32)
                    Bdts = Buf()
                    wv = w_in[l].rearrange("(kc p) f -> p kc f", p=128)
                    blocks = []

                    def addF(c0, n, scr, key, ev):
                        for b in range(n // 512):
                            blocks.append(("F", c0 + b * 512, scr, key, b * 512, ev))

                    def addT(c0, n, scr, key, ev):
                        for b in range(n // 512):
                            blocks.append(("T", c0 + b * 512, scr, key, b * 512, ev))
                    addF(OFF["q"], 1024, qT_s, "q", "qs")
                    addF(OFF["k"], 1024, kT_s, "k", "cp")
                    addT(OFF["v"], 1024, v_s, "v", "cp")
                    addT(OFF["ga"], 1024, ga_s, "ga", "silu")
                    addF(OFF["xbc"], 2048, xbcT_s, "xbc", "cp")
                    addT(OFF["z"], 1024, zs_s, "z", "silu")
                    addF(OFF["u"], 1024, uT_s, "u", "cp")
                    addT(OFF["vc"], 1024, vc_s, "vc", "cp")
                    addF(OFF["gc"], 1024, gcT_s, "gc", "silu")
                    addF(OFF["gm"], 6144, gmT_s, "gm", "sig")
                    nb = len(blocks)
                    if os.environ.get("KDBG_NB"):
                        blocks = blocks[:int(os.environ["KDBG_NB"])]
                        nb = len(blocks)
                    state = dict(bank=0, fm=0, tm=0, s32=0, ev=0)

                    def load(bi):
                        c0 = blocks[bi][1]
                        kb.dma("pool", wb[bi % 3][:], wv[:, :, c0:c0 + 512], writes=[Bw[bi % 3]])

                    def nbank():
                        b = state["bank"]
                        state["bank"] = (b + 1) % int(os.environ.get("KDBG_NBANK", "7"))
                        return b

                    def evac(ev, out, bank, W):
                        if ev == "silu":
                            ACT(out, PS[bank][:], AF.Silu, [PB[bank]], W)
                        elif ev == "sig":
                            ACT(out, PS[bank][:], AF.Sigmoid, [PB[bank]], W)
                        elif ev == "qs":
                            TS("dve", out, PS[bank][:], 0.125, None, ALU.mult, None, [PB[bank]], W)
                        else:
                            CP("dve", out, PS[bank][:], [PB[bank]], W)

                    if not os.environ.get("KDBG_NODT"):
                        kb.dma("pool", wdt[:], wv[:, :, OFF["dt"]:OFF["dt"] + 32], writes=[Bwdt])
                    load(0)
                    load(1)
                    for bi in range(nb):
                        lay, c0, scr, key, off, ev = blocks[bi]
                        if bi + 2 < nb:
                            load(bi + 2)
                        w = wb[bi % 3]
                        BW = Bw[bi % 3]
                        if lay == "F":
                            for fcl in range(4):
                                si = state["fm"]
                                state["fm"] = (si + 1) % 3
                                for tt in range(5):
                                    bk = nbank()
                                    for kc in range(16):
                                        MM(PS[bk][:], w[:, kc, fcl * 128:(fcl + 1) * 128], hT[:, kc, tt * 512:(tt + 1) * 512],
                                           kc == 0, kc == 15, [BW, BhT], [PB[bk]], kc == 15)
                                    evac(ev, sfm[si][:, tt * 512:(tt + 1) * 512], bk, [Bsfm[si]])
                                r0 = off + fcl * 128
                                kb.dma("sp", scr[r0:r0 + 128, :], sfm[si][:], reads=[Bsfm[si]], writes=[SB[key]])
                            if key == "k":
                                for ts in range(4):
                                    bk = nbank()
                                    for kc in range(16):
                                        MM(PS[bk][:], hT[:, kc, ts * 128:(ts + 1) * 128], w[:, kc, :], kc == 0, kc == 15,
                                           [BW, BhT], [PB[bk]], kc == 15)
                                    j = state["s32"]
                                    state["s32"] = j ^ 1
                                    CP("dve", s32[j][:], PS[bk][:], [PB[bk]], [Bs32[j]])
                                    kb.dma("sp", ko[l, ts * 128:(ts + 1) * 128, off:off + 512], s32[j][:], reads=[Bs32[j]],
                                           writes=[SB["ko"]])
                        else:
                            for ts in range(20):
                                bk = nbank()
                                for kc in range(16):
                                    MM(PS[bk][:], hT[:, kc, ts * 128:(ts + 1) * 128], w[:, kc, :], kc == 0, kc == 15,
                                       [BW, BhT], [PB[bk]], kc == 15)
                                si = state["tm"]
                                state["tm"] = (si + 1) % 4
                                if key == "v" and ts < 4:
                                    j = state["s32"]
                                    state["s32"] = j ^ 1
                                    CP("dve", s32[j][:], PS[bk][:], [PB[bk]], [Bs32[j]])
                                    kb.dma("sp", vo[l, ts * 128:(ts + 1) * 128, off:off + 512], s32[j][:], reads=[Bs32[j]],
                                           writes=[SB["vo"]])
                                evac(ev, stm[si][:], bk, [Bstm[si]])
                                kb.dma("sp", scr[ts * 128:(ts + 1) * 128, off:off + 512], stm[si][:], reads=[Bstm[si]],
                                       writes=[SB[key]])
                        if key == "z" and off == 512 and not os.environ.get("KDBG_NODT"):
                            for ts in range(20):
                                bk = nbank()
                                for kc in range(16):
                                    MM(PS[bk][:, 0:32], hT[:, kc, ts * 128:(ts + 1) * 128], wdt[:, kc, :], kc == 0, kc == 15,
                                       [Bwdt, BhT], [PB[bk]], kc == 15)
                                CP("dve", dts[:, ts, :], PS[bk][:, 0:32], [PB[bk]], [Bdts])
                            kb.dma("sp", dt_s[:, :], dts[:].rearrange("p t c -> p (t c)"), reads=[Bdts], writes=[SB["dt"]])

            def phase_attn(l):
                kTv = kT_s.rearrange("(fc p) t -> p fc t", p=128)
                qTv = qT_s.rearrange("(fc p) t -> p fc t", p=128)
                oaTv = oT_s[0].rearrange("(fc p) t -> p fc t", p=128)
                with ExitStack() as st:
                    sb = sbuf_alloc(st)
                    qt = [sb("qt%d" % i, [128, 8, 128], BF16) for i in range(2)]
                    sga = [sb("sga%d" % i, [128, 1024], BF16) for i in range(2)]
                    PT = [sb("PT%d" % i, [128, 1024], BF16) for i in range(3)]
                    oa = [sb("oa%d" % i, [128, 1024], BF16) for i in range(2)]
                    rden = [sb("rden%d" % i, [128, 4], F32) for i in range(2)]
                    oaT = sb("oaT", [128, 8, 512], BF16)
                    Bqt, Bsga = [Buf(), Buf()], [Buf(), Buf()]
                    BPT, Boa, Brd, BoaT = [Buf() for _ in range(3)], [Buf(), Buf()], [Buf(), Buf()], Buf()
                    cnt = dict(q=0, pt=0, s=0, o=0, oa=0)

                    def block(tok0, chunks_fn, nch, Rk, oslot, flush):
                        qi = cnt["q"] % 2
                        cnt["q"] += 1
                        kb.dma("sp", qt[qi][:], qTv[:, :, tok0:tok0 + 128], reads=[SB["q"]], writes=[Bqt[qi]])
                        kb.dma("sp", sga[qi][:], ga_s[tok0:tok0 + 128, :], reads=[SB["ga"]], writes=[Bsga[qi]])
                        oi = cnt["oa"] % 2
                        cnt["oa"] += 1
                        pend = {}

                        def qk(h):
                            hp, fc = (h % 2) * 64, h // 2
                            sr = cnt["s"] % 2
                            cnt["s"] += 1
                            pend[h] = sr
                            for ci in range(nch):
                                bank = 2 * sr + ci // 4
                                o_ = (ci % 4) * 128
                                kap, vap, bap = chunks_fn(h, ci)
                                lastb = (ci % 4 == 3) or (ci == nch - 1)
                                MM(PS[bank][:, o_:o_ + 128], kap, qt[qi][hp:hp + 64, fc, :], True, bap is None,
                                   Rk + [Bqt[qi]], [PB[bank]], lastb and bap is None)
                                if bap is not None:
                                    MM(PS[bank][:, o_:o_ + 128], identb, bap, False, True, Rk + [B_cst], [PB[bank]], lastb)

                        def pv(h):
                            sr = pend.pop(h)
                            pi = cnt["pt"] % 3
                            cnt["pt"] += 1
                            for b in range((nch + 3) // 4):
                                ncol = min(nch - 4 * b, 4) * 128
                                ACT(PT[pi][:, b * 512:b * 512 + ncol], PS[2 * sr + b][:, 0:ncol], AF.Exp, [PB[2 * sr + b]], [BPT[pi]])
                            hh = h % 4
                            ob = 4 + ((cnt["o"] // 4) % 2)
                            cnt["o"] += 1
                            Ov = PS[ob][:, 0:260].rearrange("p (a b) -> p a b", a=4)
                            for ci in range(nch):
                                kap, vap, bap = chunks_fn(h, ci)
                                MM(Ov[:, hh, :], PT[pi][:, ci * 128:(ci + 1) * 128], vap, ci == 0, ci == nch - 1,
                                   Rk + [BPT[pi]], [PB[ob]], ci == nch - 1)
                            if hh == 3:
                                ri = (h // 4) % 2
                                kb.op("dve", lambda e: e.reciprocal(out=rden[ri][:], in_=Ov[:, :, 64]), [PB[ob]], [Brd[ri]])
                                for k4 in range(4):
                                    h2 = h - 3 + k4
                                    STT(oa[oi][:, h2 * 64:(h2 + 1) * 64], Ov[:, k4, 0:64], rden[ri][:, k4:k4 + 1],
                                        sga[qi][:, h2 * 64:(h2 + 1) * 64], ALU.mult, ALU.mult,
                                        [PB[ob], Brd[ri], Bsga[qi]], [Boa[oi]])
                        qk(0)
                        for h in range(16):
                            if h + 1 < 16:
                                qk(h + 1)
                            pv(h)
                        for fc in range(8):
                            TR(PSb(6)[:, fc * 128:(fc + 1) * 128], oa[oi][:, fc * 128:(fc + 1) * 128], identb,
                               [Boa[oi], B_cst], [PB[6]], fc == 7)
                        CP("dve", oaT[:, :, oslot * 128:(oslot + 1) * 128], PSb(6).rearrange("p (a b) -> p a b", a=8),
                           [PB[6]], [BoaT])
                        if flush is not None:
                            kb.dma("pool", oaTv[:, :, flush:flush + 512], oaT[:], reads=[BoaT], writes=[SB["oa"]])

                    with ExitStack() as st2:
                        sb2 = sbuf_alloc(st2)
                        kTp = sb2("kTp", [128, 8, 512], BF16)
                        vp = sb2("vp", [128, 4, 16, 65], BF16)
                        Bkp, Bvp = Buf(), Buf()
                        kb.dma("sp", kTp[:], kTv[:, :, 0:512], reads=[SB["k"]], writes=[Bkp])
                        kb.op("pool", lambda e: e.memset(vp[:, :, :, 64:65], 1.0), [], [Bvp])
                        for c in range(4):
                            kb.dma("sp", vp[:, c, :, 0:64], v_s[c * 128:(c + 1) * 128, :].rearrange("p (h d) -> p h d", d=64),
                                   reads=[SB["v"]], writes=[Bvp])
                        for sq in range(2):
                            for t in range(2):
                                def cf(h, ci, sq=sq):
                                    hp, fc = (h % 2) * 64, h // 2
                                    return (kTp[hp:hp + 64, fc, sq * 256 + ci * 128: sq * 256 + (ci + 1) * 128],
                                            vp[:, sq * 2 + ci, h, :], None)
                                pslot = sq * 2 + t
                                block(sq * 256 + t * 128, cf, 2, [Bkp, Bvp], pslot, 0 if pslot == 3 else None)
                        kb.barrier()
                    with ExitStack() as st2:
                        sb2 = sbuf_alloc(st2)
                        kTs = sb2("kTs", [128, 8, 2560], BF16)
                        vs = sb2("vs", [128, 20, 16, 65], BF16)
                        kcT = sb2("kcT", [128, 8, 256], BF16)
                        vcx = sb2("vcx", [128, 2, 16, 65], BF16)
                        bint = sb2("bint", [128, 16, 768], BF16)
                        bsp2 = [sb2("bsp%d" % i_, [128, 16, 768], BF16) for i_ in range(2)]
                        rmt = [sb2("rmt%d" % i, [128, 768], BF16) for i in range(2)]
                        Bks, Bvs, Bkc, Bvc, Bbi, Bbs2, Brm = Buf(), Buf(), Buf(), Buf(), Buf(), [Buf(), Buf()], [Buf(), Buf()]
                        oth = REG[1 - slot]
                        okTv = oth.kT_s.rearrange("(fc p) t -> p fc t", p=128)
                        kb.op("pool", lambda e: e.memset(vs[:, 0:2, :, :], 0.0), [], [Bvs])
                        kb.op("pool", lambda e: e.memset(vs[:, 18:20, :, :], 0.0), [], [Bvs])
                        kb.op("pool", lambda e: e.memset(vs[:, :, :, 64:65], 1.0), [], [Bvs])
                        if slot == 1:
                            kb.dma("sp", kTs[:, :, 0:256], okTv[:, :, 2304:2560], reads=[oth.SB["k"]], writes=[Bks])
                            for c in range(2):
                                kb.dma("sp", vs[:, c, :, 0:64],
                                       oth.v_s[2304 + c * 128:2304 + (c + 1) * 128, :].rearrange("p (h d) -> p h d", d=64),
                                       reads=[oth.SB["v"]], writes=[Bvs])
                            kb.op("pool", lambda e: e.memset(kTs[:, :, 2304:2560], 0.0), [], [Bks])
                        else:
                            kb.op("pool", lambda e: e.memset(kTs[:, :, 0:256], 0.0), [], [Bks])
                            kb.dma("sp", kTs[:, :, 2304:2560], okTv[:, :, 512:768], reads=[oth.SB["k"]], writes=[Bks])
                            for c in range(2):
                                kb.dma("sp", vs[:, 18 + c, :, 0:64],
                                       oth.v_s[512 + c * 128:512 + (c + 1) * 128, :].rearrange("p (h d) -> p h d", d=64),
                                       reads=[oth.SB["v"]], writes=[Bvs])
                        kb.dma("sp", kTs[:, :, 256:2304], kTv[:, :, 512:2560], reads=[SB["k"]], writes=[Bks])
                        for c in range(16):
                            kb.dma("sp", vs[:, 2 + c, :, 0:64],
                                   v_s[512 + c * 128:512 + (c + 1) * 128, :].rearrange("p (h d) -> p h d", d=64),
                                   reads=[SB["v"]], writes=[Bvs])
                        kb.op("pool", lambda e: e.memset(vcx[:, :, :, 64:65], 1.0), [], [Bvc])
                        with ExitStack() as st3:
                            ckb = st3.enter_context(nc.sbuf_tensor("ckb%d_%d" % (l, slot), [128, 2, 1024], BF16))
                            Bck = Buf()
                            kb.dma("pool", ckb[:], ck[l].rearrange("(c p) f -> p c f", p=128), writes=[Bck])
                            for c in range(2):
                                kb.dma("pool", vcx[:, c, :, 0:64], cv[l, c * 128:(c + 1) * 128, :].rearrange("p (h d) -> p h d", d=64),
                                       writes=[Bvc])
                                for fc in range(8):
                                    TR(PSb(6)[:, fc * 128:(fc + 1) * 128], ckb[:, c, fc * 128:(fc + 1) * 128], identb,
                                       [Bck, B_cst], [PB[6]], fc == 7)
                                CP("dve", kcT[:, :, c * 128:(c + 1) * 128], PSb(6).rearrange("p (a b) -> p a b", a=8),
                                   [PB[6]], [Bkc])
                            kb.barrier()

                        def load_bias(dst, Bdst, kind, ridx, j):
                            kb.dma("pool", dst[:].rearrange("p a b -> p (a b)"), Fb[l, kind], writes=[Bdst])
                            kb.dma("pool", rmt[j][:], rm[ridx], writes=[Brm[j]])
                            TT("dve", dst[:], dst[:], rmt[j][:].unsqueeze(1).broadcast_to([128, 16, 768]), ALU.add,
                               [Brm[j]], [Bdst])
                        load_bias(bint, Bbi, 0, 0, 0)
                        rmj = 1
                        for ml in range(16):
                            special = ml in (0, 1, 14, 15)
                            if special:
                                sp_i = ml % 2
                                if ml in (0, 1):
                                    if ml == 0:
                                        load_bias(bsp2[0], Bbs2[0], 0, 1, 0)
                                        load_bias(bsp2[1], Bbs2[1], 0, 2, 1)
                                else:
                                    if ml == 14:
                                        load_bias(bsp2[0], Bbs2[0], 0, 3, 0)
                                        load_bias(bsp2[1], Bbs2[1], 1, 4, 1)
                            bt, Bbt = (bsp2[ml % 2], Bbs2[ml % 2]) if special else (bint, Bbi)
                            cb0 = 14 if ml == 15 else ml
                            nloc = 6 if ml in (0, 15) else 5

                            def cf(h, ci, cb0=cb0, nloc=nloc, bt=bt):
                                hp, fc = (h % 2) * 64, h // 2
                                if ci < nloc:
                                    c = cb0 + ci
                                    return (kTs[hp:hp + 64, fc, c * 128:(c + 1) * 128], vs[:, c, h, :],
                                            bt[:, h, ci * 128:(ci + 1) * 128])
                                c = ci - nloc
                                return (kcT[hp:hp + 64, fc, c * 128:(c + 1) * 128], vcx[:, c, h, :], None)
                            block(512 + ml * 128, cf, nloc + 2, [Bks, Bvs, Bkc, Bvc, Bbt], ml % 4,
                                  512 + (ml - 3) * 128 if ml % 4 == 3 else None)
                        kb.barrier()

            def phase_sgu(l):
                uTv = uT_s.rearrange("(g e) t -> e g t", e=128)
                gcTv = gcT_s.rearrange("(g e) t -> e g t", e=128)
                ocTv = oT_s[2].rearrange("(g e) t -> e g t", e=128)
                with ExitStack() as st:
                    sb = sbuf_alloc(st)
                    wst = sb("wst", [128, 8, 128], BF16)
                    bsb = sb("bsb", [128, 8, 128], F32)
                    gsg = sb("gsg", [128, 8], F32)
                    Bw, Bb, Bg = Buf(), Buf(), Buf()
                    kb.dma("pool", wst[:], wsT[l], writes=[Bw])
                    kb.dma("sp", bsb[:].rearrange("p a b -> p (a b)"), b_s[l:l + 1, :].partition_broadcast(128), writes=[Bb])
                    kb.dma("sp", gsg[:], g_sguT[l], writes=[Bg])
                    vct = [sb("vct%d" % i, [128, 1024], BF16) for i in range(2)]
                    vn = [sb("vn%d" % i, [128, 1024], BF16) for i in range(2)]
                    stt = [sb("stt%d" % i, [128, 16], F32) for i in range(2)]
                    ut = [sb("ut%d" % i, [128, 8, 512], BF16) for i in range(2)]
                    gt = [sb("gt%d" % i, [128, 8, 512], BF16) for i in range(2)]
                    oc = [sb("oc%d" % i, [128, 8, 512], BF16) for i in range(2)]
                    tmp = [sb("tmpg%d" % i, [128, 8, 128], F32) for i in range(2)]
                    Bvct, Bvn, Bstt = [Buf(), Buf()], [Buf(), Buf()], [Buf(), Buf()]
                    But, Bgt, Boc, Btmp = [Buf(), Buf()], [Buf(), Buf()], [Buf(), Buf()], [Buf(), Buf()]
                    for tt in range(5):
                        ti = tt % 2
                        kb.dma("sp", ut[ti][:], uTv[:, :, tt * 512:(tt + 1) * 512], reads=[SB["u"]], writes=[But[ti]])
                        kb.dma("sp", gt[ti][:], gcTv[:, :, tt * 512:(tt + 1) * 512], reads=[SB["gc"]], writes=[Bgt[ti]])
                        TT("pool", ut[ti][:], ut[ti][:], gt[ti][:], ALU.mult, [Bgt[ti]], [But[ti]])
                        for sc in range(4):
                            c = tt * 4 + sc
                            i = c % 2
                            kb.dma("sp", vct[i][:], vc_s[c * 128:(c + 1) * 128, :], reads=[SB["vc"]], writes=[Bvct[i]])
                            for hf in range(2):
                                kb.op("dve", lambda e: e.bn_stats(out=stt[i][:, hf * 6:(hf + 1) * 6], in_=vct[i][:, hf * 512:(hf + 1) * 512]),
                                      [Bvct[i]], [Bstt[i]])
                            kb.op("dve", lambda e: e.bn_aggr(out=stt[i][:, 12:14], in_=stt[i][:, 0:12]), [], [Bstt[i]])
                            TS("dve", stt[i][:, 14:15], stt[i][:, 13:14], EPS, None, ALU.add, None, [], [Bstt[i]])
                            ACT(stt[i][:, 14:15], stt[i][:, 14:15], AF.Sqrt, [], [Bstt[i]])
                            kb.op("dve", lambda e: e.reciprocal(out=stt[i][:, 15:16], in_=stt[i][:, 14:15]), [], [Bstt[i]])
                            TS("dve", vn[i][:], vct[i][:], stt[i][:, 12:13], stt[i][:, 15:16], ALU.subtract, ALU.mult,
                               [Bvct[i], Bstt[i]], [Bvn[i]])
                            bks = (0, 1) if c % 2 == 0 else (2, 3)
                            for g in range(8):
                                bk = bks[g // 4]
                                MM(PS[bk][:, (g % 4) * 128:(g % 4 + 1) * 128], vn[i][:, g * 128:(g + 1) * 128], wst[:, g, :],
                                   True, True, [Bvn[i], Bw], [PB[bk]], g % 4 == 3)
                            for g in range(8):
                                bk = bks[g // 4]
                                STT(tmp[i][:, g, :], PS[bk][:, (g % 4) * 128:(g % 4 + 1) * 128], gsg[:, g:g + 1], bsb[:, g, :],
                                    ALU.mult, ALU.add, [PB[bk], Bg, Bb], [Btmp[i]])
                            TT("dve", oc[ti][:, :, sc * 128:(sc + 1) * 128], tmp[i][:], ut[ti][:, :, sc * 128:(sc + 1) * 128], ALU.mult,
                               [Btmp[i], But[ti]], [Boc[ti]])
                        kb.dma("pool", ocTv[:, :, tt * 512:(tt + 1) * 512], oc[ti][:], reads=[Boc[ti]], writes=[SB["oc"]])

            def phase_o1(l):
                gmv = gmT_s.rearrange("(br fc p) t -> p br fc t", p=128, br=3)
                mgv = mgT_s.rearrange("(fc p) t -> p fc t", p=128)
                with ExitStack() as st:
                    sb = sbuf_alloc(st)
                    wbr = [sb("wbr%d" % b, [128, 8, D], BF16) for b in range(3)]
                    Bwbr = [Buf() for _ in range(3)]
                    for b in range(3):
                        kb.dma("pool", wbr[b][:], w_br[b][l].rearrange("(kc p) f -> p kc f", p=128), writes=[Bwbr[b]])
                    ot = [[sb("ot%d_%d" % (b, i), [128, 8, 512], BF16) for b in range(3)] for i in range(2)]
                    Bot = [[Buf() for _ in range(3)] for _ in range(2)]
                    gmt = [sb("gmt%d" % i, [128, 3, 512], BF16) for i in range(3)]
                    Bgm = [Buf() for _ in range(3)]
                    acc = [sb("acc%d" % i, [128, 512], F32) for i in range(2)]
                    Bacc = [Buf(), Buf()]
                    mg = [sb("mg%d" % i, [128, 16, 512], BF16) for i in range(2)]
                    Bmg = [Buf(), Buf()]
                    nbk = 0
                    for tt in range(5):
                        ti = tt % 2
                        for b in range(3):
                            kb.dma("sp", ot[ti][b][:], oT_s[b].rearrange("(kc p) t -> p kc t", p=128)[:, :, tt * 512:(tt + 1) * 512],
                                   reads=[SB[("oa", "ob", "oc")[b]]], writes=[Bot[ti][b]])
                        for fc in range(16):
                            gi = (tt * 16 + fc) % 3
                            ai = fc % 2
                            kb.dma("sp", gmt[gi][:], gmv[:, :, fc, tt * 512:(tt + 1) * 512], reads=[SB["gm"]], writes=[Bgm[gi]])
                            for b in range(3):
                                bk = nbk
                                nbk = (nbk + 1) % 7
                                for kc in range(8):
                                    MM(PS[bk][:], wbr[b][:, kc, fc * 128:(fc + 1) * 128], ot[ti][b][:, kc, :], kc == 0, kc == 7,
                                       [Bwbr[b], Bot[ti][b]], [PB[bk]], kc == 7)
                                if b == 0:
                                    TT("dve", acc[ai][:], PS[bk][:], gmt[gi][:, 0, :], ALU.mult, [PB[bk], Bgm[gi]], [Bacc[ai]])
                                elif b == 1:
                                    t2 = TT("dve", PS[bk][:], PS[bk][:], gmt[gi][:, 1, :], ALU.mult, [Bgm[gi]], [PB[bk]])
                                    TT("dve", acc[ai][:], acc[ai][:], PS[bk][:], ALU.add, [PB[bk]], [Bacc[ai]])
                                else:
                                    TT("dve", PS[bk][:], PS[bk][:], gmt[gi][:, 2, :], ALU.mult, [Bgm[gi]], [PB[bk]])
                                    TT("dve", mg[ti][:, fc, :], acc[ai][:], PS[bk][:], ALU.add, [PB[bk], Bacc[ai]], [Bmg[ti]])
                        kb.dma("pool", mgv[:, :, tt * 512:(tt + 1) * 512], mg[ti][:], reads=[Bmg[ti]], writes=[SB["mg"]])

            def phase_o2(l, xsrc, Bx, ydst, By):
                mgv = mgT_s.rearrange("(kc p) t -> p kc t", p=128)
                with ExitStack() as st:
                    sb = sbuf_alloc(st)
                    wo = sb("wo", [128, 16, D], BF16)
                    Bwo = Buf()
                    kb.dma("pool", wo[:], w_out[l].rearrange("(kc p) f -> p kc f", p=128), writes=[Bwo])
                    gpo = sb("gpo", [128, D], F32)
                    G = [sb("G%d" % g, [128, D], F32) for g in range(2)]
                    Bgpo, BG = Buf(), [Buf(), Buf()]
                    kb.dma("sp", gpo[:], g_post[l:l + 1, :].partition_broadcast(128), writes=[Bgpo])
                    for g in range(2):
                        kb.dma("sp", G[g][:], m_s[g:g + 1, 2 * D:3 * D].partition_broadcast(128), reads=[SB["m"]], writes=[BG[g]])
                        TT("dve", G[g][:], G[g][:], gpo[:], ALU.mult, [Bgpo], [BG[g]])
                    mgt = [sb("mgt%d" % i, [128, 16, 128], BF16) for i in range(2)]
                    xt = [sb("xo%d" % i, [128, D], F32) for i in range(2)]
                    zt = [sb("zt%d" % i, [128, D], F32) for i in range(2)]
                    junk = sb("junk2", [128, 512], BF16)
                    ss = [sb("sso%d" % i, [128, 8], F32) for i in range(2)]
                    Bmgt, Bxt, Bzt, Bj, Bss = [Buf(), Buf()], [Buf(), Buf()], [Buf(), Buf()], Buf(), [Buf(), Buf()]
                    for ts in range(20):
                        i = ts % 2
                        g = 0 if ts < 4 else 1
                        kb.dma("sp", mgt[i][:], mgv[:, :, ts * 128:(ts + 1) * 128], reads=[SB["mg"]], writes=[Bmgt[i]])
                        kb.dma("sp", xt[i][:], xsrc[ts * 128:(ts + 1) * 128, :], reads=[Bx], writes=[Bxt[i]])
                        bks = (0, 1, 2, 3) if i == 0 else (4, 5, 6, 0)
                        for cb in range(4):
                            bk = bks[cb]
                            for kc in range(16):
                                MM(PS[bk][:], mgt[i][:, kc, :], wo[:, kc, cb * 512:(cb + 1) * 512], kc == 0, kc == 15,
                                   [Bmgt[i], Bwo], [PB[bk]], kc == 15)
                            CP("dve", zt[i][:, cb * 512:(cb + 1) * 512], PS[bk][:], [PB[bk]], [Bzt[i]])
                            ACT(junk[:], zt[i][:, cb * 512:(cb + 1) * 512], AF.Square, [Bzt[i]], [Bj, Bss[i]], accum_out=ss[i][:, cb:cb + 1])
                        kb.op("dve", lambda e: e.tensor_reduce(out=ss[i][:, 4:5], in_=ss[i][:, 0:4], axis=AX.X, op=ALU.add), [], [Bss[i]])
                        TS("dve", ss[i][:, 5:6], ss[i][:, 4:5], 1.0 / D, EPS, ALU.mult, ALU.add, [], [Bss[i]])
                        ACT(ss[i][:, 6:7], ss[i][:, 5:6], AF.Sqrt, [], [Bss[i]])
                        kb.op("dve", lambda e: e.reciprocal(out=ss[i][:, 7:8], in_=ss[i][:, 6:7]), [], [Bss[i]])
                        STT(zt[i][:], zt[i][:], ss[i][:, 7:8], G[g][:], ALU.mult, ALU.mult, [Bss[i], BG[g]], [Bzt[i]])
                        TT("dve", zt[i][:], zt[i][:], xt[i][:], ALU.add, [Bxt[i]], [Bzt[i]])
                        kb.dma("pool", ydst[ts * 128:(ts + 1) * 128, :], zt[i][:], reads=[Bzt[i]], writes=[By])

            def phase_ssm(l):
                xbv = xbcT_s.rearrange("(fc p) t -> p fc t", p=128)
                obTv = oT_s[1].rearrange("(fc p) t -> p fc t", p=128)
                triu_b, tril_b, ones_b = cstb[:, 1, :], cstb[:, 2, :], cstb[:, 5, :]
                sgt_f, slt_f = cst[:, 3, :], cst[:, 4, :]
                one_col = cst[:, 5, 0:1]
                with ExitStack() as st:
                    sb = sbuf_alloc(st)
                    cw = sb("cw", [128, 16, 3], F32)
                    cbs = sb("cbs", [128, 16], F32)
                    dtb = sb("dtb", [128, 32], F32)
                    abc = sb("abc", [128, 32], F32)
                    dsk = sb("dsk", [128, 16], F32)
                    gss = sb("gss", [128, 1024], F32)
                    CT = sb("CTall", [128, 4, NTOK], BF16)
                    ea = sb("ea", [128, 20, 32], F32)
                    cd = sb("cd", [128, 20, 32], F32)
                    Bcw, Bdtb, Babc, Bdsk, Bgss, BCT, Bea, Bcd = (Buf() for _ in range(8))
                    kb.dma("sp", cw[:], cwT[l], writes=[Bcw])
                    kb.dma("sp", cbs[:], cbT[l], writes=[Bcw])
                    kb.dma("sp", dtb[:], dt_bias[l:l + 1, :].partition_broadcast(128), writes=[Bdtb])
                    kb.dma("sp", abc[:], a_log[l:l + 1, :].partition_broadcast(128), writes=[Babc])
                    ACT(abc[:], abc[:], AF.Exp, [], [Babc])
                    TS("dve", abc[:], abc[:], -1.0, None, ALU.mult, None, [], [Babc])
                    kb.dma("sp", dsk[:], d_skip[l:l + 1, :].partition_broadcast(128), writes=[Bdsk])
                    dtall = sb("dtall", [128, 20, 32], F32)
                    Bdtall = Buf()
                    kb.dma("sp", dtall[:].rearrange("p a b -> p (a b)"), dt_s[:, :], reads=[SB["dt"]], writes=[Bdtall])
                    TT("dve", dtall[:], dtall[:], dtb[:].unsqueeze(1).broadcast_to([128, 20, 32]), ALU.add, [Bdtb], [Bdtall])
                    ACT(dtall[:], dtall[:], AF.Exp, [], [Bdtall])
                    ACT(dtall[:], dtall[:], AF.Ln, [B_cst], [Bdtall], bias=one_col)
                    dt2 = sb("dt2", [128, 2, 20, 16], F32)
                    adt2 = sb("adt2", [128, 2, 20, 16], F32)
                    abk = sb("abk", [128, 2, 20, 16], F32)
                    ahi = sb("ahi", [128, 2, 20, 16], BF16)
                    alo = sb("alo", [128, 2, 20, 16], BF16)
                    ea2 = sb("ea2", [128, 2, 20, 16], F32)
                    cd2 = sb("cd2", [128, 2, 20, 16], F32)
                    w2 = sb("w2", [128, 2, 20, 16], F32)
                    Bsc = Buf()
                    for d in range(2):
                        CP("dve", dt2[:, d], dtall[:, :, d * 16:(d + 1) * 16], [Bdtall], [Bsc])
                        TT("dve", adt2[:, d], dt2[:, d], abc[:, d * 16:(d + 1) * 16].unsqueeze(1).broadcast_to([128, 20, 16]), ALU.mult,
                           [Babc], [Bsc])
                    CP("dve", ahi[:], adt2[:], [], [Bsc])
                    CP("dve", abk[:], ahi[:], [], [Bsc])
                    TT("dve", alo[:], adt2[:], abk[:], ALU.subtract, [], [Bsc])
                    for d in range(2):
                        V = triu_b if d == 0 else tril_b
                        for x, a_ in enumerate((ahi, alo)):
                            MM(PS[d][:, 0:320], V, a_[:, d].rearrange("p c h -> p (c h)"), x == 0, x == 1, [Bsc, B_cst], [PB[d]], x == 1)
                            MM(PS[2 + d][:, 0:320], ones_b, a_[:, d].rearrange("p c h -> p (c h)"), x == 0, x == 1, [Bsc, B_cst], [PB[2 + d]], x == 1)
                    for d in range(2):
                        f2 = lambda t: t[:, d].rearrange("p c h -> p (c h)")
                        ACT(f2(ea2), PS[d][:, 0:320], AF.Exp, [PB[d]], [Bsc])
                        ACT(f2(cd2), PS[2 + d][:, 0:320], AF.Exp, [PB[2 + d]], [Bsc])
                        CP("dve", f2(abk), PS[d][:, 0:320], [PB[d]], [Bsc])
                        TT("dve", f2(abk), PS[2 + d][:, 0:320], f2(abk), ALU.subtract, [PB[2 + d]], [Bsc])
                    ACT(abk[:], abk[:], AF.Exp, [], [Bsc])
                    TT("dve", w2[:], abk[:], dt2[:], ALU.mult, [], [Bsc])
                    Bea = Bsc
                    Bcd = Bsc
                    kb.dma("sp", gss[:], g_ssm[l:l + 1, :].partition_broadcast(128), writes=[Bgss])
                    SSTOP = int(os.environ.get("KDBG_SSTOP", "99"))
                    PBC = os.environ.get("KDBG_PBC", "dve")
                    if SSTOP <= 1:
                        kb.barrier()
                        return
                    with ExitStack() as st1:
                        sb1 = sbuf_alloc(st1)
                        xin = sb1("xin", [128, 16, 514], BF16)
                        hst = sb1("hst", [128, 16, 64], BF16)
                        Bhst = Buf()
                        xc = sb1("xc", [128, 12, 512], BF16)
                        acc = [sb1("cacc%d" % i, [128, 512], F32) for i in range(2)]
                        xs_tok = [sb1("xstok%d" % i, [128, 1024], BF16) for i in range(2)]
                        B_tok = [sb1("Btok%d" % i, [128, 4, 128], BF16) for i in range(2)]
                        sm = [sb1("sm%d" % i, [128, 8, 32], F32) for i in range(2)]
                        ahl = [sb1("ahl%d" % i, [128, 2, 32], BF16) for i in range(2)]
                        xdt = [sb1("xdt%d" % i, [128, 2, 1024], BF16) for i in range(2)]
                        xdd = [sb1("xdd%d" % i, [128, 2, 1024], BF16) for i in range(2)]
                        xsk = [sb1("xsk%d" % i, [128, 1024], F32) for i in range(2)]
                        CBm = [sb1("CBm%d" % i, [128, 2, 4, 128], BF16) for i in range(2)]
                        U = [sb1("U%d" % i, [128, 16, 128], BF16) for i in range(8)]
                        Lt = [sb1("Lt%d" % i, [128, 4, 128], BF16) for i in range(4)]
                        Mt = [sb1("Mt%d" % i, [128, 4, 128], BF16) for i in range(4)]
                        yas = [sb1("yas%d" % i, [128, 1024], F32) for i in range(2)]
                        sts = [sb1("sts%d" % i, [128, 1024], F32) for i in range(2)]
                        Bxin, Bxc = Buf(), Buf()
                        Bacc = [Buf(), Buf()]
                        Bxs, BBt, Bsm, Bahl = [Buf(), Buf()], [Buf(), Buf()], [Buf(), Buf()], [Buf(), Buf()]
                        Bxdt, Bxdd, Bxsk, BCBm = [Buf(), Buf()], [Buf(), Buf()], [Buf(), Buf()], [Buf(), Buf()]
                        BU = [Buf() for _ in range(8)]
                        BLt = [Buf() for _ in range(4)]
                        BMt = [Buf() for _ in range(4)]
                        Byas, Bsts = [Buf(), Buf()], [Buf(), Buf()]
                        kctr = dict(lm=0)
                        C1STOP = int(os.environ.get("KDBG_C1STOP", "99"))

                        def conv_tile(t0, n, left, right, lsrc=None, rsrc=None):
                            oth = REG[1 - slot]
                            if not left:
                                if lsrc is None:
                                    kb.op("pool", lambda e: e.memset(xin[:, :, 0:1], 0.0), [], [Bxin])
                                else:
                                    kb.dma("sp", hst[:], lsrc[0], reads=[oth.SB["xbc"]], writes=[Bhst])
                                    CP("dve", xin[:, :, 0:1], hst[:, :, lsrc[1]:lsrc[1] + 1], [Bhst], [Bxin])
                            if not right:
                                if rsrc is None:
                                    kb.op("pool", lambda e: e.memset(xin[:, :, n + 1:n + 2], 0.0), [], [Bxin])
                                else:
                                    kb.dma("sp", hst[:], rsrc[0], reads=[oth.SB["xbc"]], writes=[Bhst])
                                    CP("dve", xin[:, :, n + 1:n + 2], hst[:, :, rsrc[1]:rsrc[1] + 1], [Bhst], [Bxin])
                            a0, a1 = (0 if left else 1), (n + 2 if right else n + 1)
                            kb.dma("sp", xin[:, :, a0:a1], xbv[:, :, t0 - 1 + a0:t0 - 1 + a1], reads=[SB["xbc"]], writes=[Bxin])
                            CV = int(os.environ.get("KDBG_CV", "9"))
                            for fc in range(16 if CV > 1 else 0):
                                a = acc[fc % 2]
                                Ba = Bacc[fc % 2]
                                TS("dve", a[:, 0:n], xin[:, fc, 0:n], cw[:, fc, 0:1], cbs[:, fc:fc + 1], ALU.mult, ALU.add,
                                   [Bxin, Bcw], [Ba])
                                STT(a[:, 0:n], xin[:, fc, 1:n + 1], cw[:, fc, 1:2], a[:, 0:n], ALU.mult, ALU.add, [Bxin], [Ba])
                                STT(a[:, 0:n], xin[:, fc, 2:n + 2], cw[:, fc, 2:3], a[:, 0:n], ALU.mult, ALU.add, [Bxin], [Ba])
                                if CV <= 2:
                                    continue
                                if fc < 12:
                                    ACT(xc[:, fc, 0:n], a[:, 0:n], AF.Silu, [Ba], [Bxc])
                                else:
                                    ACT(CT[:, fc - 12, t0:t0 + n], a[:, 0:n], AF.Silu, [Ba], [BCT])

                        def chunk1(c, off):
                            if os.environ.get("KDBG_CBAR"):
                                kb.barrier()
                            i = c % 2
                            tok = c * 128
                            for fc in range(8):
                                TR(PSb(6)[:, fc * 128:(fc + 1) * 128], xc[:, fc, off:off + 128], identb, [Bxc, B_cst], [PB[6]], fc == 7)
                            CP("dve", xs_tok[i][:], PSb(6), [PB[6]], [Bxs[i]])
                            for g in range(4):
                                TR(PSb(6)[:, g * 128:(g + 1) * 128], xc[:, 8 + g, off:off + 128], identb, [Bxc, B_cst], [PB[6]], g == 3)
                            CP("dve", B_tok[i][:].rearrange("p a b -> p (a b)"), PSb(6)[:, 0:512], [PB[6]], [BBt[i]])
                            if c >= 1 and C1STOP <= 1:
                                return
                            if c >= 1 and C1STOP <= 3:
                                return
                            xv = xs_tok[i][:].rearrange("p (h q) -> p h q", q=64)
                            for d in range(2):
                                TT("dve", xdt[i][:, d, :].rearrange("p (h q) -> p h q", q=64), xv,
                                   dt2[:, d, c, :].unsqueeze(2).broadcast_to([128, 16, 64]), ALU.mult,
                                   [Bxs[i], Bsc], [Bxdt[i]])
                                TT(PBC, xdd[i][:, d, :].rearrange("p (h q) -> p h q", q=64), xv,
                                   w2[:, d, c, :].unsqueeze(2).broadcast_to([128, 16, 64]), ALU.mult,
                                   [Bxs[i], Bsc], [Bxdd[i]])
                            TT(PBC, xsk[i][:].rearrange("p (h q) -> p h q", q=64), xv,
                               dsk[:].unsqueeze(2).broadcast_to([128, 16, 64]), ALU.mult, [Bxs[i], Bdsk], [Bxsk[i]])
                            if c >= 1 and C1STOP <= 4:
                                return
                            for g in range(4):
                                MM(PS[4][:, g * 128:(g + 1) * 128], xc[:, 8 + g, off:off + 128], CT[:, g, tok:tok + 128], True, True,
                                   [Bxc, BCT], [PB[4]], g == 3)
                            P4 = PS[4][:].rearrange("p (a b) -> p a b", a=4)
                            TT("dve", CBm[i][:, 0], P4, triu_b.unsqueeze(1).broadcast_to([128, 4, 128]), ALU.mult, [PB[4], B_cst], [BCBm[i]])
                            TT("dve", CBm[i][:, 1], P4, tril_b.unsqueeze(1).broadcast_to([128, 4, 128]), ALU.mult, [PB[4], B_cst], [BCBm[i]])
                            if c >= 1 and C1STOP <= 5:
                                return
                            for d in range(2):
                                mk = sgt_f if d == 0 else slt_f
                                for x in range(2):
                                    TT(PBC if x == 0 else "dve", U[i * 4 + d * 2 + x][:], mk.unsqueeze(1).broadcast_to([128, 16, 128]),
                                       (ahi, alo)[x][:, d, c, :].unsqueeze(2).broadcast_to([128, 16, 128]), ALU.mult,
                                       [Bsc, B_cst], [BU[i * 4 + d * 2 + x]])
                            if c >= 1 and C1STOP <= 6:
                                return
                            for g in range(4):
                                lm = []
                                for d in range(2):
                                    bk = 2 + d
                                    V = triu_b if d == 0 else tril_b
                                    for hh in range(4):
                                        h = 4 * g + hh
                                        MM(PS[bk][:, hh * 128:(hh + 1) * 128], U[i * 4 + d * 2][:, h, :], V, True, False,
                                           [BU[i * 4 + d * 2], B_cst], [PB[bk]], False)
                                        MM(PS[bk][:, hh * 128:(hh + 1) * 128], U[i * 4 + d * 2 + 1][:, h, :], V, False, True,
                                           [BU[i * 4 + d * 2 + 1], B_cst], [PB[bk]], hh == 3)
                                    k = kctr["lm"] % 4
                                    kctr["lm"] += 1
                                    ACT(Lt[k][:].rearrange("p a b -> p (a b)"), PS[bk][:], AF.Exp, [PB[bk]], [BLt[k]])
                                    TT("dve", Mt[k][:], Lt[k][:], CBm[i][:, d, g, :].unsqueeze(1).broadcast_to([128, 4, 128]), ALU.mult,
                                       [BLt[k], BCBm[i]], [BMt[k]])
                                    lm.append(k)
                                for hh in range(4):
                                    h = 4 * g + hh
                                    yb = 0 if h < 8 else 1
                                    yo = (h % 8) * 64
                                    for d in range(2):
                                        MM(PS[yb][:, yo:yo + 64], Mt[lm[d]][:, hh, :], xdt[i][:, d, h * 64:(h + 1) * 64], d == 0, d == 1,
                                           [BMt[lm[d]], Bxdt[i]], [PB[yb]], (d == 1 and hh == 3))
                            if c >= 1 and C1STOP <= 7:
                                return
                            for yb in range(2):
                                TT("dve", yas[i][:, yb * 512:(yb + 1) * 512], PS[yb][:], xsk[i][:, yb * 512:(yb + 1) * 512], ALU.add,
                                   [PB[yb], Bxsk[i]], [Byas[i]])
                            kb.dma("pool", ya_s[tok:tok + 128, :], yas[i][:], reads=[Byas[i]], writes=[SB["ya"]])
                            if c >= 1 and C1STOP <= 8:
                                return
                            for d in range(2):
                                for gp in range(2):
                                    bk = 4 + gp
                                    for gg in range(2):
                                        g = 2 * gp + gg
                                        MM(PS[bk][:, gg * 256:(gg + 1) * 256], B_tok[i][:, g, :], xdd[i][:, d, g * 256:(g + 1) * 256],
                                           True, True, [BBt[i], Bxdd[i]], [PB[bk]], gg == 1)
                                    CP("dve", sts[d][:, gp * 512:(gp + 1) * 512], PS[bk][:], [PB[bk]], [Bsts[d]])
                                kb.dma("pool", stc_s[c, d], sts[d][:], reads=[Bsts[d]], writes=[SB["stc"]])

                        for sq in range(2):
                            conv_tile(sq * 256, 256, False, False)
                            if SSTOP <= 2:
                                break
                            for cc in range(2):
                                chunk1(sq * 2 + cc, cc * 128)
                                if SSTOP <= 3:
                                    break
                            if SSTOP <= 3:
                                break
                        NCH = int(os.environ.get("KDBG_NCH", "99"))
                        for tt in range(1, 5 if SSTOP > 3 else 0):
                            if (tt - 1) * 4 >= NCH:
                                break
                            oxbv = REG[1 - slot].xbcT_s.rearrange("(fc p) t -> p fc t", p=128)
                            conv_tile(tt * 512, 512, tt > 1, tt < 4,
                                      lsrc=(oxbv[:, :, 2496:2560], 63) if (slot == 1 and tt == 1) else None,
                                      rsrc=(oxbv[:, :, 512:576], 0) if (slot == 0 and tt == 4) else None)
                            for cc in range(4):
                                if (tt - 1) * 4 + cc >= NCH:
                                    break
                                chunk1(tt * 4 + cc, cc * 128)
                    kb.dma("sp", CT_s[:, :], CT[:].rearrange("p g t -> p (g t)"), reads=[BCT], writes=[SB["ct"]])
                    kb.dma("sp", ea_s[:, :], ea2[:].rearrange("p d c h -> p (d c h)"), reads=[Bsc], writes=[SB["ea"]])
                    kb.dma("sp", cd_s[:, :], cd2[:].rearrange("p d c h -> p (d c h)"), reads=[Bsc], writes=[SB["cd"]])
                kb.barrier()

            ns = types.SimpleNamespace(**{k: v for k, v in locals().items() if not k.startswith("_")})
            REG[slot] = ns
            return ns

        make_slot(0)
        make_slot(1)

        def ssm_pass2(l):
            triu_b = cstb[:, 1, :]
            with ExitStack() as st2:
                sb2 = sbuf_alloc(st2)
                gss = sb2("gss2", [128, 1024], F32)
                Bgss = Buf()
                kb.dma("sp", gss[:], g_ssm[l:l + 1, :].partition_broadcast(128), writes=[Bgss])
                ea2 = [sb2("ea2_%d" % s_, [128, 2, 20, 16], F32) for s_ in range(2)]
                cd2 = [sb2("cd2_%d" % s_, [128, 2, 20, 16], F32) for s_ in range(2)]
                Bea, Bcd = Buf(), Buf()
                for s_ in range(2):
                    kb.dma("sp", ea2[s_][:].rearrange("p d c h -> p (d c h)"), REG[s_].ea_s[:, :], reads=[REG[s_].SB["ea"]], writes=[Bea])
                    kb.dma("sp", cd2[s_][:].rearrange("p d c h -> p (d c h)"), REG[s_].cd_s[:, :], reads=[REG[s_].SB["cd"]], writes=[Bcd])
                H = [sb2("H%d" % d, [128, 1024], F32) for d in range(2)]
                BH = [Buf(), Buf()]
                stl = [sb2("stl%d" % i, [128, 1024], F32) for i in range(2)]
                Bstl = [Buf(), Buf()]
                hpb = [sb2("hpb%d" % i, [128, 1024], BF16) for i in range(2)]
                Bhpb = [Buf(), Buf()]
                hpf = [sb2("hpf%d" % i, [128, 1024], BF16) for i in range(2)]
                Bhpf = [Buf(), Buf()]
                ctl = [sb2("ctl%d" % i, [128, 4, 128], BF16) for i in range(2)]
                Bctl = [Buf(), Buf()]
                s0 = sb2("s0", [128, 8, 128], F32)
                Bs0 = Buf()
                fin = sb2("fin", [128, 8, 128], F32)
                Bfin = Buf()
                yal = [sb2("yal%d" % i, [128, 1024], F32) for i in range(2)]
                Byal = [Buf(), Buf()]
                zsl = [sb2("zsl%d" % i, [128, 1024], BF16) for i in range(2)]
                Bzsl = [Buf(), Buf()]
                t2 = [sb2("t2_%d" % i, [128, 1024], F32) for i in range(2)]
                Bt2 = [Buf(), Buf()]
                junk = sb2("junk3", [128, 1024], BF16)
                Bj = Buf()
                ssq = [sb2("ssq%d" % i, [128, 4], F32) for i in range(2)]
                Bssq = [Buf(), Buf()]
                obt = [sb2("obt%d" % i, [128, 1024], BF16) for i in range(2)]
                Bobt = [Buf(), Buf()]
                obT = sb2("obTst", [128, 8, 512], BF16)
                BobT = Buf()
                Bout = Buf()
                cn = dict(st=0, hp=0, f=0)

                def init_state(d, sample):
                    if not sample:
                        kb.op("dve", lambda e: e.memset(H[d][:], 0.0), [], [BH[d]])
                        return
                    kb.dma("sp", s0[:], st0[l, d].rearrange("(a p) n -> p a n", p=128), writes=[Bs0])
                    for half in range(2):
                        bk = 4 + half
                        for a in range(4):
                            TR(PS[bk][:, a * 128:(a + 1) * 128], s0[:, half * 4 + a, :], identf, [Bs0, B_cst], [PB[bk]], a == 3)
                        CP("dve", H[d][:, half * 512:(half + 1) * 512], PS[bk][:], [PB[bk]], [BH[d]])

                def prefetch_state(d, sl_, c):
                    S_ = REG[sl_]
                    j = cn["st"] % 2
                    kb.dma("sp", stl[j][:], S_.stc_s[c, d], reads=[S_.SB["stc"]], writes=[Bstl[j]])

                def step_state(d, sl_, c):
                    S_ = REG[sl_]
                    j = cn["st"] % 2
                    cn["st"] += 1
                    Hv = H[d][:].rearrange("p (h q) -> p h q", q=64)
                    TT("dve", Hv, Hv, cd2[sl_][:, d, c, :].unsqueeze(2).broadcast_to([128, 16, 64]), ALU.mult, [Bcd], [BH[d]])
                    TT("dve", H[d][:], H[d][:], stl[j][:], ALU.add, [Bstl[j]], [BH[d]])

                def emit_state(d, dst):
                    for half in range(2):
                        bk = 4 + half
                        for a in range(4):
                            k8 = half * 4 + a
                            TR(PS[bk][:, a * 128:(a + 1) * 128], H[d][:, k8 * 128:(k8 + 1) * 128], identf, [BH[d], B_cst], [PB[bk]], a == 3)
                        CP("dve", fin[:, half * 4:(half + 1) * 4, :], PS[bk][:].rearrange("p (a b) -> p a b", a=4), [PB[bk]], [Bfin])
                    kb.dma("pool", dst.rearrange("(a p) n -> p a n", p=128), fin[:], reads=[Bfin], writes=[Bout])

                def bwd_visit(sl_, c):
                    S_ = REG[sl_]
                    prefetch_state(1, sl_, c)
                    j = cn["hp"] % 2
                    cn["hp"] += 1
                    CP("dve", hpb[j][:], H[1][:], [BH[1]], [Bhpb[j]])
                    kb.dma("pool", S_.hpb_s[c], hpb[j][:], reads=[Bhpb[j]], writes=[S_.SB["hpb"]])
                    step_state(1, sl_, c)

                def fwd_visit(sl_, c):
                    S_ = REG[sl_]
                    prefetch_state(0, sl_, c)
                    i = cn["f"] % 2
                    cn["f"] += 1
                    tok = c * 128
                    CP("dve", hpf[i][:], H[0][:], [BH[0]], [Bhpf[i]])
                    kb.dma("sp", hpb[i][:], S_.hpb_s[c], reads=[S_.SB["hpb"]], writes=[Bhpb[i]])
                    kb.dma("sp", yal[i][:], S_.ya_s[tok:tok + 128, :], reads=[S_.SB["ya"]], writes=[Byal[i]])
                    kb.dma("sp", zsl[i][:], S_.zs_s[tok:tok + 128, :], reads=[S_.SB["z"]], writes=[Bzsl[i]])
                    kb.dma("sp", ctl[i][:], S_.CT_s.rearrange("p (g t) -> p g t", g=4)[:, :, tok:tok + 128], reads=[S_.SB["ct"]],
                           writes=[Bctl[i]])
                    for d in range(2):
                        hp_ = hpf[i] if d == 0 else hpb[i]
                        Bhp = Bhpf[i] if d == 0 else Bhpb[i]
                        for g in range(4):
                            bk = 2 * d + g // 2
                            MM(PS[bk][:, (g % 2) * 256:(g % 2 + 1) * 256], ctl[i][:, g, :], hp_[:, g * 256:(g + 1) * 256],
                               True, True, [Bctl[i], Bhp], [PB[bk]], g % 2 == 1)
                    for d in range(2):
                        for hb in range(2):
                            bk = 2 * d + hb
                            src = PS[bk][:].rearrange("p (h q) -> p h q", q=64)
                            dstv = t2[i][:, hb * 512:(hb + 1) * 512].rearrange("p (h q) -> p h q", q=64)
                            eav = ea2[sl_][:, d, c, hb * 8:hb * 8 + 8].unsqueeze(2).broadcast_to([128, 8, 64])
                            if d == 0:
                                TT("dve", dstv, src, eav, ALU.mult, [PB[bk], Bea], [Bt2[i]])
                            else:
                                TT("dve", src, src, eav, ALU.mult, [Bea], [PB[bk]])
                                TT("dve", t2[i][:, hb * 512:(hb + 1) * 512], t2[i][:, hb * 512:(hb + 1) * 512], PS[bk][:], ALU.add,
                                   [PB[bk]], [Bt2[i]])
                    TT("dve", t2[i][:], t2[i][:], yal[i][:], ALU.add, [Byal[i]], [Bt2[i]])
                    TT("dve", t2[i][:], t2[i][:], zsl[i][:], ALU.mult, [Bzsl[i]], [Bt2[i]])
                    ACT(junk[:], t2[i][:], AF.Square, [Bt2[i]], [Bj, Bssq[i]], accum_out=ssq[i][:, 0:1])
                    TS("dve", ssq[i][:, 1:2], ssq[i][:, 0:1], 1.0 / 1024, EPS, ALU.mult, ALU.add, [], [Bssq[i]])
                    ACT(ssq[i][:, 2:3], ssq[i][:, 1:2], AF.Sqrt, [], [Bssq[i]])
                    kb.op("dve", lambda e: e.reciprocal(out=ssq[i][:, 3:4], in_=ssq[i][:, 2:3]), [], [Bssq[i]])
                    STT(obt[i][:], t2[i][:], ssq[i][:, 3:4], gss[:], ALU.mult, ALU.mult, [Bt2[i], Bssq[i], Bgss], [Bobt[i]])
                    for fc in range(8):
                        TR(PSb(6)[:, fc * 128:(fc + 1) * 128], obt[i][:, fc * 128:(fc + 1) * 128], identb, [Bobt[i], B_cst], [PB[6]], fc == 7)
                    CP("dve", obT[:, :, (c % 4) * 128:(c % 4 + 1) * 128], PSb(6).rearrange("p (a b) -> p a b", a=8), [PB[6]], [BobT])
                    if c % 4 == 3:
                        obTv = S_.oT_s[1].rearrange("(fc p) t -> p fc t", p=128)
                        kb.dma("pool", obTv[:, :, (c - 3) * 128:(c + 1) * 128], obT[:], reads=[BobT], writes=[S_.SB["ob"]])
                    step_state(0, sl_, c)

                for sl_ in range(2):
                    for sq in range(2):
                        init_state(1, False)
                        for c in (sq * 2 + 1, sq * 2):
                            bwd_visit(sl_, c)
                        emit_state(1, REG[sl_].sbo[l, sq])
                init_state(1, True)
                for sl_ in (1, 0):
                    for c in range(19, 3, -1):
                        bwd_visit(sl_, c)
                for sl_ in range(2):
                    for sq in range(2):
                        init_state(0, False)
                        for c in (sq * 2, sq * 2 + 1):
                            fwd_visit(sl_, c)
                        emit_state(0, REG[sl_].sfo[l, sq])
                init_state(0, True)
                for sl_ in (0, 1):
                    for c in range(4, 20):
                        fwd_visit(sl_, c)
            kb.barrier()

        for l in range(n_layers):
            REG[0].phase_mod(l)
            kb.barrier()
            for sl_ in range(2):
                S_ = REG[sl_]
                xsrc = S_.x_in if l == 0 else S_.y1_s
                Bx = Buf() if l == 0 else S_.SB["y1"]
                with ExitStack() as lst:
                    hT = lst.enter_context(nc.sbuf_tensor("hT%d_%d" % (l, sl_), [128, 16, NTOK], BF16))
                    BhTs = [Buf("hT%d" % i) for i in range(20)]
                    S_.phase_norm(l, xsrc, Bx, hT, BhTs)
                    kb.barrier()
                    S_.phase_proj(l, hT, Buf("hTro"))
                kb.barrier()
            for nm in ("phase_attn", "phase_sgu", "phase_ssm"):
                for sl_ in range(2):
                    getattr(REG[sl_], nm)(l)
                    kb.barrier()
            ssm_pass2(l)
            for sl_ in range(2):
                S_ = REG[sl_]
                xsrc = S_.x_in if l == 0 else S_.y1_s
                Bx = Buf() if l == 0 else S_.SB["y1"]
                ydst = S_.y_out if l == n_layers - 1 else S_.y1_s
                By = S_.SB["yout"] if l == n_layers - 1 else S_.SB["y1"]
                S_.phase_o1(l)
                kb.barrier()
                S_.phase_o2(l, xsrc, Bx, ydst, By)
                kb.barrier()

        kb.barrier()
    return nc


def _consts():
    s = np.arange(128)[:, None]
    i = np.arange(128)[None, :]
    c = np.zeros((128, 6, 128), np.float32)
    c[:, 0] = (s == i)
    c[:, 1] = (s <= i)
    c[:, 2] = (s >= i)
    c[:, 3] = (s > i)
    c[:, 4] = (s < i)
    c[:, 5] = 1.0
    return c


def _bias_F(rpb_l):
    j = np.arange(64)[None, :]
    col = np.arange(64)[:, None]
    cs = np.clip(j - 8, 0, 48)
    valid = (col >= cs) & (col < cs + 16)
    idx = np.clip(col - j + 15, 0, 30)
    T = np.where(valid[None, None], rpb_l[:, :, idx], np.float32(NEG)).astype(np.float32)
    F = np.empty((2, 2, 64, 16, 6, 2, 64), np.float32)
    for kind in range(2):
        for c in range(6):
            for a in range(2):
                for e in range(2):
                    ri = 2 * c + a - e + (3 if kind == 0 else 1)
                    F[kind, a, :, :, c, e, :] = np.transpose(T[:, ri], (1, 0, 2))
    return F.reshape(2, 128, 16 * 6 * 128)


def _row_masks(half):
    out = np.zeros((5, 2, 64, 6, 2, 64), np.float32)
    slots = [8, 0, 1, 14, 15]
    for si, ml in enumerate(slots):
        kbg = 32 * half + (2 * ml if ml != 15 else 28) - 4
        for c in range(6):
            for a in range(2):
                kr = kbg + 2 * c + a
                for e in range(2):
                    qr = 32 * half + 2 * ml + e
                    rs = min(max(qr - 4, 0), 56)
                    ok = (0 <= kr < 64) and (rs <= kr < rs + 8)
                    out[si, a, :, c, e, :] = 0.0 if ok else NEG
    return out.reshape(5, 128, 6 * 128)


def prep_shared(inp):
    f = lambda a: np.ascontiguousarray(a, dtype=np.float32)
    sh = {}
    for k in ("w_mod", "b_mod", "g_pre", "g_post", "w_in", "g_ssm", "w_br_a", "w_br_b", "w_br_c", "w_out", "d_skip"):
        sh[k] = f(inp[k])
    sh["Fb"] = np.stack([_bias_F(np.asarray(inp["rpb"][l], np.float32)) for l in range(DEPTH)])
    sh["cwT"] = f(np.asarray(inp["conv_w"]).reshape(DEPTH, 16, 128, 3).transpose(0, 2, 1, 3))
    sh["cbT"] = f(np.asarray(inp["conv_b"]).reshape(DEPTH, 16, 128).transpose(0, 2, 1))
    sh["dt_bias"] = f(np.asarray(inp["dt_bias"]).reshape(DEPTH, 32))
    sh["a_log"] = f(np.asarray(inp["a_log"]).reshape(DEPTH, 32))
    sh["wsT"] = f(np.asarray(inp["w_s"]).transpose(0, 3, 1, 2))
    sh["b_s"] = f(np.asarray(inp["b_s"]).reshape(DEPTH, 1024))
    sh["g_sguT"] = f(np.asarray(inp["g_sgu"]).reshape(DEPTH, 8, 128).transpose(0, 2, 1))
    sh["consts"] = _consts()
    return sh


def prep_core(inp, sh, core):
    f = lambda a: np.ascontiguousarray(a, dtype=np.float32)
    b = core
    m = dict(sh)
    xs_ = []
    for slot in range(2):
        xp = np.asarray(inp["x_prompt"])[4 * core + 2 * slot:4 * core + 2 * slot + 2].reshape(512, D)
        xs = np.asarray(inp["x_sample"])[b, slot * NST:(slot + 1) * NST]
        xs_.append(np.concatenate([xp, xs], 0))
    m["x_in"] = f(np.stack(xs_, 0))
    cvec = np.stack([np.asarray(inp["c_ctx"]), np.asarray(inp["c"])[b]], 0)
    m["cvT"] = f(cvec.reshape(2, 16, 128).transpose(2, 1, 0))
    m["ck"] = f(np.asarray(inp["cache_k"])[b].reshape(DEPTH, 256, 1024))
    m["cv"] = f(np.asarray(inp["cache_v"])[b].reshape(DEPTH, 256, 1024))
    m["st0"] = f(np.stack([np.asarray(inp["state_ssm_fwd"])[b].reshape(DEPTH, 1024, 128),
                           np.asarray(inp["state_ssm_bwd"])[b].reshape(DEPTH, 1024, 128)], 1))
    m["rm"] = np.stack([_row_masks(0), _row_masks(1)], 0)
    return m


_NC_CACHE = {}
N_ACTIVE = 4


def kernel(**inputs):
    if "nc" not in _NC_CACHE:
        _NC_CACHE["nc"] = build()
    nc = _NC_CACHE["nc"]
    sh = prep_shared(inputs)
    in_maps = [prep_core(inputs, sh, c) for c in range(N_ACTIVE)]
    res = run_bass_kernel_spmd(nc, in_maps, core_ids=list(range(N_ACTIVE)))
    R = res.results
    y_p = np.empty((16, 256, D), np.float32)
    y_s = np.empty((4, 4096, D), np.float32)
    nk = np.empty((16, DEPTH, 256, 16, 64), np.float32)
    nv = np.empty((16, DEPTH, 256, 16, 64), np.float32)
    nf = np.empty((16, DEPTH, 16, 64, 128), np.float32)
    nb_ = np.empty((16, DEPTH, 16, 64, 128), np.float32)
    for c in range(N_ACTIVE):
        r = R[c]
        for slot in range(2):
            yo = r["y_out"][slot]
            y_s[c, slot * NST:(slot + 1) * NST] = yo[512:]
            for s in range(2):
                q = 4 * c + 2 * slot + s
                y_p[q] = yo[s * 256:(s + 1) * 256]
                for l in range(DEPTH):
                    nk[q, l] = r["ko"][slot, l, s * 256:(s + 1) * 256].reshape(256, 16, 64)
                    nv[q, l] = r["vo"][slot, l, s * 256:(s + 1) * 256].reshape(256, 16, 64)
                    nf[q, l] = r["sfo"][slot, l, s].reshape(16, 64, 128)
                    nb_[q, l] = r["sbo"][slot, l, s].reshape(16, 64, 128)
    return (y_p, y_s, nk, nv, nf, nb_)
```
